# Optimizing a Trainium2 kernel written in Bass

```python
import math
import jax, jax.numpy as jnp
from jax import lax
import numpy as np


D_MODEL = 2048
BATCH = 4
SEQ = 4096
DEPTH = 2

MEM_LEN = 256
RWKV_WIDTH = D_MODEL // 2
RWKV_HEAD = 64
RWKV_HEADS = RWKV_WIDTH // RWKV_HEAD
DECAY_LORA = 64
AAA_LORA = 64
GATE_LORA = 160
RWKV_COLS = 3 * RWKV_WIDTH + DECAY_LORA + AAA_LORA + GATE_LORA
DIFF_WIDTH = D_MODEL - RWKV_WIDTH
DIFF_VHEAD = 128
DIFF_HEADS = DIFF_WIDTH // DIFF_VHEAD
DIFF_QK = DIFF_VHEAD // 2
ROT_DIMS = DIFF_QK // 4
ROPE_THETA = 500000.0
N_IN = RWKV_COLS + 3 * DIFF_WIDTH
BLOCK_Q = 128
XATTN_HEADS = 4
XATTN_HEAD = 128
XATTN_WIDTH = XATTN_HEADS * XATTN_HEAD
PEER_HEADS = 8
PEER_KEYS = 128
PEER_EXPERTS = PEER_KEYS * PEER_KEYS
PEER_HALF = 128
PEER_TOPK = 16
PEER_CHUNK = 64
LN_EPS = 1e-5
GN_EPS = 64e-5
DEEPNORM_ALPHA = (2.0 * DEPTH) ** 0.25
DEEPNORM_BETA = (8.0 * DEPTH) ** -0.25

kernel_name = 'hybrid_rwkv7_diffattn_peer_block'

F32 = jnp.float32


def _f(t):
    return t.astype(F32)


def layer_norm(x, w, b):
    xf = _f(x)
    mu = xf.mean(-1, keepdims=True)
    var = jnp.mean(jnp.square(xf - mu), -1, keepdims=True)
    return ((xf - mu) * lax.rsqrt(var + LN_EPS) * _f(w) + _f(b)).astype(x.dtype)


def rwkv7_scan(r, w, k, v, a, b):
    def step(state, inp):
        r_t, w_t, k_t, v_t, a_t, b_t = inp
        sa = jnp.einsum('bhij,bhj->bhi', state, a_t)
        state = (state * w_t[:, :, None, :] + sa[..., None] * b_t[:, :, None, :]
                 + v_t[..., None] * k_t[:, :, None, :])
        y = jnp.einsum('bhij,bhj->bhi', state, r_t)
        return state, y
    B, S, H, N = r.shape
    xs = tuple(jnp.moveaxis(t, 1, 0) for t in (r, w, k, v, a, b))
    s0 = jnp.zeros((B, H, N, N), F32)
    _, ys = lax.scan(step, s0, xs)
    return jnp.moveaxis(ys, 0, 1)


def rwkv7_time_mix(p, shift_mu, w0, w_up, a0, a_up, g_up, k_k, k_a, r_k, lnx_w, lnx_b):
    B, S, _ = p.shape
    H, N, C = RWKV_HEADS, RWKV_HEAD, RWKV_WIDTH
    pf = _f(p)
    p_prev = jnp.pad(pf, ((0, 0), (1, 0), (0, 0)))[:, :-1]
    pf = pf + (p_prev - pf) * _f(shift_mu)
    r = pf[..., :C]
    k = pf[..., C:2 * C]
    v = pf[..., 2 * C:3 * C]
    o = 3 * C
    wd = pf[..., o:o + DECAY_LORA]
    o += DECAY_LORA
    ad = pf[..., o:o + AAA_LORA]
    o += AAA_LORA
    gd = pf[..., o:o + GATE_LORA]
    w_log = -jax.nn.softplus(-(_f(w0) + jnp.tanh(wd) @ _f(w_up))) - 0.5
    decay = jnp.exp(-jnp.exp(w_log))
    iclr = jax.nn.sigmoid(_f(a0) + ad @ _f(a_up))
    g = jax.nn.sigmoid(gd) @ _f(g_up)
    heads = lambda t: t.reshape(B, S, H, N)
    kk = heads(k * _f(k_k))
    kk = kk / jnp.maximum(jnp.linalg.norm(kk, axis=-1, keepdims=True), 1e-12)
    k = k * (1.0 + (iclr - 1.0) * _f(k_a))
    rh, kh, vh, ih = heads(r), heads(k), heads(v), heads(iclr)
    y = rwkv7_scan(rh, heads(decay), kh, vh, -kk, kk * ih)
    mu = y.mean(-1, keepdims=True)
    var = jnp.mean(jnp.square(y - mu), -1, keepdims=True)
    y = ((y - mu) * lax.rsqrt(var + GN_EPS)).reshape(B, S, C) * _f(lnx_w) + _f(lnx_b)
    bonus = jnp.sum(rh * kh * _f(r_k), -1, keepdims=True) * vh
    return (y + bonus.reshape(B, S, C)) * g


def partial_rope(t, cos, sin):
    half = ROT_DIMS // 2
    t1 = t[..., :half]
    t2 = t[..., half:ROT_DIMS]
    return jnp.concatenate([t1 * cos - t2 * sin, t2 * cos + t1 * sin, t[..., ROT_DIMS:]], -1)


def diff_attention(q, k, v, positions, lam_q1, lam_k1, lam_q2, lam_k2, subln_w, layer_idx):
    B, S, _ = q.shape
    H = DIFF_HEADS
    q = q.reshape(B, S, H, 2, DIFF_QK)
    k = k.reshape(B, S, H, 2, DIFF_QK)
    v = v.reshape(B, S, H, DIFF_VHEAD).transpose(0, 2, 1, 3)
    inv_freq = ROPE_THETA ** (-(jnp.arange(0, ROT_DIMS, 2, dtype=F32) / ROT_DIMS))
    ang = _f(positions)[..., None] * inv_freq
    cos = jnp.cos(ang)[:, :, None, None, :].astype(q.dtype)
    sin = jnp.sin(ang)[:, :, None, None, :].astype(q.dtype)
    q = partial_rope(q, cos, sin).transpose(0, 2, 3, 1, 4)
    k = partial_rope(k, cos, sin).transpose(0, 2, 3, 1, 4)
    lam_init = 0.8 - 0.6 * math.exp(-0.3 * layer_idx)
    lam = (jnp.exp(jnp.sum(_f(lam_q1) * _f(lam_k1))) - jnp.exp(jnp.sum(_f(lam_q2) * _f(lam_k2)))
           + lam_init)
    scale = DIFF_QK ** -0.5
    outs = []
    for i in range(S // BLOCK_Q):
        start, end = i * BLOCK_Q, (i + 1) * BLOCK_Q
        qb = q[:, :, :, start:end]
        kb = k[:, :, :, :end]
        vb = v[:, :, :end]
        s = jnp.einsum('bhcqd,bhckd->bhcqk', qb, kb, preferred_element_type=F32) * scale
        qpos = start + jnp.arange(BLOCK_Q)
        kpos = jnp.arange(end)
        s = jnp.where(kpos[None, :] <= qpos[:, None], s, -jnp.inf)
        pr = jax.nn.softmax(s, axis=-1)
        attn = pr[:, :, 0] - lam * pr[:, :, 1]
        outs.append(jnp.einsum('bhqk,bhkd->bhqd', attn.astype(vb.dtype), vb,
                               preferred_element_type=F32))
    o = jnp.concatenate(outs, axis=2)
    o = o * lax.rsqrt(jnp.mean(jnp.square(o), -1, keepdims=True) + LN_EPS) * _f(subln_w)
    o = o * (1.0 - lam_init)
    return o.transpose(0, 2, 1, 3).reshape(B, S, DIFF_WIDTH)


def memory_cross_attention(x, mem, wq, wk, wv, wo):
    B, S, _ = x.shape
    M = mem.shape[1]
    q = (x @ wq).reshape(B, S, XATTN_HEADS, XATTN_HEAD)
    k = (mem @ wk).reshape(B, M, XATTN_HEADS, XATTN_HEAD)
    v = (mem @ wv).reshape(B, M, XATTN_HEADS, XATTN_HEAD)
    s = jnp.einsum('bshd,bmhd->bhsm', q, k, preferred_element_type=F32) * (XATTN_HEAD ** -0.5)
    pr = jax.nn.softmax(s, axis=-1).astype(v.dtype)
    o = jnp.einsum('bhsm,bmhd->bshd', pr, v).reshape(B, S, XATTN_WIDTH)
    return o @ wo


def peer_ffn(x, pq, subkeys, peer_u, peer_v):
    B, S, D = x.shape
    T = B * S
    K = PEER_TOPK
    xt = x.reshape(T, D)
    q = (xt @ pq).reshape(T, PEER_HEADS, 2, PEER_HALF)
    s = jnp.einsum('thcd,ckd->thck', q, subkeys, preferred_element_type=F32)
    sv, si = lax.top_k(s, K)
    cand = (sv[:, :, 0, :, None] + sv[:, :, 1, None, :]).reshape(T, PEER_HEADS, K * K)
    fv, fi = lax.top_k(cand, K)
    i1 = jnp.take_along_axis(si[:, :, 0], fi // K, axis=-1)
    i2 = jnp.take_along_axis(si[:, :, 1], fi % K, axis=-1)
    experts = (i1 * PEER_KEYS + i2).reshape(T, PEER_HEADS * K)
    gates = jax.nn.softmax(fv, axis=-1).reshape(T, PEER_HEADS * K)
    n_chunks = T // PEER_CHUNK

    def chunk(args):
        xc, ec, gc = args
        hc = jnp.einsum('cd,ced->ce', xc, peer_u[ec], preferred_element_type=F32)
        ac = jax.nn.gelu(hc, approximate=False) * gc
        return jnp.einsum('ce,ced->cd', ac.astype(xc.dtype), peer_v[ec])

    y = lax.map(chunk, (xt.reshape(n_chunks, PEER_CHUNK, D),
                        experts.reshape(n_chunks, PEER_CHUNK, -1),
                        gates.reshape(n_chunks, PEER_CHUNK, -1)))
    return y.reshape(B, S, D).astype(x.dtype)


def setup_inputs(seed: int = 0) -> dict:
    key = jax.random.key(seed)
    ks = jax.random.split(key, 40)
    L = DEPTH
    nrm = lambda k, shape, sc: jax.random.normal(k, shape, F32) * sc
    gain = lambda k, shape: 1.0 + 0.05 * jax.random.normal(k, shape, F32)
    positions = (jnp.arange(SEQ, dtype=jnp.int32)[None, :]
                 + jax.random.randint(ks[2], (BATCH, 1), 0, 1024, dtype=jnp.int32))
    return {
        'x': nrm(ks[0], (BATCH, SEQ, D_MODEL), 1.0),
        'mem': nrm(ks[1], (BATCH, MEM_LEN, D_MODEL), 1.0),
        'positions': positions,
        'w_in': nrm(ks[3], (L, D_MODEL, N_IN), D_MODEL ** -0.5),
        'shift_mu': jax.random.uniform(ks[4], (L, RWKV_COLS), F32),
        'w0': jax.random.uniform(ks[5], (L, RWKV_WIDTH), F32, -6.0, 1.0),
        'w_up': nrm(ks[6], (L, DECAY_LORA, RWKV_WIDTH), 0.5 * DECAY_LORA ** -0.5),
        'a0': nrm(ks[7], (L, RWKV_WIDTH), 0.5),
        'a_up': nrm(ks[8], (L, AAA_LORA, RWKV_WIDTH), AAA_LORA ** -0.5),
        'g_up': nrm(ks[9], (L, GATE_LORA, RWKV_WIDTH), GATE_LORA ** -0.5),
        'k_k': 0.85 + 0.05 * jax.random.normal(ks[10], (L, RWKV_WIDTH), F32),
        'k_a': gain(ks[11], (L, RWKV_WIDTH)),
        'r_k': nrm(ks[12], (L, RWKV_HEADS, RWKV_HEAD), 0.1),
        'lnx_w': gain(ks[13], (L, RWKV_WIDTH)),
        'lnx_b': nrm(ks[14], (L, RWKV_WIDTH), 0.01),
        'lam_q1': nrm(ks[15], (L, DIFF_QK), 0.1),
        'lam_k1': nrm(ks[16], (L, DIFF_QK), 0.1),
        'lam_q2': nrm(ks[17], (L, DIFF_QK), 0.1),
        'lam_k2': nrm(ks[18], (L, DIFF_QK), 0.1),
        'subln_w': gain(ks[19], (L, DIFF_VHEAD)),
        'w_out': nrm(ks[20], (L, D_MODEL, D_MODEL), DEEPNORM_BETA * D_MODEL ** -0.5),
        'ln1_w': gain(ks[21], (L, D_MODEL)),
        'ln1_b': nrm(ks[22], (L, D_MODEL), 0.01),
        'xq': nrm(ks[23], (L, D_MODEL, XATTN_WIDTH), D_MODEL ** -0.5),
        'xk': nrm(ks[24], (L, D_MODEL, XATTN_WIDTH), D_MODEL ** -0.5),
        'xv': nrm(ks[25], (L, D_MODEL, XATTN_WIDTH), D_MODEL ** -0.5),
        'xo': nrm(ks[26], (L, XATTN_WIDTH, D_MODEL), DEEPNORM_BETA * XATTN_WIDTH ** -0.5),
        'ln2_w': gain(ks[27], (L, D_MODEL)),
        'ln2_b': nrm(ks[28], (L, D_MODEL), 0.01),
        'pq': nrm(ks[29], (L, D_MODEL, PEER_HEADS * 2 * PEER_HALF), D_MODEL ** -0.5),
        'subkeys': nrm(ks[30], (L, 2, PEER_KEYS, PEER_HALF), PEER_HALF ** -0.5),
        'peer_u': nrm(ks[31], (L, PEER_EXPERTS, D_MODEL), D_MODEL ** -0.5),
        'peer_v': nrm(ks[32], (L, PEER_EXPERTS, D_MODEL), DEEPNORM_BETA * PEER_HEADS ** -0.5),
        'ln3_w': gain(ks[33], (L, D_MODEL)),
        'ln3_b': nrm(ks[34], (L, D_MODEL), 0.01),
    }


def reference(x, mem, positions, w_in, shift_mu, w0, w_up, a0, a_up, g_up, k_k, k_a, r_k,
              lnx_w, lnx_b, lam_q1, lam_k1, lam_q2, lam_k2, subln_w, w_out, ln1_w, ln1_b,
              xq, xk, xv, xo, ln2_w, ln2_b, pq, subkeys, peer_u, peer_v, ln3_w, ln3_b):
    for l in range(DEPTH):
        proj = x @ w_in[l]
        p_rwkv = proj[..., :RWKV_COLS]
        o = RWKV_COLS
        q = proj[..., o:o + DIFF_WIDTH]
        k = proj[..., o + DIFF_WIDTH:o + 2 * DIFF_WIDTH]
        v = proj[..., o + 2 * DIFF_WIDTH:o + 3 * DIFF_WIDTH]
        y_a = rwkv7_time_mix(p_rwkv, shift_mu[l], w0[l], w_up[l], a0[l], a_up[l], g_up[l],
                             k_k[l], k_a[l], r_k[l], lnx_w[l], lnx_b[l])
        y_b = diff_attention(q, k, v, positions, lam_q1[l], lam_k1[l], lam_q2[l], lam_k2[l],
                             subln_w[l], l)
        mix = jnp.concatenate([y_a, y_b], axis=-1).astype(x.dtype) @ w_out[l]
        x = layer_norm(DEEPNORM_ALPHA * x + mix, ln1_w[l], ln1_b[l])
        xa = memory_cross_attention(x, mem, xq[l], xk[l], xv[l], xo[l])
        x = layer_norm(DEEPNORM_ALPHA * x + xa, ln2_w[l], ln2_b[l])
        xf = peer_ffn(x, pq[l], subkeys[l], peer_u[l], peer_v[l])
        x = layer_norm(DEEPNORM_ALPHA * x + xf, ln3_w[l], ln3_b[l])
    return x
```

```python
import contextlib
import numpy as np
import concourse.bass as bass
import concourse.mybir as mybir
from concourse.bass_utils import run_bass_kernel_spmd

F32 = mybir.dt.float32
I32 = mybir.dt.int32
U32 = mybir.dt.uint32
ALU = mybir.AluOpType
AF = mybir.ActivationFunctionType
AX = mybir.AxisListType


EMBED_WAIT = True


class Buf:
    def __init__(self, t, name, multi=False):
        self.t = t
        self.name = name
        self.multi = multi
        self.w = {}
        self.r = {}

    def __getitem__(self, idx):
        return self.t[idx]


class Prog:
    NDMA = 48

    def __init__(self):
        self.nc = bass.Bass("TRN2", target_bir_lowering=False)
        self.es = contextlib.ExitStack()
        nc = self.nc
        self.eng = {"pe": nc.tensor, "act": nc.scalar, "dve": nc.vector, "pool": nc.gpsimd, "sp": nc.sync}
        self.sem = {}
        self.cnt = {}
        self.seen = {e: {} for e in self.eng}
        self.semobj = {}
        for e in self.eng:
            s = self.es.enter_context(nc.semaphore("s_" + e))
            self.sem[e] = s
            self.semobj[e] = s
            self.cnt[e] = 0
        self.dsem = []
        self.dval = []
        for i in range(self.NDMA):
            s = self.es.enter_context(nc.semaphore("d%d" % i))
            self.dsem.append(s)
            self.dval.append(0)
            self.semobj[("d", i)] = s
        self.dnext = 0
        self.dpool = None
        self.dnext_sub = {}
        self.outs = []
        self.n = 0

    def din(self, name, shape, dt=F32):
        return Buf(self.nc.dram_tensor(name, list(shape), dt, kind="ExternalInput").ap(), name)

    def dout(self, name, shape, dt=F32):
        b = Buf(self.nc.dram_tensor(name, list(shape), dt, kind="ExternalOutput").ap(), name, multi=True)
        self.outs.append(b)
        return b

    def dtmp(self, name, shape, dt=F32):
        return Buf(self.nc.dram_tensor(name, list(shape), dt, kind="Internal").ap(), name, multi=True)

    def sb(self, name, shape, dt=F32, multi=True):
        self.n += 1
        return Buf(self.es.enter_context(self.nc.sbuf_tensor("%s_%d" % (name, self.n), list(shape), dt)), name, multi=multi)

    def ps(self, name, shape, dt=F32):
        self.n += 1
        return Buf(self.es.enter_context(self.nc.psum_tensor("%s_%d" % (name, self.n), list(shape), dt)), name)

    def _wait(self, e, deps, skip_self=False, keep_last=False):
        need = []
        for k, v in deps.items():
            if skip_self and k == e:
                continue
            if self.seen[e].get(k, 0) < v:
                need.append((k, v))
                self.seen[e][k] = v
        last = need.pop() if (keep_last and need) else None
        for k, v in need:
            self.eng[e].wait_ge(self.semobj[k], v)
        return last

    @staticmethod
    def _deps(reads, writes, is_dma=False):
        d = {}
        for b in reads:
            for k, v in b.w.items():
                if d.get(k, 0) < v:
                    d[k] = v
        for b in writes:
            for src in (b.w, b.r):
                for k, v in src.items():
                    if is_dma and b.multi and src is b.w and isinstance(k, tuple):
                        continue
                    if d.get(k, 0) < v:
                        d[k] = v
        return d

    @staticmethod
    def _mark(ev, reads, writes):
        k, v = ev
        for b in reads:
            if b.r.get(k, 0) < v:
                b.r[k] = v
        for b in writes:
            if b.w.get(k, 0) < v:
                b.w[k] = v

    def op(self, e, fn, reads=(), writes=()):
        last = self._wait(e, self._deps(reads, writes), skip_self=(e == "pe"), keep_last=EMBED_WAIT)
        ins = fn(self.eng[e])
        if last is not None:
            ins._wait_ge(self.semobj[last[0]], last[1])
        self.cnt[e] += 1
        ins.then_inc(self.sem[e], 1)
        self._mark((e, self.cnt[e]), reads, writes)
        return ins

    def dma(self, q, out_ap, in_ap, reads=(), writes=(), **kw):
        self._wait(q, self._deps(reads, writes, is_dma=True))
        i = self._next_dsem()
        k = ("d", i)
        if self.dval[i] > 0:
            self._wait(q, {k: self.dval[i]})
        ins = self.eng[q].dma_start(out=out_ap, in_=in_ap, **kw)
        self.dval[i] += 16
        ins.then_inc(self.dsem[i], 16)
        self._mark((k, self.dval[i]), reads, writes)
        return ins

    def finish(self):
        d = {}
        for b in self.outs:
            for k, v in b.w.items():
                if d.get(k, 0) < v:
                    d[k] = v
        self._wait("sp", d)
        self._wait("sp", {e: self.cnt[e] for e in self.eng if self.cnt[e] > 0 and e != "sp"})
        self.es.close()
        return self.nc


def run(prog_nc, in_maps):
    res = run_bass_kernel_spmd(prog_nc, in_maps, core_ids=list(range(len(in_maps))))
    return res.results


def _dma_custom(self, q, fn, reads=(), writes=()):
    self._wait(q, self._deps(reads, writes, is_dma=True))
    i = self._next_dsem()
    k = ("d", i)
    if self.dval[i] > 0:
        self._wait(q, {k: self.dval[i]})
    ins = fn(self.eng[q])
    self.dval[i] += 16
    ins.then_inc(self.dsem[i], 16)
    self._mark((k, self.dval[i]), reads, writes)
    return ins


Prog.dma_custom = _dma_custom


class _Stage:
    def __init__(self, p):
        self.p = p

    def __enter__(self):
        self.saved = self.p.es
        self.p.es = contextlib.ExitStack()
        return self

    def __exit__(self, *a):
        self.p.barrier()
        self.p.es.close()
        self.p.es = self.saved
        return False


def _barrier(self):
    alld = {("d", i): v for i, v in enumerate(self.dval) if v > 0}
    for e in self.eng:
        deps = dict(alld)
        for o in self.eng:
            if o != e and self.cnt[o] > 0:
                deps[o] = self.cnt[o]
        self._wait(e, deps)


Prog.barrier = _barrier
Prog.stage = lambda self: _Stage(self)


def _next_dsem(self):
    if self.dpool is None:
        i = self.dnext
        self.dnext = (self.dnext + 1) % self.NDMA
        return i
    lo, hi = self.dpool
    j = self.dnext_sub.get(self.dpool, 0)
    self.dnext_sub[self.dpool] = (j + 1) % (hi - lo)
    return lo + j


Prog._next_dsem = _next_dsem


def bcast_rows(ap_1d_or_2d, nparts):
    return ap_1d_or_2d.partition_broadcast(nparts)


def build_gemm(K, M, N, MB=512, NB=512):
    p = Prog()
    at = p.din("at", [K, M])
    b = p.din("b", [K, N])
    c = p.dout("c", [M, N])
    KP = min(K, 128)
    KC = (K + 127) // 128
    atv = at.t.rearrange("(kc kp) m -> kp kc m", kp=KP)
    bv = b.t.rearrange("(kc kp) n -> kp kc n", kp=KP)
    MB = min(MB, M)
    a_sb = [p.sb("a", [KP, KC, MB]) for _ in range(2)]
    b_sb = [p.sb("b", [KP, KC, NB]) for _ in range(2)]
    pss = [p.ps("ps", [128, NB]) for _ in range(4)]
    o_sb = [p.sb("o", [128, NB]) for _ in range(4)]
    ia = ib = io = 0
    for m0 in range(0, M, MB):
        mw = min(MB, M - m0)
        A = a_sb[ia % 2]
        ia += 1
        for kc in range(KC):
            p.dma("sp", A[:, kc, 0:mw], atv[:, kc, m0:m0 + mw], writes=[A])
        for n0 in range(0, N, NB):
            nw = min(NB, N - n0)
            Bt = b_sb[ib % 2]
            ib += 1
            for kc in range(KC):
                p.dma("act" if kc % 2 else "sp", Bt[:, kc, 0:nw], bv[:, kc, n0:n0 + nw], writes=[Bt])
            for mt in range(0, mw, 128):
                mm = min(128, mw - mt)
                P_ = pss[io % 4]
                O = o_sb[io % 4]
                io += 1
                for kc in range(KC):
                    p.op("pe", lambda e, kc=kc: e.matmul(P_[0:mm, 0:nw], A[:, kc, mt:mt + mm], Bt[:, kc, 0:nw],
                                                         start=(kc == 0), stop=(kc == KC - 1)),
                         reads=[A, Bt], writes=[P_])
                if io % 2:
                    p.op("act", lambda e: e.activation(out=O[0:mm, 0:nw], in_=P_[0:mm, 0:nw], func=AF.Copy),
                         reads=[P_], writes=[O])
                else:
                    p.op("dve", lambda e: e.tensor_copy(out=O[0:mm, 0:nw], in_=P_[0:mm, 0:nw]),
                         reads=[P_], writes=[O])
                p.dma("pool", c[m0 + mt:m0 + mt + mm, n0:n0 + nw], O[0:mm, 0:nw], reads=[O], writes=[c])
    return p.finish()


def build_ln(T, D, alpha, eps):
    p = Prog()
    x = p.din("x", [T, D])
    y = p.din("y", [T, D])
    w = p.din("w", [1, D])
    b = p.din("b", [1, D])
    o = p.dout("o", [T, D])
    wb = p.sb("wb", [128, D])
    bb = p.sb("bb", [128, D])
    p.dma("sp", wb[:], bcast_rows(w.t[0], 128), writes=[wb])
    p.dma("sp", bb[:], bcast_rows(b.t[0], 128), writes=[bb])
    NBUF = 2
    xs = [p.sb("x", [128, D]) for _ in range(NBUF)]
    ys = [p.sb("y", [128, D]) for _ in range(NBUF)]
    zs = [p.sb("z", [128, D]) for _ in range(NBUF)]
    st = [p.sb("st", [128, 8]) for _ in range(NBUF)]
    for i, t0 in enumerate(range(0, T, 128)):
        X, Y, Z, S = xs[i % NBUF], ys[i % NBUF], zs[i % NBUF], st[i % NBUF]
        p.dma("sp", X[:], x[t0:t0 + 128, :], writes=[X])
        p.dma("act", Y[:], y[t0:t0 + 128, :], writes=[Y])
        p.op("dve", lambda e: e.scalar_tensor_tensor(out=Z[:], in0=X[:], scalar=float(alpha), in1=Y[:],
                                                     op0=ALU.mult, op1=ALU.add), reads=[X, Y], writes=[Z])
        ln_rows(p, Z, X, S, wb, bb, D, eps)
        p.dma("pool", o[t0:t0 + 128, :], X[:], reads=[X], writes=[o])
    return p.finish()


def ln_rows(p, Z, OUT, S, wb, bb, D, eps, tmp=None):
    p.op("dve", lambda e: e.reduce_sum(out=S[:, 0:1], in_=Z[:], axis=AX.X), reads=[Z], writes=[S])
    p.op("dve", lambda e: e.tensor_scalar(out=S[:, 1:2], in0=S[:, 0:1], scalar1=1.0 / D, scalar2=None, op0=ALU.mult),
         reads=[S], writes=[S])
    p.op("dve", lambda e: e.tensor_scalar(out=Z[:], in0=Z[:], scalar1=S[:, 1:2], scalar2=None, op0=ALU.subtract),
         reads=[Z, S], writes=[Z])
    p.op("dve", lambda e: e.tensor_tensor(out=OUT[:], in0=Z[:], in1=Z[:], op=ALU.mult), reads=[Z], writes=[OUT])
    p.op("dve", lambda e: e.reduce_sum(out=S[:, 2:3], in_=OUT[:], axis=AX.X), reads=[OUT], writes=[S])
    rsqrt_col(p, S, 2, 3, 1.0 / D, eps)
    p.op("dve", lambda e: e.scalar_tensor_tensor(out=OUT[:], in0=Z[:], scalar=S[:, 3:4], in1=wb[:],
                                                 op0=ALU.mult, op1=ALU.mult), reads=[Z, S, wb], writes=[OUT])
    p.op("dve", lambda e: e.tensor_tensor(out=OUT[:], in0=OUT[:], in1=bb[:], op=ALU.add), reads=[OUT, bb], writes=[OUT])


def rsqrt_col(p, S, i, o, mul, eps):
    p.op("dve", lambda e: e.tensor_scalar(out=S[:, o:o + 1], in0=S[:, i:i + 1], scalar1=float(mul), scalar2=float(eps),
                                          op0=ALU.mult, op1=ALU.add), reads=[S], writes=[S])
    p.op("act", lambda e: e.sqrt(out=S[:, o:o + 1], in_=S[:, o:o + 1]), reads=[S], writes=[S])
    p.op("dve", lambda e: e.reciprocal(out=S[:, o:o + 1], in_=S[:, o:o + 1]), reads=[S], writes=[S])


def tt(p, eng, out, a, b, op, R, W):
    return p.op(eng, lambda e: e.tensor_tensor(out=out, in0=a, in1=b, op=op), reads=R, writes=W)


def ts(p, eng, out, a, s1, op0, R, W, s2=None, op1=None):
    if op1 is None:
        return p.op(eng, lambda e: e.tensor_scalar(out=out, in0=a, scalar1=s1, scalar2=None, op0=op0), reads=R, writes=W)
    return p.op(eng, lambda e: e.tensor_scalar(out=out, in0=a, scalar1=s1, scalar2=s2, op0=op0, op1=op1),
                reads=R, writes=W)


def actf(p, out, a, func, R, W, scale=1.0):
    return p.op("act", lambda e: e.activation(out=out, in_=a, func=func, scale=float(scale)), reads=R, writes=W)


def rsum(p, eng, out, a, R, W):
    return p.op(eng, lambda e: e.reduce_sum(out=out, in_=a, axis=AX.X), reads=R, writes=W)


def load_bcast(p, name, dram, n):
    t = p.sb(name, [128, n])
    p.dma("sp", t[:], dram.t[0].partition_broadcast(128), writes=[t])
    return t


def build_ew1(T, NCOL=3360, C=1024):
    p = Prog()
    x = p.din("p", [T, NCOL])
    xp = p.din("pp", [T, NCOL])
    mu = p.din("mu", [1, NCOL])
    o = p.dout("o", [T, NCOL])
    mub = load_bcast(p, "mub", mu, NCOL)
    NB = 2
    X = [p.sb("x", [128, NCOL]) for _ in range(NB)]
    XP = [p.sb("xp", [128, NCOL]) for _ in range(NB)]
    for i, t0 in enumerate(range(0, T, 128)):
        a, b = X[i % NB], XP[i % NB]
        p.dma("sp", a[:], x[t0:t0 + 128, :], writes=[a])
        p.dma("act", b[:], xp[t0:t0 + 128, :], writes=[b])
        tt(p, "dve", b[:], b[:], a[:], ALU.subtract, [a, b], [b])
        tt(p, "pool", b[:], b[:], mub[:], ALU.mult, [b, mub], [b])
        tt(p, "dve", a[:], a[:], b[:], ALU.add, [a, b], [a])
        o1 = 3 * C
        actf(p, a[:, o1:o1 + 64], a[:, o1:o1 + 64], AF.Tanh, [a], [a])
        actf(p, a[:, o1 + 128:NCOL], a[:, o1 + 128:NCOL], AF.Sigmoid, [a], [a])
        p.dma("pool", o[t0:t0 + 128, :], a[:], reads=[a], writes=[o])
    return p.finish()


def build_ew2(T, C=1024, N=64):
    H = C // N
    p = Prog()
    pf = p.din("pf", [T, 3 * C])
    wl = p.din("wl", [T, C])
    al = p.din("al", [T, C])
    prm = {k: load_bcast(p, k, p.din(k, [1, C]), C) for k in ("w0", "a0", "kk", "ka", "rk")}
    outs = {k: p.dout(k, [T, C]) for k in ("dec", "km", "a", "b", "bon")}
    NB = 2
    mk = lambda nm: [p.sb(nm, [128, C]) for _ in range(NB)]
    Rs, Ks, Vs, WLs, ALs, KKs, T1s, T2s = mk("r"), mk("k"), mk("v"), mk("wl"), mk("al"), mk("kk"), mk("t1"), mk("t2")
    SSs = [p.sb("ss", [128, 2 * H]) for _ in range(NB)]
    v3 = lambda ap: ap.rearrange("p (h j) -> p h j", j=N)
    for i, t0 in enumerate(range(0, T, 128)):
        R, K, V, WL, AL, KK, T1, T2, SS = (z[i % NB] for z in (Rs, Ks, Vs, WLs, ALs, KKs, T1s, T2s, SSs))
        p.dma("sp", R[:], pf[t0:t0 + 128, 0:C], writes=[R])
        p.dma("act", K[:], pf[t0:t0 + 128, C:2 * C], writes=[K])
        p.dma("sp", V[:], pf[t0:t0 + 128, 2 * C:3 * C], writes=[V])
        p.dma("act", WL[:], wl[t0:t0 + 128, :], writes=[WL])
        p.dma("sp", AL[:], al[t0:t0 + 128, :], writes=[AL])
        tt(p, "dve", WL[:], WL[:], prm["w0"][:], ALU.add, [WL, prm["w0"]], [WL])
        actf(p, WL[:], WL[:], AF.Sigmoid, [WL], [WL])
        actf(p, WL[:], WL[:], AF.Exp, [WL], [WL], scale=-0.6065306597126334)
        p.dma("pool", outs["dec"][t0:t0 + 128, :], WL[:], reads=[WL], writes=[outs["dec"]])
        tt(p, "pool", AL[:], AL[:], prm["a0"][:], ALU.add, [AL, prm["a0"]], [AL])
        actf(p, AL[:], AL[:], AF.Sigmoid, [AL], [AL])
        tt(p, "dve", KK[:], K[:], prm["kk"][:], ALU.mult, [K, prm["kk"]], [KK])
        tt(p, "pool", T1[:], KK[:], KK[:], ALU.mult, [KK], [T1])
        rsum(p, "dve", SS[:, 0:H], v3(T1[:]), [T1], [SS])
        p.op("act", lambda e: e.sqrt(out=SS[:, 0:H], in_=SS[:, 0:H]), reads=[SS], writes=[SS])
        ts(p, "dve", SS[:, 0:H], SS[:, 0:H], 1e-12, ALU.max, [SS], [SS])
        p.op("dve", lambda e: e.reciprocal(out=SS[:, 0:H], in_=SS[:, 0:H]), reads=[SS], writes=[SS])
        tt(p, "dve", v3(KK[:]), v3(KK[:]), SS[:, 0:H].unsqueeze(2).to_broadcast([128, H, N]), ALU.mult, [KK, SS], [KK])
        p.op("dve", lambda e: e.scalar_tensor_tensor(out=T1[:], in0=AL[:], scalar=-1.0, in1=prm["ka"][:],
                                                     op0=ALU.add, op1=ALU.mult), reads=[AL, prm["ka"]], writes=[T1])
        ts(p, "pool", T1[:], T1[:], 1.0, ALU.add, [T1], [T1])
        tt(p, "dve", T1[:], T1[:], K[:], ALU.mult, [T1, K], [T1])
        p.dma("pool", outs["km"][t0:t0 + 128, :], T1[:], reads=[T1], writes=[outs["km"]])
        tt(p, "pool", T2[:], KK[:], AL[:], ALU.mult, [KK, AL], [T2])
        p.dma("pool", outs["b"][t0:t0 + 128, :], T2[:], reads=[T2], writes=[outs["b"]])
        ts(p, "dve", KK[:], KK[:], -1.0, ALU.mult, [KK], [KK])
        p.dma("pool", outs["a"][t0:t0 + 128, :], KK[:], reads=[KK], writes=[outs["a"]])
        tt(p, "dve", R[:], R[:], T1[:], ALU.mult, [R, T1], [R])
        tt(p, "pool", R[:], R[:], prm["rk"][:], ALU.mult, [R, prm["rk"]], [R])
        rsum(p, "dve", SS[:, H:2 * H], v3(R[:]), [R], [SS])
        tt(p, "dve", v3(V[:]), v3(V[:]), SS[:, H:2 * H].unsqueeze(2).to_broadcast([128, H, N]), ALU.mult, [V, SS], [V])
        p.dma("pool", outs["bon"][t0:t0 + 128, :], V[:], reads=[V], writes=[outs["bon"]])
    return p.finish()


def build_scan(T, Tt=8):
    p = Prog()
    bc = p.din("bc", [5, 2, T * 256])
    vfm = p.din("vfm", [128, T * 4])
    yfm = p.dout("yfm", [128, T * 4])
    NB = 2
    BC = [[p.sb("bc%d" % a, [128, Tt * 256]) for a in range(5)] for _ in range(NB)]
    Vt = [p.sb("v", [128, Tt * 4]) for _ in range(NB)]
    KV = [p.sb("kv", [128, Tt * 256]) for _ in range(NB)]
    Yt = [p.sb("y", [128, Tt * 4]) for _ in range(NB)]
    S = [p.sb("s", [128, 256]) for _ in range(2)]
    S2 = p.sb("s2", [128, 256])
    TMP = p.sb("tmp", [128, 256])
    TMP2 = p.sb("tmp2", [128, 256])
    TMP3 = [p.sb("tmp3", [128, 256]) for _ in range(2)]
    SA = p.sb("sa", [128, 4])
    p.op("dve", lambda e: e.memset(S[0][:], 0.0), writes=[S[0]])
    v3 = lambda ap: ap.rearrange("p (g j) -> p g j", j=64)
    step = 0
    for it, t0 in enumerate(range(0, T, Tt)):
        sl = it % NB
        A, W, B, K, R = BC[sl]
        for a in range(5):
            for half in range(2):
                p.dma("sp" if (a + half) % 2 else "act", BC[sl][a][half * 64:(half + 1) * 64, :],
                      bc.t[a, half, t0 * 256:(t0 + Tt) * 256].partition_broadcast(64), writes=[BC[sl][a]])
        p.dma("sp", Vt[sl][:], vfm[:, t0 * 4:(t0 + Tt) * 4], writes=[Vt[sl]])
        tt(p, "pool", KV[sl][:].rearrange("p (tg j) -> p tg j", j=64), K[:].rearrange("p (tg j) -> p tg j", j=64),
           Vt[sl][:].unsqueeze(2).to_broadcast([128, Tt * 4, 64]), ALU.mult, [K, Vt[sl]], [KV[sl]])
        for q in range(Tt):
            cs = slice(q * 256, (q + 1) * 256)
            Sc, Sn = S[step % 2], S[(step + 1) % 2]
            T3 = TMP3[step % 2]
            tt(p, "dve", TMP[:], Sc[:], A[:, cs], ALU.mult, [Sc, A], [TMP])
            tt(p, "pool", S2[:], Sc[:], W[:, cs], ALU.mult, [Sc, W], [S2])
            rsum(p, "dve", SA[:], v3(TMP[:]), [TMP], [SA])
            tt(p, "dve", v3(TMP2[:]), v3(B[:, cs]), SA[:].unsqueeze(2).to_broadcast([128, 4, 64]), ALU.mult, [B, SA], [TMP2])
            tt(p, "dve", S2[:], S2[:], TMP2[:], ALU.add, [S2, TMP2], [S2])
            tt(p, "dve", Sn[:], S2[:], KV[sl][:, cs], ALU.add, [S2, KV[sl]], [Sn])
            tt(p, "pool", T3[:], Sn[:], R[:, cs], ALU.mult, [Sn, R], [T3])
            rsum(p, "dve", Yt[sl][:, q * 4:(q + 1) * 4], v3(T3[:]), [T3], [Yt[sl]])
            step += 1
        p.dma("pool", yfm[:, t0 * 4:(t0 + Tt) * 4], Yt[sl][:], reads=[Yt[sl]], writes=[yfm])
    return p.finish()


def scan_host_in(a, w, b, k, r, v):
    T = a.shape[0]
    def lay(x):
        return np.ascontiguousarray(x.reshape(T, 4, 2, 64).transpose(2, 0, 1, 3)).reshape(2, T * 256)
    bc = np.stack([lay(a), lay(w), lay(b), lay(k), lay(r)], 0)
    vfm = np.ascontiguousarray(v.reshape(T, 4, 2, 64).transpose(2, 3, 0, 1)).reshape(128, T * 4)
    return {"bc": bc, "vfm": vfm}


def scan_host_out(yfm, T):
    return np.ascontiguousarray(yfm.reshape(2, 64, T, 4).transpose(2, 3, 0, 1)).reshape(T, 8, 64)


def rsqrt_cols(p, S, i, o, n, mul, eps):
    ts(p, "dve", S[:, o:o + n], S[:, i:i + n], float(mul), ALU.mult, [S], [S], s2=float(eps), op1=ALU.add)
    p.op("act", lambda e: e.sqrt(out=S[:, o:o + n], in_=S[:, o:o + n]), reads=[S], writes=[S])
    p.op("dve", lambda e: e.reciprocal(out=S[:, o:o + n], in_=S[:, o:o + n]), reads=[S], writes=[S])


def build_attn(NH, NC, DK, DV, Tq, Tk, causal, scale):
    p = Prog()
    qT = p.din("qT", [NH * NC, DK, Tq])
    kT = p.din("kT", [NH * NC, DK, Tk])
    v = p.din("v", [NH, Tk, DV])
    oT = p.dout("oT", [NH * NC, DV, Tq])
    NKT = Tk // 128
    QB = min(512, Tq)
    if causal:
        msk = p.din("mask", [4, 128, 512])
        M = p.sb("mask", [128, 4, 512])
        for j in range(4):
            p.dma("sp", M[:, j, :], msk[j], writes=[M])
    ones = p.sb("ones", [128, 128])
    p.op("dve", lambda e: e.memset(ones[:], 1.0), writes=[ones])
    QT = [p.sb("qT", [DK, Tq]) for _ in range(2)]
    KT = [p.sb("kT", [DK, Tk]) for _ in range(2)]
    V = [p.sb("v", [128, NKT, DV]) for _ in range(2)]
    PS_S = [p.ps("pss", [128, 512]) for _ in range(2)]
    PS_O = [p.ps("pso", [128, 512]) for _ in range(2)]
    PS_Z = [p.ps("psz", [128, 512]) for _ in range(2)]
    P = [p.sb("p", [128, 512]) for _ in range(3)]
    O = [p.sb("o", [128, 512]) for _ in range(2)]
    RZ = [p.sb("rz", [128, 512]) for _ in range(2)]
    ihc = 0
    iq = 0
    ik = 0
    for h in range(NH):
        Vh = V[h % 2]
        p.dma("sp", Vh[:], v.t[h].rearrange("(kt p) d -> p kt d", p=128), writes=[Vh])
        for c in range(NC):
            hc = h * NC + c
            Q, K = QT[ihc % 2], KT[ihc % 2]
            ihc += 1
            p.dma("act", Q[:], qT[hc], writes=[Q])
            p.dma("sp", K[:], kT[hc], writes=[K])
            for qb in range(Tq // QB):
                q0 = qb * QB
                nkt = min(NKT, (q0 + QB) // 128) if causal else NKT
                pso, psz = PS_O[iq % 2], PS_Z[iq % 2]
                ob, rz = O[iq % 2], RZ[iq % 2]
                iq += 1

                def oz(kt, Pt):
                    p.op("pe", lambda e: e.matmul(pso[0:DV, 0:QB], Vh[:, kt, :], Pt[:, 0:QB], start=(kt == 0), stop=(kt == nkt - 1)),
                         reads=[Vh, Pt], writes=[pso])
                    p.op("pe", lambda e: e.matmul(psz[:, 0:QB], ones[:], Pt[:, 0:QB], start=(kt == 0), stop=(kt == nkt - 1)),
                         reads=[ones, Pt], writes=[psz])
                prev = None
                for kt in range(nkt):
                    pss = PS_S[ik % 2]
                    Pt = P[ik % 3]
                    ik += 1
                    p.op("pe", lambda e: e.matmul(pss[:, 0:QB], K[:, kt * 128:(kt + 1) * 128], Q[:, q0:q0 + QB], start=True, stop=True),
                         reads=[K, Q], writes=[pss])
                    actf(p, Pt[:, 0:QB], pss[:, 0:QB], AF.Exp, [pss], [Pt], scale=scale)
                    if causal and kt * 128 + 127 > q0:
                        j = kt - q0 // 128
                        tt(p, "pool", Pt[:, 0:QB], Pt[:, 0:QB], M[:, j, 0:QB], ALU.mult, [Pt, M], [Pt])
                    if prev is not None:
                        oz(*prev)
                    prev = (kt, Pt)
                oz(*prev)
                p.op("dve", lambda e: e.reciprocal(out=rz[0:DV, 0:QB], in_=psz[0:DV, 0:QB]), reads=[psz], writes=[rz])
                tt(p, "dve", ob[0:DV, 0:QB], pso[0:DV, 0:QB], rz[0:DV, 0:QB], ALU.mult, [pso, rz], [ob])
                p.dma("pool", oT[hc, :, q0:q0 + QB], ob[0:DV, 0:QB], reads=[ob], writes=[oT])
    return p.finish()


def causal_masks():
    k = np.arange(128)[:, None]
    q = np.arange(512)[None, :]
    return np.stack([(j * 128 + k <= q) for j in range(4)], 0).astype(np.float32)


def build_ew3(T, lam_init, gn_eps, ln_eps, C=1024):
    p = Prog()
    names = ("y", "bon", "g", "o0", "o1")
    ins = {k: p.din(k, [T, C]) for k in names}
    lnxw = load_bcast(p, "lnxw", p.din("lnxw", [1, C]), C)
    lnxb = load_bcast(p, "lnxb", p.din("lnxb", [1, C]), C)
    sw = load_bcast(p, "sw", p.din("sw", [1, C]), C)
    lv = {k: load_bcast(p, k, p.din(k, [1, 64]), 64) for k in ("lq1", "lk1", "lq2", "lk2")}
    mix = p.dout("mix", [T, 2 * C])
    L = p.sb("lam", [128, 8])
    tt(p, "dve", lv["lq1"][:], lv["lq1"][:], lv["lk1"][:], ALU.mult, [lv["lq1"], lv["lk1"]], [lv["lq1"]])
    tt(p, "dve", lv["lq2"][:], lv["lq2"][:], lv["lk2"][:], ALU.mult, [lv["lq2"], lv["lk2"]], [lv["lq2"]])
    rsum(p, "dve", L[:, 0:1], lv["lq1"][:], [lv["lq1"]], [L])
    rsum(p, "dve", L[:, 1:2], lv["lq2"][:], [lv["lq2"]], [L])
    actf(p, L[:, 0:2], L[:, 0:2], AF.Exp, [L], [L])
    tt(p, "dve", L[:, 2:3], L[:, 1:2], L[:, 0:1], ALU.subtract, [L], [L])
    ts(p, "dve", L[:, 3:4], L[:, 2:3], -float(lam_init), ALU.add, [L], [L])
    NB = 2
    mk = lambda nm: [p.sb(nm, [128, C]) for _ in range(NB)]
    B_ = {k: mk(k) for k in names}
    TM = mk("tm")
    ST = [p.sb("st", [128, 64]) for _ in range(NB)]
    v64 = lambda ap: ap.rearrange("p (h j) -> p h j", j=64)
    v128 = lambda ap: ap.rearrange("p (h j) -> p h j", j=128)
    for i, t0 in enumerate(range(0, T, 128)):
        Y, BON, G, O0, O1 = (B_[k][i % NB] for k in names)
        X, S = TM[i % NB], ST[i % NB]
        for j, k in enumerate(names):
            p.dma("sp" if j % 2 else "act", B_[k][i % NB][:], ins[k][t0:t0 + 128, :], writes=[B_[k][i % NB]])
        rsum(p, "dve", S[:, 0:16], v64(Y[:]), [Y], [S])
        ts(p, "dve", S[:, 0:16], S[:, 0:16], 1.0 / 64, ALU.mult, [S], [S])
        tt(p, "dve", v64(Y[:]), v64(Y[:]), S[:, 0:16].unsqueeze(2).to_broadcast([128, 16, 64]), ALU.subtract, [Y, S], [Y])
        tt(p, "pool", X[:], Y[:], Y[:], ALU.mult, [Y], [X])
        rsum(p, "dve", S[:, 16:32], v64(X[:]), [X], [S])
        rsqrt_cols(p, S, 16, 16, 16, 1.0 / 64, gn_eps)
        tt(p, "dve", v64(Y[:]), v64(Y[:]), S[:, 16:32].unsqueeze(2).to_broadcast([128, 16, 64]), ALU.mult, [Y, S], [Y])
        tt(p, "pool", Y[:], Y[:], lnxw[:], ALU.mult, [Y, lnxw], [Y])
        tt(p, "pool", Y[:], Y[:], lnxb[:], ALU.add, [Y, lnxb], [Y])
        tt(p, "dve", Y[:], Y[:], BON[:], ALU.add, [Y, BON], [Y])
        tt(p, "pool", Y[:], Y[:], G[:], ALU.mult, [Y, G], [Y])
        p.dma("pool", mix[t0:t0 + 128, 0:C], Y[:], reads=[Y], writes=[mix])
        p.op("dve", lambda e: e.scalar_tensor_tensor(out=O0[:], in0=O1[:], scalar=L[:, 3:4], in1=O0[:],
                                                     op0=ALU.mult, op1=ALU.add), reads=[O0, O1, L], writes=[O0])
        tt(p, "pool", O1[:], O0[:], O0[:], ALU.mult, [O0], [O1])
        rsum(p, "dve", S[:, 32:40], v128(O1[:]), [O1], [S])
        rsqrt_cols(p, S, 32, 32, 8, 1.0 / 128, ln_eps)
        tt(p, "dve", v128(O0[:]), v128(O0[:]), S[:, 32:40].unsqueeze(2).to_broadcast([128, 8, 128]), ALU.mult, [O0, S], [O0])
        p.op("dve", lambda e: e.scalar_tensor_tensor(out=O0[:], in0=O0[:], scalar=float(1.0 - lam_init), in1=sw[:],
                                                     op0=ALU.mult, op1=ALU.mult), reads=[O0, sw], writes=[O0])
        p.dma("pool", mix[t0:t0 + 128, C:2 * C], O0[:], reads=[O0], writes=[mix])
    return p.finish()


def build_rope(T, G=32):
    p = Prog()
    x = p.din("x", [T, G * 64])
    pos = p.din("pos", [T, 1], I32)
    invf = load_bcast(p, "invf", p.din("invf", [1, 8]), 8)
    o = p.dout("o", [T, G * 64])
    NB = 2
    X = [p.sb("x", [128, G * 64]) for _ in range(NB)]
    PI = [p.sb("pi", [128, 1], I32) for _ in range(NB)]
    PF = [p.sb("pf", [128, 1]) for _ in range(NB)]
    CS = [p.sb("cs", [128, 16]) for _ in range(NB)]
    kf = p.sb("kf", [128, 16])
    ki = p.sb("ki", [128, 16], I32)
    TA = [p.sb("ta", [128, G, 8]) for _ in range(NB)]
    TB = [p.sb("tb", [128, G, 8]) for _ in range(NB)]
    TC = [p.sb("tc", [128, G, 8]) for _ in range(NB)]
    TD = [p.sb("td", [128, G, 8]) for _ in range(NB)]
    PI_ = 3.141592653589793
    for i, t0 in enumerate(range(0, T, 128)):
        x_, pi_, pf_, cs, ta, tb, tc, td = (z[i % NB] for z in (X, PI, PF, CS, TA, TB, TC, TD))
        p.dma("sp", x_[:], x[t0:t0 + 128, :], writes=[x_])
        p.dma("act", pi_[:], pos[t0:t0 + 128, :], writes=[pi_])
        p.op("dve", lambda e: e.tensor_copy(out=pf_[:], in_=pi_[:]), reads=[pi_], writes=[pf_])
        ts(p, "dve", cs[:, 0:8], invf[:], pf_[:, 0:1], ALU.mult, [invf, pf_], [cs])
        ts(p, "dve", cs[:, 8:16], cs[:, 0:8], 0.5 * PI_, ALU.add, [cs], [cs])
        ts(p, "dve", kf[:], cs[:], 1.0 / (2 * PI_), ALU.mult, [cs], [kf])
        p.op("dve", lambda e: e.tensor_copy(out=ki[:], in_=kf[:]), reads=[kf], writes=[ki])
        p.op("dve", lambda e: e.tensor_copy(out=kf[:], in_=ki[:]), reads=[ki], writes=[kf])
        p.op("dve", lambda e: e.scalar_tensor_tensor(out=cs[:], in0=kf[:], scalar=-6.28125, in1=cs[:],
                                                     op0=ALU.mult, op1=ALU.add), reads=[kf, cs], writes=[cs])
        p.op("dve", lambda e: e.scalar_tensor_tensor(out=cs[:], in0=kf[:], scalar=-0.0019353071795864769, in1=cs[:],
                                                     op0=ALU.mult, op1=ALU.add), reads=[kf, cs], writes=[cs])
        ts(p, "dve", kf[:], cs[:], PI_, ALU.is_gt, [cs], [kf])
        p.op("dve", lambda e: e.scalar_tensor_tensor(out=cs[:], in0=kf[:], scalar=-2 * PI_, in1=cs[:],
                                                     op0=ALU.mult, op1=ALU.add), reads=[kf, cs], writes=[cs])
        ts(p, "dve", cs[:], cs[:], PI_, ALU.min, [cs], [cs], s2=-PI_, op1=ALU.max)
        actf(p, cs[:], cs[:], AF.Sin, [cs], [cs])
        xv = x_[:].rearrange("p (g d) -> p g d", d=64)
        t1, t2 = xv[:, :, 0:8], xv[:, :, 8:16]
        sinb = cs[:, 0:8].unsqueeze(1).to_broadcast([128, G, 8])
        cosb = cs[:, 8:16].unsqueeze(1).to_broadcast([128, G, 8])
        tt(p, "dve", ta[:], t1, cosb, ALU.mult, [x_, cs], [ta])
        tt(p, "pool", tb[:], t2, sinb, ALU.mult, [x_, cs], [tb])
        tt(p, "dve", tc[:], t2, cosb, ALU.mult, [x_, cs], [tc])
        tt(p, "pool", td[:], t1, sinb, ALU.mult, [x_, cs], [td])
        tt(p, "dve", t1, ta[:], tb[:], ALU.subtract, [ta, tb], [x_])
        tt(p, "dve", t2, tc[:], td[:], ALU.add, [tc, td], [x_])
        p.dma("pool", o[t0:t0 + 128, :], x_[:], reads=[x_], writes=[o])
    return p.finish()


def build_topk(T, H=8, NK=128, K=16):
    p = Prog()
    s1 = p.din("s1", [T, H * NK])
    s2 = p.din("s2", [T, H * NK])
    iota_d = p.din("iota", [1, K * K])
    ex = p.dout("ex", [T, H * K], I32)
    gt = p.dout("gt", [T, H * K])
    IOTA = load_bcast(p, "iota", iota_d, K * K)
    NB = 2
    S = [[p.sb("s", [128, H * NK]) for _ in range(2)] for _ in range(NB)]
    MV = [p.sb("mv", [128, K]) for _ in range(2)]
    MI = [p.sb("mi", [128, K], U32) for _ in range(2)]
    MF = [p.sb("mf", [128, K]) for _ in range(2)]
    WK = p.sb("wk", [128, NK])
    CAND = p.sb("cand", [128, K * K])
    CE = p.sb("ce", [128, K * K])
    WK2 = p.sb("wk2", [128, K * K])
    FV = p.sb("fv", [128, K])
    FI = p.sb("fi", [128, K], U32)
    FF = p.sb("ff", [128, K])
    EQ = p.sb("eq", [128, K, K * K])
    SM = p.sb("sm", [128, 4])
    E = p.sb("e", [128, K])
    EX = [p.sb("ex", [128, H * K]) for _ in range(NB)]
    EXI = [p.sb("exi", [128, H * K], I32) for _ in range(NB)]
    GT = [p.sb("gt", [128, H * K]) for _ in range(NB)]
    NEG = -1e30

    def top16(src, srcbuf, mv, mi, wk):
        p.op("dve", lambda e: e.max(out=mv[:, 0:8], in_=src), reads=[srcbuf], writes=[mv])
        p.op("dve", lambda e: e.max_index(out=mi[:, 0:8], in_max=mv[:, 0:8], in_values=src), reads=[srcbuf, mv], writes=[mi])
        p.op("dve", lambda e: e.match_replace(out=wk[:], in_to_replace=mv[:, 0:8], in_values=src, imm_value=NEG),
             reads=[srcbuf, mv], writes=[wk])
        p.op("dve", lambda e: e.max(out=mv[:, 8:16], in_=wk[:]), reads=[wk], writes=[mv])
        p.op("dve", lambda e: e.max_index(out=mi[:, 8:16], in_max=mv[:, 8:16], in_values=wk[:]), reads=[wk, mv], writes=[mi])

    for i, t0 in enumerate(range(0, T, 128)):
        S1, S2 = S[i % NB]
        ex_, exi_, gt_ = EX[i % NB], EXI[i % NB], GT[i % NB]
        p.dma("sp", S1[:], s1[t0:t0 + 128, :], writes=[S1])
        p.dma("act", S2[:], s2[t0:t0 + 128, :], writes=[S2])
        for h in range(H):
            for c, Sx in ((0, S1), (1, S2)):
                top16(Sx[:, h * NK:(h + 1) * NK], Sx, MV[c], MI[c], WK)
                p.op("dve", lambda e: e.tensor_copy(out=MF[c][:], in_=MI[c][:]), reads=[MI[c]], writes=[MF[c]])
            c3 = lambda ap: ap.rearrange("p (a b) -> p a b", b=K)
            tt(p, "dve", c3(CAND[:]), MV[0][:].unsqueeze(2).to_broadcast([128, K, K]),
               MV[1][:].unsqueeze(1).to_broadcast([128, K, K]), ALU.add, [MV[0], MV[1]], [CAND])
            ts(p, "dve", MF[0][:], MF[0][:], float(NK), ALU.mult, [MF[0]], [MF[0]])
            tt(p, "dve", c3(CE[:]), MF[0][:].unsqueeze(2).to_broadcast([128, K, K]),
               MF[1][:].unsqueeze(1).to_broadcast([128, K, K]), ALU.add, [MF[0], MF[1]], [CE])
            top16(CAND[:], CAND, FV, FI, WK2)
            p.op("dve", lambda e: e.tensor_copy(out=FF[:], in_=FI[:]), reads=[FI], writes=[FF])
            tt(p, "dve", EQ[:], FF[:].unsqueeze(2).to_broadcast([128, K, K * K]),
               IOTA[:].unsqueeze(1).to_broadcast([128, K, K * K]), ALU.is_equal, [FF, IOTA], [EQ])
            tt(p, "dve", EQ[:], EQ[:], CE[:].unsqueeze(1).to_broadcast([128, K, K * K]), ALU.mult, [EQ, CE], [EQ])
            rsum(p, "dve", ex_[:, h * K:(h + 1) * K], EQ[:], [EQ], [ex_])
            ts(p, "dve", SM[:, 0:1], FV[:, 0:1], -1.0, ALU.mult, [FV], [SM])
            p.op("act", lambda e: e.activation(out=E[:], in_=FV[:], func=AF.Exp, bias=SM[:, 0:1], scale=1.0),
                 reads=[FV, SM], writes=[E])
            rsum(p, "dve", SM[:, 1:2], E[:], [E], [SM])
            p.op("dve", lambda e: e.reciprocal(out=SM[:, 2:3], in_=SM[:, 1:2]), reads=[SM], writes=[SM])
            ts(p, "dve", gt_[:, h * K:(h + 1) * K], E[:], SM[:, 2:3], ALU.mult, [E, SM], [gt_])
        p.op("dve", lambda e: e.tensor_copy(out=exi_[:], in_=ex_[:]), reads=[ex_], writes=[exi_])
        p.dma("pool", ex[t0:t0 + 128, :], exi_[:], reads=[exi_], writes=[ex])
        p.dma("pool", gt[t0:t0 + 128, :], gt_[:], reads=[gt_], writes=[gt])
    return p.finish()


def build_peer(T, D=2048, NE=16384, NS=128):
    p = Prog()
    x = p.din("x", [T, D])
    exT = p.din("exT", [NS, T], I32)
    gtT = p.din("gtT", [NS, T])
    pu = p.din("pu", [NE, D])
    pv = p.din("pv", [NE, D])
    c2d = p.din("c2", [1, 255])
    y = p.dout("y", [T, D])
    C2 = load_bcast(p, "c2", c2d, 255)
    EXT = p.sb("ext", [NS, T], I32)
    GTT = p.sb("gtt", [NS, T])
    p.dma("sp", EXT[:], exT[:, :], writes=[EXT])
    p.dma("act", GTT[:], gtT[:, :], writes=[GTT])
    NB = 3
    UG = [p.sb("ug", [NS, D]) for _ in range(NB)]
    VG = [p.sb("vg", [NS, D]) for _ in range(NB)]
    XB = [p.sb("xb", [NS, D]) for _ in range(NB)]
    JK = [p.sb("junk", [NS, D]) for _ in range(2)]
    HT = [p.sb("ht", [NS, 4]) for _ in range(NB)]
    ACM = [p.sb("acm", [NS, 128]) for _ in range(NB)]
    PSY = [p.ps("psy", [128, 512]) for _ in range(4)]
    YO = [p.sb("yo", [128, 512]) for _ in range(4)]
    for t in range(T):
        tl = t % 128
        ug, vg, xb, ht, acm = (z[t % NB] for z in (UG, VG, XB, HT, ACM))
        p.dma_custom("pool", lambda e: e.indirect_dma_start(
            out=ug[:, :], out_offset=None, in_=pu[:, :],
            in_offset=bass.IndirectOffsetOnAxis(ap=EXT[:, t:t + 1], axis=0)), reads=[EXT, pu], writes=[ug])
        p.dma_custom("pool", lambda e: e.indirect_dma_start(
            out=vg[:, :], out_offset=None, in_=pv[:, :],
            in_offset=bass.IndirectOffsetOnAxis(ap=EXT[:, t:t + 1], axis=0)), reads=[EXT, pv], writes=[vg])
        p.dma("sp" if t % 2 else "act", xb[:], x.t[t].partition_broadcast(NS), reads=[x], writes=[xb])
        jk = JK[t % 2]
        tt(p, "pool" if t % 2 else "dve", jk[:], ug[:], xb[:], ALU.mult, [ug, xb], [jk])
        rsum(p, "dve", ht[:, 0:1], jk[:], [jk], [ht])
        actf(p, ht[:, 1:2], ht[:, 0:1], AF.Gelu, [ht], [ht])
        tt(p, "dve", ht[:, 2:3], ht[:, 1:2], GTT[:, t:t + 1], ALU.mult, [ht, GTT], [ht])
        ts(p, "dve", acm[:], C2[:, 127 - tl:255 - tl], ht[:, 2:3], ALU.mult, [C2, ht], [acm])
        for n in range(4):
            p.op("pe", lambda e: e.matmul(PSY[n][:, :], acm[:], vg[:, n * 512:(n + 1) * 512],
                                          start=(tl == 0), stop=(tl == 127 or t == T - 1)),
                 reads=[acm, vg], writes=[PSY[n]])
        if tl == 127 or t == T - 1:
            t0 = t - tl
            nr = tl + 1
            for n in range(4):
                if n % 2:
                    p.op("act", lambda e: e.activation(out=YO[n][:], in_=PSY[n][:], func=AF.Copy), reads=[PSY[n]], writes=[YO[n]])
                else:
                    p.op("dve", lambda e: e.tensor_copy(out=YO[n][:], in_=PSY[n][:]), reads=[PSY[n]], writes=[YO[n]])
                p.dma("sp", y[t0:t0 + nr, n * 512:(n + 1) * 512], YO[n][0:nr, :], reads=[YO[n]], writes=[y])
    return p.finish()


def st_gemm(p, at_b, at_ap, b_b, b_ap, c_b, c_ap, K, M, N, MB=512, NB=512):
    with p.stage():
        KC = (K + 127) // 128
        MB = min(MB, M)
        NB = min(NB, N)
        a_sb = [p.sb("a", [128, KC, MB]) for _ in range(2)]
        b_sb = [p.sb("b", [128, KC, NB]) for _ in range(2)]
        pss = [p.ps("ps", [128, 512]) for _ in range(4)]
        o_sb = [p.sb("o", [128, NB]) for _ in range(4)]
        ia = ib = io = 0
        for m0 in range(0, M, MB):
            mw = min(MB, M - m0)
            A = a_sb[ia % 2]
            ia += 1
            for kc in range(KC):
                kr = min(128, K - kc * 128)
                p.dma("sp" if kc % 2 else "act", A[0:kr, kc, 0:mw], at_ap[kc * 128:kc * 128 + kr, m0:m0 + mw], reads=[at_b], writes=[A])
            for n0 in range(0, N, NB):
                nw = min(NB, N - n0)
                Bt = b_sb[ib % 2]
                ib += 1
                for kc in range(KC):
                    kr = min(128, K - kc * 128)
                    p.dma("act" if kc % 2 else "sp", Bt[0:kr, kc, 0:nw], b_ap[kc * 128:kc * 128 + kr, n0:n0 + nw], reads=[b_b], writes=[Bt])
                for mt in range(0, mw, 128):
                    mm = min(128, mw - mt)
                    P_ = pss[io % 4]
                    O = o_sb[io % 4]
                    io += 1
                    for kc in range(KC):
                        kr = min(128, K - kc * 128)
                        p.op("pe", lambda e, kc=kc, kr=kr: e.matmul(P_[0:mm, 0:nw], A[0:kr, kc, mt:mt + mm], Bt[0:kr, kc, 0:nw],
                                                                    start=(kc == 0), stop=(kc == KC - 1)),
                             reads=[A, Bt], writes=[P_])
                    if io % 2:
                        p.op("act", lambda e: e.activation(out=O[0:mm, 0:nw], in_=P_[0:mm, 0:nw], func=AF.Copy), reads=[P_], writes=[O])
                    else:
                        p.op("dve", lambda e: e.tensor_copy(out=O[0:mm, 0:nw], in_=P_[0:mm, 0:nw]), reads=[P_], writes=[O])
                    p.dma("pool", c_ap[m0 + mt:m0 + mt + mm, n0:n0 + nw], O[0:mm, 0:nw], reads=[O], writes=[c_b])


def st_transpose(p, s_b, s_ap, d_b, d_ap, T, N, ident, rows=None):
    with p.stage():
        CB = 512
        X = [p.sb("x", [128, CB]) for _ in range(3)]
        PS = [p.ps("ps", [128, 512]) for _ in range(3)]
        O = [p.sb("o", [128, 4, 128]) for _ in range(3)]
        i = 0
        for t0 in range(0, T, 128):
            for n0 in range(0, N, CB):
                nw = min(CB, N - n0)
                x, ps, o = X[i % 3], PS[i % 3], O[i % 3]
                i += 1
                src_rows = s_ap[t0:t0 + 128, :] if rows is None else rows(t0)
                p.dma("sp" if i % 2 else "act", x[:, 0:nw], src_rows[:, n0:n0 + nw], reads=[s_b], writes=[x])
                nb = (nw + 127) // 128
                for j in range(nb):
                    cw = min(128, nw - j * 128)
                    p.op("pe", lambda e, j=j, cw=cw: e.transpose(out=ps[0:cw, j * 128:(j + 1) * 128], in_=x[:, j * 128:j * 128 + cw],
                                                                 identity=ident[:, :]), reads=[x, ident], writes=[ps])
                ov = o[:].rearrange("p j t -> p (j t)")
                if i % 2:
                    p.op("act", lambda e: e.activation(out=ov[:, 0:nb * 128], in_=ps[:, 0:nb * 128], func=AF.Copy), reads=[ps], writes=[o])
                else:
                    p.op("dve", lambda e: e.tensor_copy(out=ov[:, 0:nb * 128], in_=ps[:, 0:nb * 128]), reads=[ps], writes=[o])
                for j in range(nb):
                    cw = min(128, nw - j * 128)
                    p.dma("pool", d_ap[n0 + j * 128:n0 + j * 128 + cw, t0:t0 + 128], o[0:cw, j, :], reads=[o], writes=[d_b])


def st_ln(p, x_b, x_ap, y_b, y_ap, w_d, b_d, o_b, o_ap, T, D, alpha, eps):
    with p.stage():
        wb = load_bcast(p, "wb", w_d, D)
        bb = load_bcast(p, "bb", b_d, D)
        NBUF = 2
        xs = [p.sb("x", [128, D]) for _ in range(NBUF)]
        ys = [p.sb("y", [128, D]) for _ in range(NBUF)]
        zs = [p.sb("z", [128, D]) for _ in range(NBUF)]
        st = [p.sb("st", [128, 8]) for _ in range(NBUF)]
        for i, t0 in enumerate(range(0, T, 128)):
            X, Y, Z, S = xs[i % NBUF], ys[i % NBUF], zs[i % NBUF], st[i % NBUF]
            p.dma("sp", X[:], x_ap[t0:t0 + 128, :], reads=[x_b], writes=[X])
            p.dma("act", Y[:], y_ap[t0:t0 + 128, :], reads=[y_b], writes=[Y])
            p.op("dve", lambda e: e.scalar_tensor_tensor(out=Z[:], in0=X[:], scalar=float(alpha), in1=Y[:],
                                                         op0=ALU.mult, op1=ALU.add), reads=[X, Y], writes=[Z])
            ln_rows(p, Z, X, S, wb, bb, D, eps)
            p.dma("pool", o_ap[t0:t0 + 128, :], X[:], reads=[X], writes=[o_b])


def st_ew1(p, P_b, P_ap, mu_d, o_b, o_ap, T, NCOL, C3):
    with p.stage():
        mub = load_bcast(p, "mub", mu_d, NCOL)
        NB = 2
        X = [p.sb("x", [128, NCOL]) for _ in range(NB)]
        XP = [p.sb("xp", [128, NCOL]) for _ in range(NB)]
        for i, t0 in enumerate(range(0, T, 128)):
            a, b = X[i % NB], XP[i % NB]
            p.dma("sp", a[:], P_ap[t0:t0 + 128, 0:NCOL], reads=[P_b], writes=[a])
            if t0 == 0:
                p.op("dve", lambda e: e.memset(b[:], 0.0), writes=[b])
                p.dma("act", b[1:128, :], P_ap[0:127, 0:NCOL], reads=[P_b], writes=[b])
            else:
                p.dma("act", b[:], P_ap[t0 - 1:t0 + 127, 0:NCOL], reads=[P_b], writes=[b])
            tt(p, "dve", b[:], b[:], a[:], ALU.subtract, [a, b], [b])
            tt(p, "pool", b[:], b[:], mub[:], ALU.mult, [b, mub], [b])
            tt(p, "dve", a[:], a[:], b[:], ALU.add, [a, b], [a])
            actf(p, a[:, C3:C3 + 64], a[:, C3:C3 + 64], AF.Tanh, [a], [a])
            actf(p, a[:, C3 + 128:NCOL], a[:, C3 + 128:NCOL], AF.Sigmoid, [a], [a])
            p.dma("pool", o_ap[t0:t0 + 128, :], a[:], reads=[a], writes=[o_b])


def scan_out(p, dst, X, t0, roff):
    for hf in range(2):
        p.dma("pool", dst.t[hf, roff + t0:roff + t0 + 128, :].rearrange("t (g j) -> t g j", j=64),
              X[:].rearrange("p (g hf j) -> p hf g j", hf=2, j=64)[:, hf], reads=[X], writes=[dst])


def st_ew2(p, O1_b, O1_ap, WL, AL, prm_d, outs, T, C, N=64):
    H = C // N
    with p.stage():
        prm = {k: load_bcast(p, k, prm_d[k], C) for k in ("w0", "a0", "kk", "ka", "rk")}
        NB = 2
        mk = lambda nm: [p.sb(nm, [128, C]) for _ in range(NB)]
        Rs, Ks, Vs, WLs, ALs, KKs, T1s, T2s = mk("r"), mk("k"), mk("v"), mk("wl"), mk("al"), mk("kk"), mk("t1"), mk("t2")
        SSs = [p.sb("ss", [128, 2 * H]) for _ in range(NB)]
        v3 = lambda ap: ap.rearrange("p (h j) -> p h j", j=N)
        for i, t0 in enumerate(range(0, T, 128)):
            R, K, V, WLt, ALt, KK, T1, T2, SS = (z[i % NB] for z in (Rs, Ks, Vs, WLs, ALs, KKs, T1s, T2s, SSs))
            p.dma("sp", R[:], O1_ap[t0:t0 + 128, 0:C], reads=[O1_b], writes=[R])
            p.dma("act", K[:], O1_ap[t0:t0 + 128, C:2 * C], reads=[O1_b], writes=[K])
            p.dma("sp", V[:], O1_ap[t0:t0 + 128, 2 * C:3 * C], reads=[O1_b], writes=[V])
            scan_out(p, outs["r"], R, t0, 1)
            p.dma("act", WLt[:], WL[t0:t0 + 128, :], reads=[WL], writes=[WLt])
            p.dma("sp", ALt[:], AL[t0:t0 + 128, :], reads=[AL], writes=[ALt])
            tt(p, "dve", WLt[:], WLt[:], prm["w0"][:], ALU.add, [WLt, prm["w0"]], [WLt])
            actf(p, WLt[:], WLt[:], AF.Sigmoid, [WLt], [WLt])
            actf(p, WLt[:], WLt[:], AF.Exp, [WLt], [WLt], scale=-0.6065306597126334)
            scan_out(p, outs["dec"], WLt, t0, 0)
            tt(p, "pool", ALt[:], ALt[:], prm["a0"][:], ALU.add, [ALt, prm["a0"]], [ALt])
            actf(p, ALt[:], ALt[:], AF.Sigmoid, [ALt], [ALt])
            tt(p, "dve", KK[:], K[:], prm["kk"][:], ALU.mult, [K, prm["kk"]], [KK])
            tt(p, "pool", T1[:], KK[:], KK[:], ALU.mult, [KK], [T1])
            rsum(p, "dve", SS[:, 0:H], v3(T1[:]), [T1], [SS])
            p.op("act", lambda e: e.sqrt(out=SS[:, 0:H], in_=SS[:, 0:H]), reads=[SS], writes=[SS])
            ts(p, "dve", SS[:, 0:H], SS[:, 0:H], 1e-12, ALU.max, [SS], [SS])
            p.op("dve", lambda e: e.reciprocal(out=SS[:, 0:H], in_=SS[:, 0:H]), reads=[SS], writes=[SS])
            tt(p, "dve", v3(KK[:]), v3(KK[:]), SS[:, 0:H].unsqueeze(2).to_broadcast([128, H, N]), ALU.mult, [KK, SS], [KK])
            p.op("dve", lambda e: e.scalar_tensor_tensor(out=T1[:], in0=ALt[:], scalar=-1.0, in1=prm["ka"][:],
                                                         op0=ALU.add, op1=ALU.mult), reads=[ALt, prm["ka"]], writes=[T1])
            ts(p, "pool", T1[:], T1[:], 1.0, ALU.add, [T1], [T1])
            tt(p, "dve", T1[:], T1[:], K[:], ALU.mult, [T1, K], [T1])
            scan_out(p, outs["km"], T1, t0, 0)
            tt(p, "pool", T2[:], KK[:], ALt[:], ALU.mult, [KK, ALt], [T2])
            scan_out(p, outs["b"], T2, t0, 0)
            ts(p, "dve", KK[:], KK[:], -1.0, ALU.mult, [KK], [KK])
            scan_out(p, outs["a"], KK, t0, 0)
            tt(p, "dve", R[:], R[:], T1[:], ALU.mult, [R, T1], [R])
            tt(p, "pool", R[:], R[:], prm["rk"][:], ALU.mult, [R, prm["rk"]], [R])
            rsum(p, "dve", SS[:, H:2 * H], v3(R[:]), [R], [SS])
            tt(p, "dve", v3(V[:]), v3(V[:]), SS[:, H:2 * H].unsqueeze(2).to_broadcast([128, H, N]), ALU.mult, [V, SS], [V])
            p.dma("pool", outs["bon"][t0:t0 + 128, :], V[:], reads=[V], writes=[outs["bon"]])


def st_rope(p, x_b, x_ap, pos_d, invf_d, o_b, T, G):
    with p.stage():
        invf = load_bcast(p, "invf", invf_d, 8)
        NB = 2
        X = [p.sb("x", [128, G * 64]) for _ in range(NB)]
        PI = [p.sb("pi", [128, 1], I32) for _ in range(NB)]
        PF = [p.sb("pf", [128, 1]) for _ in range(NB)]
        CS = [p.sb("cs", [128, 16]) for _ in range(NB)]
        kf = p.sb("kf", [128, 16])
        ki = p.sb("ki", [128, 16], I32)
        TA = [p.sb("ta", [128, G, 8]) for _ in range(NB)]
        TB = [p.sb("tb", [128, G, 8]) for _ in range(NB)]
        TC = [p.sb("tc", [128, G, 8]) for _ in range(NB)]
        TD = [p.sb("td", [128, G, 8]) for _ in range(NB)]
        PI_ = 3.141592653589793
        for i, t0 in enumerate(range(0, T, 128)):
            x_, pi_, pf_, cs, ta, tb, tc, td = (z[i % NB] for z in (X, PI, PF, CS, TA, TB, TC, TD))
            p.dma("sp", x_[:], x_ap[t0:t0 + 128, :], reads=[x_b], writes=[x_])
            p.dma("act", pi_[:], pos_d[t0:t0 + 128, :], writes=[pi_])
            p.op("dve", lambda e: e.tensor_copy(out=pf_[:], in_=pi_[:]), reads=[pi_], writes=[pf_])
            ts(p, "dve", cs[:, 0:8], invf[:], pf_[:, 0:1], ALU.mult, [invf, pf_], [cs])
            ts(p, "dve", cs[:, 8:16], cs[:, 0:8], 0.5 * PI_, ALU.add, [cs], [cs])
            ts(p, "dve", kf[:], cs[:], 1.0 / (2 * PI_), ALU.mult, [cs], [kf])
            p.op("dve", lambda e: e.tensor_copy(out=ki[:], in_=kf[:]), reads=[kf], writes=[ki])
            p.op("dve", lambda e: e.tensor_copy(out=kf[:], in_=ki[:]), reads=[ki], writes=[kf])
            p.op("dve", lambda e: e.scalar_tensor_tensor(out=cs[:], in0=kf[:], scalar=-6.28125, in1=cs[:],
                                                         op0=ALU.mult, op1=ALU.add), reads=[kf, cs], writes=[cs])
            p.op("dve", lambda e: e.scalar_tensor_tensor(out=cs[:], in0=kf[:], scalar=-0.0019353071795864769, in1=cs[:],
                                                         op0=ALU.mult, op1=ALU.add), reads=[kf, cs], writes=[cs])
            ts(p, "dve", kf[:], cs[:], PI_, ALU.is_gt, [cs], [kf])
            p.op("dve", lambda e: e.scalar_tensor_tensor(out=cs[:], in0=kf[:], scalar=-2 * PI_, in1=cs[:],
                                                         op0=ALU.mult, op1=ALU.add), reads=[kf, cs], writes=[cs])
            ts(p, "dve", cs[:], cs[:], PI_, ALU.min, [cs], [cs], s2=-PI_, op1=ALU.max)
            actf(p, cs[:], cs[:], AF.Sin, [cs], [cs])
            xv = x_[:].rearrange("p (g d) -> p g d", d=64)
            t1, t2 = xv[:, :, 0:8], xv[:, :, 8:16]
            sinb = cs[:, 0:8].unsqueeze(1).to_broadcast([128, G, 8])
            cosb = cs[:, 8:16].unsqueeze(1).to_broadcast([128, G, 8])
            tt(p, "dve", ta[:], t1, cosb, ALU.mult, [x_, cs], [ta])
            tt(p, "pool", tb[:], t2, sinb, ALU.mult, [x_, cs], [tb])
            tt(p, "dve", tc[:], t2, cosb, ALU.mult, [x_, cs], [tc])
            tt(p, "pool", td[:], t1, sinb, ALU.mult, [x_, cs], [td])
            tt(p, "dve", t1, ta[:], tb[:], ALU.subtract, [ta, tb], [x_])
            tt(p, "dve", t2, tc[:], td[:], ALU.add, [tc, td], [x_])
            p.dma("pool", o_b[t0:t0 + 128, :], x_[:], reads=[x_], writes=[o_b])


def st_scan(p, X2, VT, YT, T, Tt=8):
    if True:
        NB = 2
        AR = [p.sb("ar", [128, Tt, 2, 256]) for _ in range(NB)]
        Wt = [p.sb("w", [128, Tt * 256]) for _ in range(NB)]
        Bt = [p.sb("b", [128, Tt * 256]) for _ in range(NB)]
        Kt = [p.sb("k", [128, Tt * 256]) for _ in range(NB)]
        Vt = [p.sb("v", [128, 4, Tt]) for _ in range(NB)]
        KV = [p.sb("kv", [128, Tt * 256]) for _ in range(NB)]
        RS = [p.sb("rs", [128, 2, 4, Tt]) for _ in range(NB)]
        S = [p.sb("s", [128, 256]) for _ in range(2)]
        S2 = [p.sb("s2", [128, 256]) for _ in range(2)]
        TMP = [p.sb("tmp", [128, 2, 256]) for _ in range(2)]
        TMP2 = p.sb("tmp2", [128, 256])
        p.op("dve", lambda e: e.memset(S[0][:], 0.0), writes=[S[0]])
        v3 = lambda ap: ap.rearrange("p (g j) -> p g j", j=64)
        step = 0
        qn = 0
        for it, t0 in enumerate(range(0, T, Tt)):
            sl = it % NB
            ar, W, B, K, rs = AR[sl], Wt[sl], Bt[sl], Kt[sl], RS[sl]
            for half in range(2):
                ps_ = slice(half * 64, (half + 1) * 64)
                for (dst, src, buf) in ((ar[ps_, :, 0, :], X2["a"].t[half, t0:t0 + Tt, :], ar),
                                        (ar[ps_, :, 1, :], X2["r"].t[half, t0:t0 + Tt, :], ar),
                                        (W[ps_, :].rearrange("p (t c) -> p t c", c=256), X2["dec"].t[half, t0:t0 + Tt, :], W),
                                        (B[ps_, :].rearrange("p (t c) -> p t c", c=256), X2["b"].t[half, t0:t0 + Tt, :], B),
                                        (K[ps_, :].rearrange("p (t c) -> p t c", c=256), X2["km"].t[half, t0:t0 + Tt, :], K)):
                    qn += 1
                    srcb = X2["a"] if buf is ar and dst is not None else None
                    p.dma("sp", dst, src.partition_broadcast(64),
                          reads=[X2["a"], X2["r"], X2["dec"], X2["b"], X2["km"]], writes=[buf])
            for g in range(4):
                p.dma("sp", Vt[sl][:, g, :], VT[g * 128:(g + 1) * 128, t0:t0 + Tt], reads=[VT], writes=[Vt[sl]])
            tt(p, "pool", KV[sl][:].rearrange("p (t g j) -> p t g j", g=4, j=64), K[:].rearrange("p (t g j) -> p t g j", g=4, j=64),
               Vt[sl][:].rearrange("p g t -> p t g").unsqueeze(3).to_broadcast([128, Tt, 4, 64]), ALU.mult, [K, Vt[sl]], [KV[sl]])
            for q in range(Tt):
                cs = slice(q * 256, (q + 1) * 256)
                Sc, Sn = S[step % 2], S[(step + 1) % 2]
                s2, tmp = S2[step % 2], TMP[step % 2]
                tt(p, "pool", s2[:], Sc[:], W[:, cs], ALU.mult, [Sc, W], [s2])
                tt(p, "pool", s2[:], s2[:], KV[sl][:, cs], ALU.add, [s2, KV[sl]], [s2])
                tt(p, "dve", tmp[:], Sc[:].unsqueeze(1).to_broadcast([128, 2, 256]), ar[:, q, :, :], ALU.mult, [Sc, ar], [tmp])
                rsum(p, "dve", rs[:, :, :, q].rearrange("p a g -> p (a g)"), tmp[:].rearrange("p a (g j) -> p (a g) j", j=64), [tmp], [rs])
                tt(p, "dve", v3(TMP2[:]), v3(B[:, cs]), rs[:, 0, :, q].unsqueeze(2).to_broadcast([128, 4, 64]), ALU.mult, [B, rs], [TMP2])
                tt(p, "dve", Sn[:], s2[:], TMP2[:], ALU.add, [s2, TMP2], [Sn])
                step += 1
            qa = 1 if t0 == 0 else 0
            for g in range(4):
                p.dma("pool", YT[g * 128:(g + 1) * 128, t0 - 1 + qa:t0 + Tt - 1], rs[:, 1, g, qa:Tt], reads=[rs], writes=[YT])
        Sc = S[step % 2]
        arL = AR[0]
        for half in range(2):
            p.dma("sp", arL[half * 64:(half + 1) * 64, 0, 0, :], X2["r"].t[half, T, :].partition_broadcast(64), reads=[X2["r"]], writes=[arL])
        tmp = TMP[0]
        tt(p, "dve", tmp[:, 0, :], Sc[:], arL[:, 0, 0, :], ALU.mult, [Sc, arL], [tmp])
        rsum(p, "dve", RS[0][:, 0, :, 0], v3(tmp[:, 0, :]), [tmp], [RS[0]])
        for g in range(4):
            p.dma("sp", YT[g * 128:(g + 1) * 128, T - 1:T], RS[0][:, 0, g, 0:1], reads=[RS[0]], writes=[YT], allow_slow_non_contiguous=True)


def st_attn(p, q_b, qT, k_b, kT, v_b, v_ap, o_b, oT, mask_d, NH, NC, DK, DV, Tq, Tk, causal, scale):
    with p.stage():
        NKT = Tk // 128
        QB = min(512, Tq)
        if causal:
            M = p.sb("mask", [128, 4, 512])
            for j in range(4):
                p.dma("sp", M[:, j, :], mask_d[j], writes=[M])
        ones = p.sb("ones", [128, 128])
        p.op("dve", lambda e: e.memset(ones[:], 1.0), writes=[ones])
        QT = [p.sb("qT", [DK, Tq]) for _ in range(2)]
        KT = [p.sb("kT", [DK, Tk]) for _ in range(2)]
        V = [p.sb("v", [128, NKT, DV]) for _ in range(2)]
        PS_S = [p.ps("pss", [128, 512]) for _ in range(2)]
        PS_O = [p.ps("pso", [128, 512]) for _ in range(2)]
        PS_Z = [p.ps("psz", [128, 512]) for _ in range(2)]
        P = [p.sb("p", [128, 512]) for _ in range(3)]
        O = [p.sb("o", [128, 512]) for _ in range(2)]
        RZ = [p.sb("rz", [128, 512]) for _ in range(2)]
        ihc = iq = ik = 0
        for h in range(NH):
            Vh = V[h % 2]
            p.dma("sp", Vh[:], v_ap(h).rearrange("(kt p) d -> p kt d", p=128), reads=[v_b], writes=[Vh])
            for c in range(NC):
                hc = h * NC + c
                Q, K = QT[ihc % 2], KT[ihc % 2]
                ihc += 1
                p.dma("act", Q[:], qT(hc), reads=[q_b], writes=[Q])
                p.dma("sp", K[:], kT(hc), reads=[k_b], writes=[K])
                for qb in range(Tq // QB):
                    q0 = qb * QB
                    nkt = min(NKT, (q0 + QB) // 128) if causal else NKT
                    pso, psz = PS_O[iq % 2], PS_Z[iq % 2]
                    ob, rz = O[iq % 2], RZ[iq % 2]
                    iq += 1

                    def oz(kt, Pt):
                        p.op("pe", lambda e: e.matmul(pso[0:DV, 0:QB], Vh[:, kt, :], Pt[:, 0:QB], start=(kt == 0), stop=(kt == nkt - 1)),
                             reads=[Vh, Pt], writes=[pso])
                        p.op("pe", lambda e: e.matmul(psz[:, 0:QB], ones[:], Pt[:, 0:QB], start=(kt == 0), stop=(kt == nkt - 1)),
                             reads=[ones, Pt], writes=[psz])
                    prev = None
                    for kt in range(nkt):
                        pss = PS_S[ik % 2]
                        Pt = P[ik % 3]
                        ik += 1
                        p.op("pe", lambda e: e.matmul(pss[:, 0:QB], K[:, kt * 128:(kt + 1) * 128], Q[:, q0:q0 + QB], start=True, stop=True),
                             reads=[K, Q], writes=[pss])
                        actf(p, Pt[:, 0:QB], pss[:, 0:QB], AF.Exp, [pss], [Pt], scale=scale)
                        if causal and kt * 128 + 127 > q0:
                            j = kt - q0 // 128
                            tt(p, "pool", Pt[:, 0:QB], Pt[:, 0:QB], M[:, j, 0:QB], ALU.mult, [Pt, M], [Pt])
                        if prev is not None:
                            oz(*prev)
                        prev = (kt, Pt)
                    oz(*prev)
                    p.op("dve", lambda e: e.reciprocal(out=rz[0:DV, 0:QB], in_=psz[0:DV, 0:QB]), reads=[psz], writes=[rz])
                    tt(p, "dve", ob[0:DV, 0:QB], pso[0:DV, 0:QB], rz[0:DV, 0:QB], ALU.mult, [pso, rz], [ob])
                    p.dma("pool", oT(hc)[:, q0:q0 + QB], ob[0:DV, 0:QB], reads=[ob], writes=[o_b])


def st_ew3(p, Y, BON, G, O01, prm_d, lamc_d, mix, T, C, gn_eps, ln_eps, Z01=None):
    HR = C // 64
    HD = C // 128
    with p.stage():
        lnxw = load_bcast(p, "lnxw", prm_d["lnxw"], C)
        lnxb = load_bcast(p, "lnxb", prm_d["lnxb"], C)
        sw = load_bcast(p, "sw", prm_d["sw"], C)
        lv = {k: load_bcast(p, k, prm_d[k], 64) for k in ("lq1", "lk1", "lq2", "lk2")}
        lc = load_bcast(p, "lamc", lamc_d, 2)
        L = p.sb("lam", [128, 8])
        tt(p, "dve", lv["lq1"][:], lv["lq1"][:], lv["lk1"][:], ALU.mult, [lv["lq1"], lv["lk1"]], [lv["lq1"]])
        tt(p, "dve", lv["lq2"][:], lv["lq2"][:], lv["lk2"][:], ALU.mult, [lv["lq2"], lv["lk2"]], [lv["lq2"]])
        rsum(p, "dve", L[:, 0:1], lv["lq1"][:], [lv["lq1"]], [L])
        rsum(p, "dve", L[:, 1:2], lv["lq2"][:], [lv["lq2"]], [L])
        actf(p, L[:, 0:2], L[:, 0:2], AF.Exp, [L], [L])
        tt(p, "dve", L[:, 2:3], L[:, 1:2], L[:, 0:1], ALU.subtract, [L], [L])
        tt(p, "dve", L[:, 3:4], L[:, 2:3], lc[:, 0:1], ALU.subtract, [L, lc], [L])
        NB = 2
        mk = lambda nm: [p.sb(nm, [128, C]) for _ in range(NB)]
        Ys, Bs, Gs, O0s, O1s, TM = mk("y"), mk("bon"), mk("g"), mk("o0"), mk("o1"), mk("tm")
        ST = [p.sb("st", [128, 64]) for _ in range(NB)]
        v64 = lambda ap: ap.rearrange("p (h j) -> p h j", j=64)
        v128 = lambda ap: ap.rearrange("p (h j) -> p h j", j=128)
        for i, t0 in enumerate(range(0, T, 128)):
            Yt, BONt, Gt, O0, O1, X, S = (z[i % NB] for z in (Ys, Bs, Gs, O0s, O1s, TM, ST))
            p.dma("sp", Yt[:], Y[t0:t0 + 128, :], reads=[Y], writes=[Yt])
            p.dma("act", BONt[:], BON[t0:t0 + 128, :], reads=[BON], writes=[BONt])
            p.dma("sp", Gt[:], G[t0:t0 + 128, :], reads=[G], writes=[Gt])
            o4 = O01[t0:t0 + 128, :].rearrange("t (h c d) -> t h c d", c=2, d=128)
            p.dma("act", v128(O0[:]), o4[:, :, 0, :], reads=[O01], writes=[O0])
            p.dma("sp", v128(O1[:]), o4[:, :, 1, :], reads=[O01], writes=[O1])
            if Z01 is not None:
                z4 = Z01[t0:t0 + 128, :].rearrange("t (h c d) -> t h c d", c=2, d=128)
                p.dma("act", v128(X[:]), z4[:, :, 0, :], reads=[Z01], writes=[X])
                p.op("dve", lambda e: e.reciprocal(out=X[:], in_=X[:]), reads=[X], writes=[X])
                tt(p, "dve", O0[:], O0[:], X[:], ALU.mult, [O0, X], [O0])
                p.dma("act", v128(X[:]), z4[:, :, 1, :], reads=[Z01], writes=[X])
                p.op("dve", lambda e: e.reciprocal(out=X[:], in_=X[:]), reads=[X], writes=[X])
                tt(p, "dve", O1[:], O1[:], X[:], ALU.mult, [O1, X], [O1])
            rsum(p, "dve", S[:, 0:HR], v64(Yt[:]), [Yt], [S])
            ts(p, "dve", S[:, 0:HR], S[:, 0:HR], 1.0 / 64, ALU.mult, [S], [S])
            tt(p, "dve", v64(Yt[:]), v64(Yt[:]), S[:, 0:HR].unsqueeze(2).to_broadcast([128, HR, 64]), ALU.subtract, [Yt, S], [Yt])
            tt(p, "pool", X[:], Yt[:], Yt[:], ALU.mult, [Yt], [X])
            rsum(p, "dve", S[:, 16:16 + HR], v64(X[:]), [X], [S])
            rsqrt_cols(p, S, 16, 16, HR, 1.0 / 64, gn_eps)
            tt(p, "dve", v64(Yt[:]), v64(Yt[:]), S[:, 16:16 + HR].unsqueeze(2).to_broadcast([128, HR, 64]), ALU.mult, [Yt, S], [Yt])
            tt(p, "pool", Yt[:], Yt[:], lnxw[:], ALU.mult, [Yt, lnxw], [Yt])
            tt(p, "pool", Yt[:], Yt[:], lnxb[:], ALU.add, [Yt, lnxb], [Yt])
            tt(p, "dve", Yt[:], Yt[:], BONt[:], ALU.add, [Yt, BONt], [Yt])
            tt(p, "pool", Yt[:], Yt[:], Gt[:], ALU.mult, [Yt, Gt], [Yt])
            p.dma("pool", mix[t0:t0 + 128, 0:C], Yt[:], reads=[Yt], writes=[mix])
            p.op("dve", lambda e: e.scalar_tensor_tensor(out=O0[:], in0=O1[:], scalar=L[:, 3:4], in1=O0[:],
                                                         op0=ALU.mult, op1=ALU.add), reads=[O0, O1, L], writes=[O0])
            tt(p, "pool", O1[:], O0[:], O0[:], ALU.mult, [O0], [O1])
            rsum(p, "dve", S[:, 32:32 + HD], v128(O1[:]), [O1], [S])
            rsqrt_cols(p, S, 32, 32, HD, 1.0 / 128, ln_eps)
            tt(p, "dve", v128(O0[:]), v128(O0[:]), S[:, 32:32 + HD].unsqueeze(2).to_broadcast([128, HD, 128]), ALU.mult, [O0, S], [O0])
            p.op("dve", lambda e: e.scalar_tensor_tensor(out=O0[:], in0=O0[:], scalar=lc[:, 1:2], in1=sw[:],
                                                         op0=ALU.mult, op1=ALU.mult), reads=[O0, sw, lc], writes=[O0])
            p.dma("pool", mix[t0:t0 + 128, C:2 * C], O0[:], reads=[O0], writes=[mix])


def st_topk(p, S1, S2, iota_d, EXF, GT, T, H=8, NK=128, K=16):
    with p.stage():
        IOTA = load_bcast(p, "iota", iota_d, K * K)
        NB = 2
        S = [[p.sb("s", [128, H * NK]) for _ in range(2)] for _ in range(NB)]
        MV = [p.sb("mv", [128, K]) for _ in range(2)]
        MI = [p.sb("mi", [128, K], U32) for _ in range(2)]
        MF = [p.sb("mf", [128, K]) for _ in range(2)]
        WK = p.sb("wk", [128, NK])
        CAND = p.sb("cand", [128, K * K])
        CE = p.sb("ce", [128, K * K])
        WK2 = p.sb("wk2", [128, K * K])
        FV = p.sb("fv", [128, K])
        FI = p.sb("fi", [128, K], U32)
        FF = p.sb("ff", [128, K])
        OH = p.sb("oh", [128, K, K])
        T2 = p.sb("t2", [128, K, K])
        IX = p.sb("ix", [128, 4 * K])
        LO16 = p.sb("lo16", [128, 2 * K])
        ts(p, "dve", LO16[:, 0:K], IOTA[:, 0:K], 16.0, ALU.mult, [IOTA], [LO16])
        ts(p, "dve", LO16[:, K:2 * K], LO16[:, 0:K], 16.0, ALU.add, [LO16], [LO16])
        SM = p.sb("sm", [128, 4])
        E = p.sb("e", [128, K])
        EX = [p.sb("ex", [128, H * K]) for _ in range(NB)]
        GTt = [p.sb("gt", [128, H * K]) for _ in range(NB)]
        NEG = -1e30

        def top16(src, srcbuf, mv, mi, wk):
            p.op("dve", lambda e: e.max(out=mv[:, 0:8], in_=src), reads=[srcbuf], writes=[mv])
            p.op("dve", lambda e: e.max_index(out=mi[:, 0:8], in_max=mv[:, 0:8], in_values=src), reads=[srcbuf, mv], writes=[mi])
            p.op("dve", lambda e: e.match_replace(out=wk[:], in_to_replace=mv[:, 0:8], in_values=src, imm_value=NEG),
                 reads=[srcbuf, mv], writes=[wk])
            p.op("dve", lambda e: e.max(out=mv[:, 8:16], in_=wk[:]), reads=[wk], writes=[mv])
            p.op("dve", lambda e: e.max_index(out=mi[:, 8:16], in_max=mv[:, 8:16], in_values=wk[:]), reads=[wk, mv], writes=[mi])

        for i, t0 in enumerate(range(0, T, 128)):
            S1t, S2t = S[i % NB]
            ex_, gt_ = EX[i % NB], GTt[i % NB]
            p.dma("sp", S1t[:], S1[t0:t0 + 128, :], reads=[S1], writes=[S1t])
            p.dma("act", S2t[:], S2[t0:t0 + 128, :], reads=[S2], writes=[S2t])
            for h in range(H):
                for c, Sx in ((0, S1t), (1, S2t)):
                    top16(Sx[:, h * NK:(h + 1) * NK], Sx, MV[c], MI[c], WK)
                    p.op("dve", lambda e: e.tensor_copy(out=MF[c][:], in_=MI[c][:]), reads=[MI[c]], writes=[MF[c]])
                c3 = lambda ap: ap.rearrange("p (a b) -> p a b", b=K)
                tt(p, "dve", c3(CAND[:]), MV[0][:].unsqueeze(2).to_broadcast([128, K, K]),
                   MV[1][:].unsqueeze(1).to_broadcast([128, K, K]), ALU.add, [MV[0], MV[1]], [CAND])
                ts(p, "dve", MF[0][:], MF[0][:], float(NK), ALU.mult, [MF[0]], [MF[0]])
                top16(CAND[:], CAND, FV, FI, WK2)
                p.op("dve", lambda e: e.tensor_copy(out=FF[:], in_=FI[:]), reads=[FI], writes=[FF])
                colv = lambda t_: t_[:].unsqueeze(2).to_broadcast([128, K, K])
                rowv = lambda ap: ap.unsqueeze(1).to_broadcast([128, K, K])
                tt(p, "dve", OH[:], colv(FF), rowv(LO16[:, 0:K]), ALU.is_ge, [FF, LO16], [OH])
                tt(p, "dve", T2[:], colv(FF), rowv(LO16[:, K:2 * K]), ALU.is_lt, [FF, LO16], [T2])
                tt(p, "dve", OH[:], OH[:], T2[:], ALU.mult, [OH, T2], [OH])
                tt(p, "dve", T2[:], OH[:], rowv(MF[0][:]), ALU.mult, [OH, MF[0]], [T2])
                rsum(p, "dve", IX[:, 0:K], T2[:], [T2], [IX])
                tt(p, "dve", T2[:], OH[:], rowv(LO16[:, 0:K]), ALU.mult, [OH, LO16], [T2])
                rsum(p, "dve", IX[:, K:2 * K], T2[:], [T2], [IX])
                tt(p, "dve", IX[:, 2 * K:3 * K], FF[:], IX[:, K:2 * K], ALU.subtract, [FF, IX], [IX])
                tt(p, "dve", OH[:], IX[:, 2 * K:3 * K].unsqueeze(2).to_broadcast([128, K, K]), rowv(IOTA[:, 0:K]), ALU.is_equal, [IX, IOTA], [OH])
                tt(p, "dve", T2[:], OH[:], rowv(MF[1][:]), ALU.mult, [OH, MF[1]], [T2])
                rsum(p, "dve", IX[:, 3 * K:4 * K], T2[:], [T2], [IX])
                tt(p, "dve", ex_[:, h * K:(h + 1) * K], IX[:, 0:K], IX[:, 3 * K:4 * K], ALU.add, [IX], [ex_])
                ts(p, "dve", SM[:, 0:1], FV[:, 0:1], -1.0, ALU.mult, [FV], [SM])
                p.op("act", lambda e: e.activation(out=E[:], in_=FV[:], func=AF.Exp, bias=SM[:, 0:1], scale=1.0),
                     reads=[FV, SM], writes=[E])
                rsum(p, "dve", SM[:, 1:2], E[:], [E], [SM])
                p.op("dve", lambda e: e.reciprocal(out=SM[:, 2:3], in_=SM[:, 1:2]), reads=[SM], writes=[SM])
                ts(p, "dve", gt_[:, h * K:(h + 1) * K], E[:], SM[:, 2:3], ALU.mult, [E, SM], [gt_])
            p.dma("pool", EXF[t0:t0 + 128, :], ex_[:], reads=[ex_], writes=[EXF])
            p.dma("pool", GT[t0:t0 + 128, :], gt_[:], reads=[gt_], writes=[GT])


def st_peer(p, x_b, x_ap, EXFT, GTT_d, pu, pv, c2_d, y_b, T, D=2048, NS=128):
    with p.stage():
        C2 = load_bcast(p, "c2", c2_d, 255)
        EXF = p.sb("exf", [NS, T])
        EXT = p.sb("ext", [NS, T], I32)
        GTT = p.sb("gtt", [NS, T])
        p.dma("sp", EXF[:], EXFT[:, :], reads=[EXFT], writes=[EXF])
        p.dma("act", GTT[:], GTT_d[:, :], reads=[GTT_d], writes=[GTT])
        p.op("dve", lambda e: e.tensor_copy(out=EXT[:], in_=EXF[:]), reads=[EXF], writes=[EXT])
        NB = 3
        UG = [p.sb("ug", [NS, D]) for _ in range(NB)]
        VG = [p.sb("vg", [NS, D]) for _ in range(NB)]
        XB = [p.sb("xb", [NS, D]) for _ in range(NB)]
        JK = [p.sb("junk", [NS, D]) for _ in range(2)]
        HT = [p.sb("ht", [NS, 4]) for _ in range(NB)]
        ACM = [p.sb("acm", [NS, 128]) for _ in range(NB)]
        PSY = [p.ps("psy", [128, 512]) for _ in range(4)]
        YO = [p.sb("yo", [128, 512]) for _ in range(4)]
        for t in range(T):
            tl = t % 128
            ug, vg, xb, ht, acm = (z[t % NB] for z in (UG, VG, XB, HT, ACM))
            p.dma_custom("pool", lambda e: e.indirect_dma_start(
                out=ug[:, :], out_offset=None, in_=pu[:, :],
                in_offset=bass.IndirectOffsetOnAxis(ap=EXT[:, t:t + 1], axis=0)), reads=[EXT, pu], writes=[ug])
            p.dma_custom("pool", lambda e: e.indirect_dma_start(
                out=vg[:, :], out_offset=None, in_=pv[:, :],
                in_offset=bass.IndirectOffsetOnAxis(ap=EXT[:, t:t + 1], axis=0)), reads=[EXT, pv], writes=[vg])
            p.dma("sp" if t % 2 else "act", xb[:], x_ap[t].partition_broadcast(NS), reads=[x_b], writes=[xb])
            jk = JK[t % 2]
            tt(p, "pool" if t % 2 else "dve", jk[:], ug[:], xb[:], ALU.mult, [ug, xb], [jk])
            rsum(p, "dve", ht[:, 0:1], jk[:], [jk], [ht])
            actf(p, ht[:, 1:2], ht[:, 0:1], AF.Gelu, [ht], [ht])
            tt(p, "dve", ht[:, 2:3], ht[:, 1:2], GTT[:, t:t + 1], ALU.mult, [ht, GTT], [ht])
            ts(p, "dve", acm[:], C2[:, 127 - tl:255 - tl], ht[:, 2:3], ALU.mult, [C2, ht], [acm])
            for n in range(4):
                p.op("pe", lambda e: e.matmul(PSY[n][:, :], acm[:], vg[:, n * 512:(n + 1) * 512],
                                              start=(tl == 0), stop=(tl == 127 or t == T - 1)),
                     reads=[acm, vg], writes=[PSY[n]])
            if tl == 127 or t == T - 1:
                t0 = t - tl
                nr = tl + 1
                for n in range(4):
                    if n % 2:
                        p.op("act", lambda e: e.activation(out=YO[n][:], in_=PSY[n][:], func=AF.Copy), reads=[PSY[n]], writes=[YO[n]])
                    else:
                        p.op("dve", lambda e: e.tensor_copy(out=YO[n][:], in_=PSY[n][:]), reads=[PSY[n]], writes=[YO[n]])
                    p.dma("sp", y_b[t0:t0 + nr, n * 512:(n + 1) * 512], YO[n][0:nr, :], reads=[YO[n]], writes=[y_b])


def make_ident(p):
    idd = p.din("ident", [128, 128])
    ident = p.sb("ident", [128, 128])
    p.dma("sp", ident[:], idd[:, :], writes=[ident])
    return ident


def phase_A(p, T, xT, mixout, pfx=""):
    D = 2048
    C = 512
    NR = 3 * C + 288
    NCOL = NR + 3 * C
    I = lambda n, s, dt=F32: p.din(pfx + n, s, dt)
    w_c = I("w_c", [D, NCOL])
    mu = I("mu", [1, NR])
    wup, aup, gup = I("wup", [64, C]), I("aup", [64, C]), I("gup", [160, C])
    prm2 = {k: I(k, [1, C]) for k in ("w0", "a0", "kk", "ka", "rk")}
    prm3 = {k: I(k, [1, C]) for k in ("lnxw", "lnxb", "sw")}
    prm3.update({k: I(k, [1, 64]) for k in ("lq1", "lk1", "lq2", "lk2")})
    lamc = I("lamc", [1, 2])
    return dict(w_c=w_c, mu=mu, wup=wup, aup=aup, gup=gup, prm2=prm2, prm3=prm3, lamc=lamc, NR=NR, NCOL=NCOL, C=C)


def run_phase_A(p, T, xT, mixout, W, cst, tag):
    D, C, NR, NCOL = 2048, W["C"], W["NR"], W["NCOL"]
    Dt = lambda n, s: p.dtmp(tag + n, s)
    P = Dt("P", [T, NCOL])
    st_gemm(p, xT, xT.t, W["w_c"], W["w_c"].t, P, P.t, D, T, NCOL)
    O1 = Dt("O1", [T, NR])
    st_ew1(p, P, P.t, W["mu"], O1, O1.t, T, NR, 3 * C)
    LT = Dt("LT", [288, T])
    st_transpose(p, O1, O1.t[:, 3 * C:NR], LT, LT.t, T, 288, cst["ident"])
    WL, AL, GG = Dt("WL", [T, C]), Dt("AL", [T, C]), Dt("GG", [T, C])
    st_gemm(p, LT, LT.t[0:64, :], W["wup"], W["wup"].t, WL, WL.t, 64, T, C)
    st_gemm(p, LT, LT.t[64:128, :], W["aup"], W["aup"].t, AL, AL.t, 64, T, C)
    st_gemm(p, LT, LT.t[128:288, :], W["gup"], W["gup"].t, GG, GG.t, 160, T, C)
    e2 = {k: Dt(k, [2, T, C // 2]) for k in ("dec", "km", "a", "b")}
    e2["r"] = Dt("r", [2, T + 1, C // 2])
    e2["bon"] = Dt("bon", [T, C])
    st_ew2(p, O1, O1.t, WL, AL, W["prm2"], e2, T, C)
    QKR = Dt("QKR", [T, 2 * C])
    st_rope(p, P, P.t[:, NR:NR + 2 * C], cst["pos"], cst["invf"], QKR, T, 16)
    VT = Dt("VT", [C, T])
    st_transpose(p, O1, O1.t[:, 2 * C:3 * C], VT, VT.t, T, C, cst["ident"])
    YT = Dt("YT", [C, T])
    QKT = Dt("QKT", [2 * C, T])
    st_transpose(p, QKR, QKR.t, QKT, QKT.t, T, 2 * C, cst["ident"])
    OT = Dt("OT", [2 * C, T])
    ZT = Dt("ZT", [2 * C, T])
    with p.stage():
        p.dpool = (24, 48)
        attn_body_pe_act(p, QKT, lambda hc: QKT.t[hc * 64:(hc + 1) * 64, :], QKT, lambda hc: QKT.t[C + hc * 64:C + (hc + 1) * 64, :],
                         P, lambda h: P.t[:, NR + 2 * C + h * 128:NR + 2 * C + (h + 1) * 128],
                         OT, lambda hc: OT.t[hc * 128:(hc + 1) * 128, :], ZT, lambda hc: ZT.t[hc * 128:(hc + 1) * 128, :],
                         cst["maskb"], cst["ones"], cst["ident"], 4, 2, 64, 128, T, 64 ** -0.5)
        p.dpool = (0, 24)
        st_scan(p, e2, VT, YT, T)
        p.dpool = None
    Y = Dt("Y", [T, C])
    st_transpose(p, YT, YT.t, Y, Y.t, C, T, cst["ident"])
    O01 = Dt("O01", [T, 2 * C])
    st_transpose(p, OT, OT.t, O01, O01.t, 2 * C, T, cst["ident"])
    Z01 = Dt("Z01", [T, 2 * C])
    st_transpose(p, ZT, ZT.t, Z01, Z01.t, 2 * C, T, cst["ident"])
    st_ew3(p, Y, e2["bon"], GG, O01, W["prm3"], W["lamc"], mixout, T, C, 64e-5, 1e-5, Z01=Z01)


def build_A(T):
    p = Prog()
    xT = p.din("xT", [2048, T])
    mix = p.dout("mix", [T, 1024])
    cst = dict(pos=p.din("pos", [T, 1], I32), invf=p.din("invf", [1, 8]), maskb=p.din("maskb", [4, 128, 512]),
               ones=p.din("ones", [128, 128]), ident=make_ident(p))
    W = phase_A(p, T, xT, mix)
    run_phase_A(p, T, xT, mix, W, cst, "a_")
    return p.finish()


def phase_B_inputs(p, pfx=""):
    D = 2048
    I = lambda n, s, dt=F32: p.din(pfx + n, s, dt)
    W = dict(w_out=I("w_out", [D, D]), xq=I("xq", [D, 512]), xk=I("xk", [D, 512]), xv=I("xv", [D, 512]), xo=I("xo", [512, D]),
             pq=I("pq", [D, D]), skT=I("skT", [2, 128, 128]), pu=I("pu", [16384, D]), pv=I("pv", [16384, D]))
    for k in ("ln1w", "ln1b", "ln2w", "ln2b", "ln3w", "ln3b"):
        W[k] = I(k, [1, D])
    return W


def run_phase_B(p, Tc, x_b, mixT, memT, out_b, W, cst, alpha, tag):
    D = 2048
    Dt = lambda n, s: p.dtmp(tag + n, s)
    MO = Dt("MO", [Tc, D])
    st_gemm(p, mixT, mixT.t, W["w_out"], W["w_out"].t, MO, MO.t, D, Tc, D)
    X1 = Dt("X1", [Tc, D])
    st_ln(p, x_b, x_b.t, MO, MO.t, W["ln1w"], W["ln1b"], X1, X1.t, Tc, D, alpha, 1e-5)
    X1T = Dt("X1T", [D, Tc])
    st_transpose(p, X1, X1.t, X1T, X1T.t, Tc, D, cst["ident"])
    QXT, KXT, VX = Dt("QXT", [512, Tc]), Dt("KXT", [512, 256]), Dt("VX", [256, 512])
    st_gemm(p, W["xq"], W["xq"].t, X1T, X1T.t, QXT, QXT.t, D, 512, Tc)
    st_gemm(p, W["xk"], W["xk"].t, memT, memT.t, KXT, KXT.t, D, 512, 256)
    st_gemm(p, memT, memT.t, W["xv"], W["xv"].t, VX, VX.t, D, 256, 512)
    OXT = Dt("OXT", [512, Tc])
    st_attn(p, QXT, lambda h: QXT.t[h * 128:(h + 1) * 128, :], KXT, lambda h: KXT.t[h * 128:(h + 1) * 128, :],
            VX, lambda h: VX.t[:, h * 128:(h + 1) * 128], OXT, lambda h: OXT.t[h * 128:(h + 1) * 128, :],
            None, 4, 1, 128, 128, Tc, 256, False, 128 ** -0.5)
    XA = Dt("XA", [Tc, D])
    st_gemm(p, OXT, OXT.t, W["xo"], W["xo"].t, XA, XA.t, 512, Tc, D)
    X2 = Dt("X2", [Tc, D])
    st_ln(p, X1, X1.t, XA, XA.t, W["ln2w"], W["ln2b"], X2, X2.t, Tc, D, alpha, 1e-5)
    X2T = Dt("X2T", [D, Tc])
    st_transpose(p, X2, X2.t, X2T, X2T.t, Tc, D, cst["ident"])
    QPT = Dt("QPT", [D, Tc])
    st_gemm(p, W["pq"], W["pq"].t, X2T, X2T.t, QPT, QPT.t, D, D, Tc)
    S1, S2 = Dt("S1", [Tc, 1024]), Dt("S2", [Tc, 1024])
    st_scores(p, QPT, W["skT"], S1, S2, Tc)
    EXF, GT = Dt("EXF", [Tc, 128]), Dt("GT", [Tc, 128])
    st_topk(p, S1, S2, cst["iota"], EXF, GT, Tc)
    EXFT, GTT = Dt("EXFT", [128, Tc]), Dt("GTT", [128, Tc])
    st_transpose(p, EXF, EXF.t, EXFT, EXFT.t, Tc, 128, cst["ident"])
    st_transpose(p, GT, GT.t, GTT, GTT.t, Tc, 128, cst["ident"])
    YP = Dt("YP", [Tc, D])
    st_peer2(p, X2, X2.t, EXFT, GTT, W["pu"], W["pv"], cst["c2"], cst["ident"], YP, Tc)
    st_ln(p, X2, X2.t, YP, YP.t, W["ln3w"], W["ln3b"], out_b, out_b.t, Tc, D, alpha, 1e-5)


def build_B(Tc, alpha):
    p = Prog()
    x = p.din("x", [Tc, 2048])
    mixT = p.din("mixT", [2048, Tc])
    memT = p.din("memT", [2048, 256])
    out = p.dout("out", [Tc, 2048])
    cst = dict(iota=p.din("iota", [1, 256]), c2=p.din("c2", [1, 255]), ident=make_ident(p))
    W = phase_B_inputs(p)
    run_phase_B(p, Tc, x, mixT, memT, out, W, cst, alpha, "b_")
    return p.finish()


PAIRS = [[0, 1], [2, 3], [4, 5], [6, 7]]


def st_allgather(p, src, dst, rows, CH):
    for i in range(rows // CH):
        p.op("pool", lambda e: e.collective_compute("AllGather", ALU.bypass, replica_groups=PAIRS,
                                                    ins=[src.t[i * CH:(i + 1) * CH, :].opt()],
                                                    outs=[dst.t[i * 2 * CH:(i + 1) * 2 * CH, :].opt()]), reads=[src], writes=[dst])


def ag_rows(dst, CH, r, t0, n):
    i, j = t0 // CH, t0 % CH
    base = i * 2 * CH + r * CH + j
    return dst.t[base:base + n, :]


def st_sel_transpose(p, G, flags_d, MIXT, T, Tc, ident, CH):
    with p.stage():
        F = load_bcast(p, "flags", flags_d, 2)
        LO = [p.sb("lo", [128, 512]) for _ in range(3)]
        HI = [p.sb("hi", [128, 512]) for _ in range(3)]
        PS = [p.ps("ps", [128, 512]) for _ in range(3)]
        O = [p.sb("o", [128, 4, 128]) for _ in range(3)]
        i = 0
        for r in range(2):
            for t0 in range(0, Tc, 128):
                for n0 in range(0, 1024, 512):
                    lo, hi, ps, o = LO[i % 3], HI[i % 3], PS[i % 3], O[i % 3]
                    i += 1
                    p.dma("sp", lo[:], ag_rows(G, CH, r, t0, 128)[:, n0:n0 + 512], reads=[G], writes=[lo])
                    p.dma("act", hi[:], ag_rows(G, CH, r, Tc + t0, 128)[:, n0:n0 + 512], reads=[G], writes=[hi])
                    ts(p, "dve", lo[:], lo[:], F[:, 0:1], ALU.mult, [lo, F], [lo])
                    p.op("dve", lambda e: e.scalar_tensor_tensor(out=lo[:], in0=hi[:], scalar=F[:, 1:2], in1=lo[:],
                                                                 op0=ALU.mult, op1=ALU.add), reads=[lo, hi, F], writes=[lo])
                    for j in range(4):
                        p.op("pe", lambda e, j=j: e.transpose(out=ps[:, j * 128:(j + 1) * 128], in_=lo[:, j * 128:(j + 1) * 128],
                                                              identity=ident[:, :]), reads=[lo, ident], writes=[ps])
                    ov = o[:].rearrange("p j t -> p (j t)")
                    if i % 2:
                        p.op("act", lambda e: e.activation(out=ov, in_=ps[:, :], func=AF.Copy), reads=[ps], writes=[o])
                    else:
                        p.op("dve", lambda e: e.tensor_copy(out=ov, in_=ps[:, :]), reads=[ps], writes=[o])
                    for j in range(4):
                        r0 = r * 1024 + n0 + j * 128
                        p.dma("pool", MIXT[r0:r0 + 128, t0:t0 + 128], o[:, j, :], reads=[o], writes=[MIXT])


def build_M(T, L, alpha):
    Tc = T // 2
    p = Prog()
    xT = p.din("xT", [2048, T])
    x0 = p.din("x", [Tc, 2048])
    memT = p.din("memT", [2048, 256])
    flags = p.din("flags", [1, 2])
    out = p.dout("out", [Tc, 2048])
    cst = dict(pos=p.din("pos", [T, 1], I32), invf=p.din("invf", [1, 8]), maskb=p.din("maskb", [4, 128, 512]),
               ones=p.din("ones", [128, 128]), iota=p.din("iota", [1, 256]), c2=p.din("c2", [1, 255]), ident=make_ident(p))
    WA = [phase_A(p, T, None, None, pfx="l%d_" % l) for l in range(L)]
    WB = [phase_B_inputs(p, pfx="l%d_" % l) for l in range(L)]
    x_res = x0
    for l in range(L):
        mix = p.dtmp("mix%d" % l, [T, 1024])
        run_phase_A(p, T, xT, mix, WA[l], cst, "a%d_" % l)
        G = p.dtmp("G%d" % l, [2 * T, 1024])
        CHM = min(512, T)
        st_allgather(p, mix, G, T, CHM)
        MIXT = p.dtmp("MIXT%d" % l, [2048, Tc])
        st_sel_transpose(p, G, flags, MIXT, T, Tc, cst["ident"], CHM)
        last = (l == L - 1)
        xn = out if last else p.dtmp("XN%d" % l, [Tc, 2048])
        run_phase_B(p, Tc, x_res, MIXT, memT, xn, WB[l], cst, alpha, "b%d_" % l)
        if not last:
            GX = p.dtmp("GX%d" % l, [T, 2048])
            CHX = min(256, Tc)
            st_allgather(p, xn, GX, Tc, CHX)
            xT = p.dtmp("XT%d" % (l + 1), [2048, T])
            st_transpose(p, GX, None, xT, xT.t, T, 2048, cst["ident"],
                         rows=lambda t0: ag_rows(GX, CHX, t0 // Tc, t0 % Tc, 128))
            x_res = xn
    return p.finish()


BF16 = mybir.dt.bfloat16


def st_peer2(p, x_b, x_ap, EXFT, GTT_d, pu, pv, c2_d, ident, y_b, T, D=2048, NS=128):
    with p.stage():
        C2 = load_bcast(p, "c2", c2_d, 255)
        identb = p.sb("identb", [128, 128], BF16)
        p.op("dve", lambda e: e.tensor_copy(out=identb[:], in_=ident[:]), reads=[ident], writes=[identb])
        EXF = p.sb("exf", [NS, T])
        EXT = p.sb("ext", [NS, T], I32)
        GTT = p.sb("gtt", [NS, T])
        p.dma("sp", EXF[:], EXFT[:, :], reads=[EXFT], writes=[EXF])
        p.dma("act", GTT[:], GTT_d[:, :], reads=[GTT_d], writes=[GTT])
        p.op("dve", lambda e: e.tensor_copy(out=EXT[:], in_=EXF[:]), reads=[EXF], writes=[EXT])
        NB = 3
        UG = [p.sb("ug", [NS, D]) for _ in range(NB)]
        VG = [p.sb("vg", [NS, D]) for _ in range(NB)]
        JK = p.sb("junk", [NS, D])
        XF = p.sb("xf", [128, D])
        R1 = p.sb("r1", [128, D])
        XS = [[p.sb("xs%d" % k, [128, D], BF16) for k in range(3)] for _ in range(2)]
        SEL = [p.sb("sel", [128, 128], BF16) for _ in range(3)]
        HT = [p.sb("ht", [NS, 4]) for _ in range(NB)]
        ACM = [p.sb("acm", [NS, 128]) for _ in range(NB)]
        PSX = [p.ps("psx", [128, 512]) for _ in range(4)]
        PSY = [p.ps("psy", [128, 512]) for _ in range(4)]
        YO = [p.sb("yo", [128, 512]) for _ in range(4)]

        def gathers(t):
            ug, vg = UG[t % NB], VG[t % NB]
            p.dma_custom("pool", lambda e: e.indirect_dma_start(
                out=ug[:, :], out_offset=None, in_=pu[:, :],
                in_offset=bass.IndirectOffsetOnAxis(ap=EXT[:, t:t + 1], axis=0)), reads=[EXT, pu], writes=[ug])
            p.dma_custom("pool", lambda e: e.indirect_dma_start(
                out=vg[:, :], out_offset=None, in_=pv[:, :],
                in_offset=bass.IndirectOffsetOnAxis(ap=EXT[:, t:t + 1], axis=0)), reads=[EXT, pv], writes=[vg])

        def split_tile(t0):
            xs = XS[(t0 // 128) % 2]
            nr = min(128, T - t0)
            p.dma("sp", XF[0:nr, :], x_ap[t0:t0 + nr, :], reads=[x_b], writes=[XF])
            p.op("act", lambda e: e.activation(out=xs[0][:], in_=XF[:], func=AF.Copy), reads=[XF], writes=[xs[0]])
            tt(p, "pool", R1[:], XF[:], xs[0][:], ALU.subtract, [XF, xs[0]], [R1])
            p.op("act", lambda e: e.activation(out=xs[1][:], in_=R1[:], func=AF.Copy), reads=[R1], writes=[xs[1]])
            tt(p, "pool", R1[:], R1[:], xs[1][:], ALU.subtract, [R1, xs[1]], [R1])
            p.op("act", lambda e: e.activation(out=xs[2][:], in_=R1[:], func=AF.Copy), reads=[R1], writes=[xs[2]])

        def bcast(t):
            tl = t % 128
            xs = XS[(t // 128) % 2]
            sel = SEL[t % 3]
            p.op("act", lambda e: e.activation(out=sel[:], in_=identb[:, tl:tl + 1].to_broadcast([128, 128]), func=AF.Copy),
                 reads=[identb], writes=[sel])
            for n in range(4):
                for k in range(3):
                    p.op("pe", lambda e, n=n, k=k: e.matmul(PSX[n][:, :], sel[:], xs[k][:, n * 512:(n + 1) * 512],
                                                            start=(k == 0), stop=(k == 2)), reads=[sel, xs[k]], writes=[PSX[n]])

        PRE = 2
        for t in range(min(PRE, T)):
            gathers(t)
        split_tile(0)
        bcast(0)
        for t in range(T):
            tl = t % 128
            ug, vg, ht, acm = (z[t % NB] for z in (UG, VG, HT, ACM))
            if t + PRE < T:
                gathers(t + PRE)
            for n in range(4):
                tt(p, "dve", JK[:, n * 512:(n + 1) * 512], ug[:, n * 512:(n + 1) * 512], PSX[n][:, :], ALU.mult, [ug, PSX[n]], [JK])
            rsum(p, "dve", ht[:, 0:1], JK[:], [JK], [ht])
            if t + 1 < T:
                if (t + 1) % 128 == 0:
                    split_tile(t + 1)
                bcast(t + 1)
            actf(p, ht[:, 1:2], ht[:, 0:1], AF.Gelu, [ht], [ht])
            tt(p, "dve", ht[:, 2:3], ht[:, 1:2], GTT[:, t:t + 1], ALU.mult, [ht, GTT], [ht])
            ts(p, "dve", acm[:], C2[:, 127 - tl:255 - tl], ht[:, 2:3], ALU.mult, [C2, ht], [acm])
            for n in range(4):
                p.op("pe", lambda e: e.matmul(PSY[n][:, :], acm[:], vg[:, n * 512:(n + 1) * 512],
                                              start=(tl == 0), stop=(tl == 127 or t == T - 1)),
                     reads=[acm, vg], writes=[PSY[n]])
            if tl == 127 or t == T - 1:
                t0 = t - tl
                nr = tl + 1
                for n in range(4):
                    if n % 2:
                        p.op("act", lambda e: e.activation(out=YO[n][:], in_=PSY[n][:], func=AF.Copy), reads=[PSY[n]], writes=[YO[n]])
                    else:
                        p.op("dve", lambda e: e.tensor_copy(out=YO[n][:], in_=PSY[n][:]), reads=[PSY[n]], writes=[YO[n]])
                    p.dma("sp", y_b[t0:t0 + nr, n * 512:(n + 1) * 512], YO[n][0:nr, :], reads=[YO[n]], writes=[y_b])


def attn_body_pe_act(p, q_b, qT, k_b, kT, v_b, v_ap, o_b, oT, z_b, zT, maskb_d, ones_d, ident, NH, NC, DK, DV, T, scale):
    NKT = T // 128
    QB = 512
    MB = p.sb("maskb", [128, 4, 512])
    for j in range(4):
        p.dma("act", MB[:, j, :], maskb_d[j], writes=[MB])
    ones = p.sb("ones", [128, 128])
    p.dma("act", ones[:], ones_d[:, :], writes=[ones])
    Q = p.sb("qT", [DK, T])
    K = p.sb("kT", [DK, T])
    Vh = p.sb("v", [128, NKT, DV])
    PS_S = [p.ps("pss", [128, 512]) for _ in range(2)]
    PS_O = [p.ps("pso", [128, 512]) for _ in range(2)]
    PS_Z = [p.ps("psz", [128, 512]) for _ in range(2)]
    P = [p.sb("p", [128, 512]) for _ in range(3)]
    O = [p.sb("o", [128, 512]) for _ in range(2)]
    Z = [p.sb("z", [128, 512]) for _ in range(2)]
    iq = ik = 0
    for h in range(NH):
        p.dma("act", Vh[:], v_ap(h).rearrange("(kt p) d -> p kt d", p=128), reads=[v_b], writes=[Vh])
        for c in range(NC):
            hc = h * NC + c
            p.dma("act", Q[:], qT(hc), reads=[q_b], writes=[Q])
            p.dma("act", K[:], kT(hc), reads=[k_b], writes=[K])
            for qb in range(T // QB):
                q0 = qb * QB
                nkt = (q0 + QB) // 128
                pso, psz = PS_O[iq % 2], PS_Z[iq % 2]
                ob, zb = O[iq % 2], Z[iq % 2]
                iq += 1

                def oz(kt, Pt):
                    p.op("pe", lambda e: e.matmul(pso[0:DV, :], Vh[:, kt, :], Pt[:, :], start=(kt == 0), stop=(kt == nkt - 1)),
                         reads=[Vh, Pt], writes=[pso])
                    p.op("pe", lambda e: e.matmul(psz[:, :], ones[:], Pt[:, :], start=(kt == 0), stop=(kt == nkt - 1)),
                         reads=[ones, Pt], writes=[psz])
                prev = None
                for kt in range(nkt):
                    pss = PS_S[ik % 2]
                    Pt = P[ik % 3]
                    ik += 1
                    diag = kt * 128 + 127 > q0
                    p.op("pe", lambda e: e.matmul(pss[:, :], K[:, kt * 128:(kt + 1) * 128], Q[:, q0:q0 + QB], start=True, stop=not diag),
                         reads=[K, Q], writes=[pss])
                    if diag:
                        j = kt - q0 // 128
                        p.op("pe", lambda e: e.matmul(pss[:, :], ident[:, :], MB[:, j, :], start=False, stop=True),
                             reads=[ident, MB], writes=[pss])
                    actf(p, Pt[:, :], pss[:, :], AF.Exp, [pss], [Pt], scale=scale)
                    if prev is not None:
                        oz(*prev)
                    prev = (kt, Pt)
                oz(*prev)
                p.op("act", lambda e: e.activation(out=ob[0:DV, :], in_=pso[0:DV, :], func=AF.Copy), reads=[pso], writes=[ob])
                p.op("act", lambda e: e.activation(out=zb[0:DV, :], in_=psz[0:DV, :], func=AF.Copy), reads=[psz], writes=[zb])
                p.dma("act", oT(hc)[:, q0:q0 + QB], ob[0:DV, :], reads=[ob], writes=[o_b])
                p.dma("act", zT(hc)[:, q0:q0 + QB], zb[0:DV, :], reads=[zb], writes=[z_b])


def st_scores(p, QPT, skT_b, S1, S2, Tc):
    with p.stage():
        SK = p.sb("sk", [128, 2, 128])
        for c in range(2):
            p.dma("sp", SK[:, c, :], skT_b.t[c], reads=[skT_b], writes=[SK])
        A = [p.sb("a", [128, 512]) for _ in range(3)]
        PS = [p.ps("ps", [128, 512]) for _ in range(8)]
        O = [p.sb("o", [128, 512]) for _ in range(8)]
        ia = ig = 0
        for m0 in range(0, Tc, 512):
            mw = min(512, Tc - m0)
            for c, Sx in ((0, S1), (1, S2)):
                for hg in range(2):
                    base = (ig % 2) * 4
                    ig += 1
                    for hh in range(4):
                        h = hg * 4 + hh
                        r0 = (h * 2 + c) * 128
                        a = A[ia % 3]
                        ia += 1
                        p.dma("sp" if ia % 2 else "act", a[:, 0:mw], QPT.t[r0:r0 + 128, m0:m0 + mw], reads=[QPT], writes=[a])
                        for mt in range(0, mw, 128):
                            ps = PS[base + mt // 128]
                            p.op("pe", lambda e: e.matmul(ps[:, hh * 128:(hh + 1) * 128], a[:, mt:mt + 128], SK[:, c, :],
                                                          start=True, stop=True), reads=[a, SK], writes=[ps])
                    for mt in range(0, mw, 128):
                        ps, o = PS[base + mt // 128], O[base + mt // 128]
                        if (mt // 128) % 2:
                            p.op("act", lambda e: e.activation(out=o[:], in_=ps[:], func=AF.Copy), reads=[ps], writes=[o])
                        else:
                            p.op("dve", lambda e: e.tensor_copy(out=o[:], in_=ps[:]), reads=[ps], writes=[o])
                        p.dma("pool", Sx[m0 + mt:m0 + mt + 128, hg * 512:(hg + 1) * 512], o[:], reads=[o], writes=[Sx])


NCORES = 8
_PROGS = {}
_WPERM = np.concatenate([np.arange(0, 512), np.arange(1024, 1536), np.arange(512, 1024), np.arange(1536, 2048)])


def _c(a):
    return np.ascontiguousarray(a)


def _row(v):
    return _c(np.asarray(v, np.float32).reshape(1, -1))


def _consts():
    invf = (np.float32(500000.0) ** (-(np.arange(0, 16, 2, dtype=np.float32) / np.float32(16)))).astype(np.float32)[None]
    iota = np.arange(256, dtype=np.float32)[None]
    c2 = np.zeros((1, 255), np.float32)
    c2[0, 127] = 1
    return dict(invf=invf, iota=iota, c2=c2, maskb=(causal_masks() - 1.0).astype(np.float32) * np.float32(1e5),
                ones=np.ones((128, 128), np.float32), ident=np.eye(128, dtype=np.float32))


def _A_weights(l, hh, P):
    f = lambda a: np.asarray(a, np.float32)
    C = 512
    s = slice(hh * C, (hh + 1) * C)
    ar = np.arange(hh * C, (hh + 1) * C)
    cols = np.concatenate([ar, 1024 + ar, 2048 + ar, np.arange(3072, 3360), 3360 + ar, 4384 + ar, 5408 + ar])
    lam_init = 0.8 - 0.6 * float(np.exp(-0.3 * l))
    return {"w_c": _c(f(P["w_in"][l])[:, cols]), "mu": _row(f(P["shift_mu"][l])[cols[:1824]]),
            "wup": _c(f(P["w_up"][l])[:, s]), "aup": _c(f(P["a_up"][l])[:, s]), "gup": _c(f(P["g_up"][l])[:, s]),
            "w0": _row(f(P["w0"][l])[s]), "a0": _row(f(P["a0"][l])[s]), "kk": _row(f(P["k_k"][l])[s]), "ka": _row(f(P["k_a"][l])[s]),
            "rk": _row(f(P["r_k"][l]).reshape(-1)[s]), "lnxw": _row(f(P["lnx_w"][l])[s]), "lnxb": _row(f(P["lnx_b"][l])[s]),
            "sw": _row(np.tile(f(P["subln_w"][l]), 4)),
            "lq1": _row(P["lam_q1"][l]), "lk1": _row(P["lam_k1"][l]), "lq2": _row(P["lam_q2"][l]), "lk2": _row(P["lam_k2"][l]),
            "lamc": np.array([[lam_init, 1.0 - lam_init]], np.float32)}


def _B_weights(l, P):
    f = lambda a: np.asarray(a, np.float32)
    return {"w_out": _c(f(P["w_out"][l])[_WPERM]), "xq": _c(f(P["xq"][l])), "xk": _c(f(P["xk"][l])), "xv": _c(f(P["xv"][l])),
            "xo": _c(f(P["xo"][l])), "pq": _c(f(P["pq"][l])), "skT": _c(f(P["subkeys"][l]).transpose(0, 2, 1)),
            "pu": _c(f(P["peer_u"][l])), "pv": _c(f(P["peer_v"][l])),
            "ln1w": _row(P["ln1_w"][l]), "ln1b": _row(P["ln1_b"][l]), "ln2w": _row(P["ln2_w"][l]), "ln2b": _row(P["ln2_b"][l]),
            "ln3w": _row(P["ln3_w"][l]), "ln3b": _row(P["ln3_b"][l])}


def kernel(**inputs):
    P = inputs
    x = np.asarray(P["x"], np.float32)
    mem = np.asarray(P["mem"], np.float32)
    pos = np.asarray(P["positions"]).astype(np.int32)
    B, S, D = x.shape
    L = np.asarray(P["w_in"]).shape[0]
    alpha = (2.0 * L) ** 0.25
    Tc = S // 2
    K = _consts()
    WB = [_B_weights(l, P) for l in range(L)]
    WA = [[_A_weights(l, hh, P) for hh in range(2)] for l in range(L)]
    maps = []
    for c in range(NCORES):
        b_, hh = c // 2, c % 2
        m = {"xT": _c(x[b_].T), "x": _c(x[b_, hh * Tc:(hh + 1) * Tc]), "memT": _c(mem[b_].T),
             "flags": np.array([[1.0 - hh, float(hh)]], np.float32), "pos": _c(pos[b_].reshape(-1, 1))}
        m.update(K)
        for l in range(L):
            m.update({"l%d_%s" % (l, k): v for k, v in WA[l][hh].items()})
            m.update({"l%d_%s" % (l, k): v for k, v in WB[l].items()})
        maps.append(m)
    key = (S, L)
    if key not in _PROGS:
        _PROGS[key] = build_M(S, L, alpha)
    res = run_bass_kernel_spmd(_PROGS[key], maps, core_ids=list(range(NCORES)))
    out = np.zeros((B, S, D), np.float32)
    for c in range(NCORES):
        b_, hh = c // 2, c % 2
        out[b_, hh * Tc:(hh + 1) * Tc] = res.results[c]["out"]
    return out
```

```python
import contextlib
import numpy as np
import concourse.bass as bass
import concourse.mybir as mybir
from concourse.bass_utils import run_bass_kernel_spmd

F32 = mybir.dt.float32
I32 = mybir.dt.int32
U32 = mybir.dt.uint32
ALU = mybir.AluOpType
AF = mybir.ActivationFunctionType
AX = mybir.AxisListType


EMBED_WAIT = True


class Buf:
    def __init__(self, t, name, multi=False):
        self.t = t
        self.name = name
        self.multi = multi
        self.w = {}
        self.r = {}

    def __getitem__(self, idx):
        return self.t[idx]


class Prog:
    NDMA = 48

    def __init__(self):
        self.nc = bass.Bass("TRN2", target_bir_lowering=False)
        self.es = contextlib.ExitStack()
        nc = self.nc
        self.eng = {"pe": nc.tensor, "act": nc.scalar, "dve": nc.vector, "pool": nc.gpsimd, "sp": nc.sync}
        self.sem = {}
        self.cnt = {}
        self.seen = {e: {} for e in self.eng}
        self.semobj = {}
        for e in self.eng:
            s = self.es.enter_context(nc.semaphore("s_" + e))
            self.sem[e] = s
            self.semobj[e] = s
            self.cnt[e] = 0
        self.dsem = []
        self.dval = []
        for i in range(self.NDMA):
            s = self.es.enter_context(nc.semaphore("d%d" % i))
            self.dsem.append(s)
            self.dval.append(0)
            self.semobj[("d", i)] = s
        self.dnext = 0
        self.dpool = None
        self.dnext_sub = {}
        self.outs = []
        self.n = 0

    def din(self, name, shape, dt=F32):
        return Buf(self.nc.dram_tensor(name, list(shape), dt, kind="ExternalInput").ap(), name)

    def dout(self, name, shape, dt=F32):
        b = Buf(self.nc.dram_tensor(name, list(shape), dt, kind="ExternalOutput").ap(), name, multi=True)
        self.outs.append(b)
        return b

    def dtmp(self, name, shape, dt=F32):
        return Buf(self.nc.dram_tensor(name, list(shape), dt, kind="Internal").ap(), name, multi=True)

    def sb(self, name, shape, dt=F32, multi=True):
        self.n += 1
        return Buf(self.es.enter_context(self.nc.sbuf_tensor("%s_%d" % (name, self.n), list(shape), dt)), name, multi=multi)

    def ps(self, name, shape, dt=F32):
        self.n += 1
        return Buf(self.es.enter_context(self.nc.psum_tensor("%s_%d" % (name, self.n), list(shape), dt)), name)

    def _wait(self, e, deps, skip_self=False, keep_last=False):
        need = []
        for k, v in deps.items():
            if skip_self and k == e:
                continue
            if self.seen[e].get(k, 0) < v:
                need.append((k, v))
                self.seen[e][k] = v
        last = need.pop() if (keep_last and need) else None
        for k, v in need:
            self.eng[e].wait_ge(self.semobj[k], v)
        return last

    @staticmethod
    def _deps(reads, writes, is_dma=False):
        d = {}
        for b in reads:
            for k, v in b.w.items():
                if d.get(k, 0) < v:
                    d[k] = v
        for b in writes:
            for src in (b.w, b.r):
                for k, v in src.items():
                    if is_dma and b.multi and src is b.w and isinstance(k, tuple):
                        continue
                    if d.get(k, 0) < v:
                        d[k] = v
        return d

    @staticmethod
    def _mark(ev, reads, writes):
        k, v = ev
        for b in reads:
            if b.r.get(k, 0) < v:
                b.r[k] = v
        for b in writes:
            if b.w.get(k, 0) < v:
                b.w[k] = v

    def op(self, e, fn, reads=(), writes=()):
        last = self._wait(e, self._deps(reads, writes), skip_self=(e == "pe"), keep_last=EMBED_WAIT)
        ins = fn(self.eng[e])
        if last is not None:
            ins._wait_ge(self.semobj[last[0]], last[1])
        self.cnt[e] += 1
        ins.then_inc(self.sem[e], 1)
        self._mark((e, self.cnt[e]), reads, writes)
        return ins

    def dma(self, q, out_ap, in_ap, reads=(), writes=(), **kw):
        self._wait(q, self._deps(reads, writes, is_dma=True))
        i = self._next_dsem()
        k = ("d", i)
        if self.dval[i] > 0:
            self._wait(q, {k: self.dval[i]})
        ins = self.eng[q].dma_start(out=out_ap, in_=in_ap, **kw)
        self.dval[i] += 16
        ins.then_inc(self.dsem[i], 16)
        self._mark((k, self.dval[i]), reads, writes)
        return ins

    def finish(self):
        d = {}
        for b in self.outs:
            for k, v in b.w.items():
                if d.get(k, 0) < v:
                    d[k] = v
        self._wait("sp", d)
        self._wait("sp", {e: self.cnt[e] for e in self.eng if self.cnt[e] > 0 and e != "sp"})
        self.es.close()
        return self.nc


def run(prog_nc, in_maps):
    res = run_bass_kernel_spmd(prog_nc, in_maps, core_ids=list(range(len(in_maps))))
    return res.results


def _dma_custom(self, q, fn, reads=(), writes=()):
    self._wait(q, self._deps(reads, writes, is_dma=True))
    i = self._next_dsem()
    k = ("d", i)
    if self.dval[i] > 0:
        self._wait(q, {k: self.dval[i]})
    ins = fn(self.eng[q])
    self.dval[i] += 16
    ins.then_inc(self.dsem[i], 16)
    self._mark((k, self.dval[i]), reads, writes)
    return ins


Prog.dma_custom = _dma_custom


class _Stage:
    def __init__(self, p):
        self.p = p

    def __enter__(self):
        self.saved = self.p.es
        self.p.es = contextlib.ExitStack()
        return self

    def __exit__(self, *a):
        self.p.barrier()
        self.p.es.close()
        self.p.es = self.saved
        return False


def _barrier(self):
    alld = {("d", i): v for i, v in enumerate(self.dval) if v > 0}
    for e in self.eng:
        deps = dict(alld)
        for o in self.eng:
            if o != e and self.cnt[o] > 0:
                deps[o] = self.cnt[o]
        self._wait(e, deps)


Prog.barrier = _barrier
Prog.stage = lambda self: _Stage(self)


def _next_dsem(self):
    if self.dpool is None:
        i = self.dnext
        self.dnext = (self.dnext + 1) % self.NDMA
        return i
    lo, hi = self.dpool
    j = self.dnext_sub.get(self.dpool, 0)
    self.dnext_sub[self.dpool] = (j + 1) % (hi - lo)
    return lo + j


Prog._next_dsem = _next_dsem


def bcast_rows(ap_1d_or_2d, nparts):
    return ap_1d_or_2d.partition_broadcast(nparts)


def build_gemm(K, M, N, MB=512, NB=512):
    p = Prog()
    at = p.din("at", [K, M])
    b = p.din("b", [K, N])
    c = p.dout("c", [M, N])
    KP = min(K, 128)
    KC = (K + 127) // 128
    atv = at.t.rearrange("(kc kp) m -> kp kc m", kp=KP)
    bv = b.t.rearrange("(kc kp) n -> kp kc n", kp=KP)
    MB = min(MB, M)
    a_sb = [p.sb("a", [KP, KC, MB]) for _ in range(2)]
    b_sb = [p.sb("b", [KP, KC, NB]) for _ in range(2)]
    pss = [p.ps("ps", [128, NB]) for _ in range(4)]
    o_sb = [p.sb("o", [128, NB]) for _ in range(4)]
    ia = ib = io = 0
    for m0 in range(0, M, MB):
        mw = min(MB, M - m0)
        A = a_sb[ia % 2]
        ia += 1
        for kc in range(KC):
            p.dma("sp", A[:, kc, 0:mw], atv[:, kc, m0:m0 + mw], writes=[A])
        for n0 in range(0, N, NB):
            nw = min(NB, N - n0)
            Bt = b_sb[ib % 2]
            ib += 1
            for kc in range(KC):
                p.dma("act" if kc % 2 else "sp", Bt[:, kc, 0:nw], bv[:, kc, n0:n0 + nw], writes=[Bt])
            for mt in range(0, mw, 128):
                mm = min(128, mw - mt)
                P_ = pss[io % 4]
                O = o_sb[io % 4]
                io += 1
                for kc in range(KC):
                    p.op("pe", lambda e, kc=kc: e.matmul(P_[0:mm, 0:nw], A[:, kc, mt:mt + mm], Bt[:, kc, 0:nw],
                                                         start=(kc == 0), stop=(kc == KC - 1)),
                         reads=[A, Bt], writes=[P_])
                if io % 2:
                    p.op("act", lambda e: e.activation(out=O[0:mm, 0:nw], in_=P_[0:mm, 0:nw], func=AF.Copy),
                         reads=[P_], writes=[O])
                else:
                    p.op("dve", lambda e: e.tensor_copy(out=O[0:mm, 0:nw], in_=P_[0:mm, 0:nw]),
                         reads=[P_], writes=[O])
                p.dma("pool", c[m0 + mt:m0 + mt + mm, n0:n0 + nw], O[0:mm, 0:nw], reads=[O], writes=[c])
    return p.finish()


def build_ln(T, D, alpha, eps):
    p = Prog()
    x = p.din("x", [T, D])
    y = p.din("y", [T, D])
    w = p.din("w", [1, D])
    b = p.din("b", [1, D])
    o = p.dout("o", [T, D])
    wb = p.sb("wb", [128, D])
    bb = p.sb("bb", [128, D])
    p.dma("sp", wb[:], bcast_rows(w.t[0], 128), writes=[wb])
    p.dma("sp", bb[:], bcast_rows(b.t[0], 128), writes=[bb])
    NBUF = 2
    xs = [p.sb("x", [128, D]) for _ in range(NBUF)]
    ys = [p.sb("y", [128, D]) for _ in range(NBUF)]
    zs = [p.sb("z", [128, D]) for _ in range(NBUF)]
    st = [p.sb("st", [128, 8]) for _ in range(NBUF)]
    for i, t0 in enumerate(range(0, T, 128)):
        X, Y, Z, S = xs[i % NBUF], ys[i % NBUF], zs[i % NBUF], st[i % NBUF]
        p.dma("sp", X[:], x[t0:t0 + 128, :], writes=[X])
        p.dma("act", Y[:], y[t0:t0 + 128, :], writes=[Y])
        p.op("dve", lambda e: e.scalar_tensor_tensor(out=Z[:], in0=X[:], scalar=float(alpha), in1=Y[:],
                                                     op0=ALU.mult, op1=ALU.add), reads=[X, Y], writes=[Z])
        ln_rows(p, Z, X, S, wb, bb, D, eps)
        p.dma("pool", o[t0:t0 + 128, :], X[:], reads=[X], writes=[o])
    return p.finish()


def ln_rows(p, Z, OUT, S, wb, bb, D, eps, tmp=None):
    p.op("dve", lambda e: e.reduce_sum(out=S[:, 0:1], in_=Z[:], axis=AX.X), reads=[Z], writes=[S])
    p.op("dve", lambda e: e.tensor_scalar(out=S[:, 1:2], in0=S[:, 0:1], scalar1=1.0 / D, scalar2=None, op0=ALU.mult),
         reads=[S], writes=[S])
    p.op("dve", lambda e: e.tensor_scalar(out=Z[:], in0=Z[:], scalar1=S[:, 1:2], scalar2=None, op0=ALU.subtract),
         reads=[Z, S], writes=[Z])
    p.op("dve", lambda e: e.tensor_tensor(out=OUT[:], in0=Z[:], in1=Z[:], op=ALU.mult), reads=[Z], writes=[OUT])
    p.op("dve", lambda e: e.reduce_sum(out=S[:, 2:3], in_=OUT[:], axis=AX.X), reads=[OUT], writes=[S])
    rsqrt_col(p, S, 2, 3, 1.0 / D, eps)
    p.op("dve", lambda e: e.scalar_tensor_tensor(out=OUT[:], in0=Z[:], scalar=S[:, 3:4], in1=wb[:],
                                                 op0=ALU.mult, op1=ALU.mult), reads=[Z, S, wb], writes=[OUT])
    p.op("dve", lambda e: e.tensor_tensor(out=OUT[:], in0=OUT[:], in1=bb[:], op=ALU.add), reads=[OUT, bb], writes=[OUT])


def rsqrt_col(p, S, i, o, mul, eps):
    p.op("dve", lambda e: e.tensor_scalar(out=S[:, o:o + 1], in0=S[:, i:i + 1], scalar1=float(mul), scalar2=float(eps),
                                          op0=ALU.mult, op1=ALU.add), reads=[S], writes=[S])
    p.op("act", lambda e: e.sqrt(out=S[:, o:o + 1], in_=S[:, o:o + 1]), reads=[S], writes=[S])
    p.op("dve", lambda e: e.reciprocal(out=S[:, o:o + 1], in_=S[:, o:o + 1]), reads=[S], writes=[S])


def tt(p, eng, out, a, b, op, R, W):
    return p.op(eng, lambda e: e.tensor_tensor(out=out, in0=a, in1=b, op=op), reads=R, writes=W)


def ts(p, eng, out, a, s1, op0, R, W, s2=None, op1=None):
    if op1 is None:
        return p.op(eng, lambda e: e.tensor_scalar(out=out, in0=a, scalar1=s1, scalar2=None, op0=op0), reads=R, writes=W)
    return p.op(eng, lambda e: e.tensor_scalar(out=out, in0=a, scalar1=s1, scalar2=s2, op0=op0, op1=op1),
                reads=R, writes=W)


def actf(p, out, a, func, R, W, scale=1.0):
    return p.op("act", lambda e: e.activation(out=out, in_=a, func=func, scale=float(scale)), reads=R, writes=W)


def rsum(p, eng, out, a, R, W):
    return p.op(eng, lambda e: e.reduce_sum(out=out, in_=a, axis=AX.X), reads=R, writes=W)


def load_bcast(p, name, dram, n):
    t = p.sb(name, [128, n])
    p.dma("sp", t[:], dram.t[0].partition_broadcast(128), writes=[t])
    return t


def build_ew1(T, NCOL=3360, C=1024):
    p = Prog()
    x = p.din("p", [T, NCOL])
    xp = p.din("pp", [T, NCOL])
    mu = p.din("mu", [1, NCOL])
    o = p.dout("o", [T, NCOL])
    mub = load_bcast(p, "mub", mu, NCOL)
    NB = 2
    X = [p.sb("x", [128, NCOL]) for _ in range(NB)]
    XP = [p.sb("xp", [128, NCOL]) for _ in range(NB)]
    for i, t0 in enumerate(range(0, T, 128)):
        a, b = X[i % NB], XP[i % NB]
        p.dma("sp", a[:], x[t0:t0 + 128, :], writes=[a])
        p.dma("act", b[:], xp[t0:t0 + 128, :], writes=[b])
        tt(p, "dve", b[:], b[:], a[:], ALU.subtract, [a, b], [b])
        tt(p, "pool", b[:], b[:], mub[:], ALU.mult, [b, mub], [b])
        tt(p, "dve", a[:], a[:], b[:], ALU.add, [a, b], [a])
        o1 = 3 * C
        actf(p, a[:, o1:o1 + 64], a[:, o1:o1 + 64], AF.Tanh, [a], [a])
        actf(p, a[:, o1 + 128:NCOL], a[:, o1 + 128:NCOL], AF.Sigmoid, [a], [a])
        p.dma("pool", o[t0:t0 + 128, :], a[:], reads=[a], writes=[o])
    return p.finish()


def build_ew2(T, C=1024, N=64):
    H = C // N
    p = Prog()
    pf = p.din("pf", [T, 3 * C])
    wl = p.din("wl", [T, C])
    al = p.din("al", [T, C])
    prm = {k: load_bcast(p, k, p.din(k, [1, C]), C) for k in ("w0", "a0", "kk", "ka", "rk")}
    outs = {k: p.dout(k, [T, C]) for k in ("dec", "km", "a", "b", "bon")}
    NB = 2
    mk = lambda nm: [p.sb(nm, [128, C]) for _ in range(NB)]
    Rs, Ks, Vs, WLs, ALs, KKs, T1s, T2s = mk("r"), mk("k"), mk("v"), mk("wl"), mk("al"), mk("kk"), mk("t1"), mk("t2")
    SSs = [p.sb("ss", [128, 2 * H]) for _ in range(NB)]
    v3 = lambda ap: ap.rearrange("p (h j) -> p h j", j=N)
    for i, t0 in enumerate(range(0, T, 128)):
        R, K, V, WL, AL, KK, T1, T2, SS = (z[i % NB] for z in (Rs, Ks, Vs, WLs, ALs, KKs, T1s, T2s, SSs))
        p.dma("sp", R[:], pf[t0:t0 + 128, 0:C], writes=[R])
        p.dma("act", K[:], pf[t0:t0 + 128, C:2 * C], writes=[K])
        p.dma("sp", V[:], pf[t0:t0 + 128, 2 * C:3 * C], writes=[V])
        p.dma("act", WL[:], wl[t0:t0 + 128, :], writes=[WL])
        p.dma("sp", AL[:], al[t0:t0 + 128, :], writes=[AL])
        tt(p, "dve", WL[:], WL[:], prm["w0"][:], ALU.add, [WL, prm["w0"]], [WL])
        actf(p, WL[:], WL[:], AF.Sigmoid, [WL], [WL])
        actf(p, WL[:], WL[:], AF.Exp, [WL], [WL], scale=-0.6065306597126334)
        p.dma("pool", outs["dec"][t0:t0 + 128, :], WL[:], reads=[WL], writes=[outs["dec"]])
        tt(p, "pool", AL[:], AL[:], prm["a0"][:], ALU.add, [AL, prm["a0"]], [AL])
        actf(p, AL[:], AL[:], AF.Sigmoid, [AL], [AL])
        tt(p, "dve", KK[:], K[:], prm["kk"][:], ALU.mult, [K, prm["kk"]], [KK])
        tt(p, "pool", T1[:], KK[:], KK[:], ALU.mult, [KK], [T1])
        rsum(p, "dve", SS[:, 0:H], v3(T1[:]), [T1], [SS])
        p.op("act", lambda e: e.sqrt(out=SS[:, 0:H], in_=SS[:, 0:H]), reads=[SS], writes=[SS])
        ts(p, "dve", SS[:, 0:H], SS[:, 0:H], 1e-12, ALU.max, [SS], [SS])
        p.op("dve", lambda e: e.reciprocal(out=SS[:, 0:H], in_=SS[:, 0:H]), reads=[SS], writes=[SS])
        tt(p, "dve", v3(KK[:]), v3(KK[:]), SS[:, 0:H].unsqueeze(2).to_broadcast([128, H, N]), ALU.mult, [KK, SS], [KK])
        p.op("dve", lambda e: e.scalar_tensor_tensor(out=T1[:], in0=AL[:], scalar=-1.0, in1=prm["ka"][:],
                                                     op0=ALU.add, op1=ALU.mult), reads=[AL, prm["ka"]], writes=[T1])
        ts(p, "pool", T1[:], T1[:], 1.0, ALU.add, [T1], [T1])
        tt(p, "dve", T1[:], T1[:], K[:], ALU.mult, [T1, K], [T1])
        p.dma("pool", outs["km"][t0:t0 + 128, :], T1[:], reads=[T1], writes=[outs["km"]])
        tt(p, "pool", T2[:], KK[:], AL[:], ALU.mult, [KK, AL], [T2])
        p.dma("pool", outs["b"][t0:t0 + 128, :], T2[:], reads=[T2], writes=[outs["b"]])
        ts(p, "dve", KK[:], KK[:], -1.0, ALU.mult, [KK], [KK])
        p.dma("pool", outs["a"][t0:t0 + 128, :], KK[:], reads=[KK], writes=[outs["a"]])
        tt(p, "dve", R[:], R[:], T1[:], ALU.mult, [R, T1], [R])
        tt(p, "pool", R[:], R[:], prm["rk"][:], ALU.mult, [R, prm["rk"]], [R])
        rsum(p, "dve", SS[:, H:2 * H], v3(R[:]), [R], [SS])
        tt(p, "dve", v3(V[:]), v3(V[:]), SS[:, H:2 * H].unsqueeze(2).to_broadcast([128, H, N]), ALU.mult, [V, SS], [V])
        p.dma("pool", outs["bon"][t0:t0 + 128, :], V[:], reads=[V], writes=[outs["bon"]])
    return p.finish()


def build_scan(T, Tt=8):
    p = Prog()
    bc = p.din("bc", [5, 2, T * 256])
    vfm = p.din("vfm", [128, T * 4])
    yfm = p.dout("yfm", [128, T * 4])
    NB = 2
    BC = [[p.sb("bc%d" % a, [128, Tt * 256]) for a in range(5)] for _ in range(NB)]
    Vt = [p.sb("v", [128, Tt * 4]) for _ in range(NB)]
    KV = [p.sb("kv", [128, Tt * 256]) for _ in range(NB)]
    Yt = [p.sb("y", [128, Tt * 4]) for _ in range(NB)]
    S = [p.sb("s", [128, 256]) for _ in range(2)]
    S2 = p.sb("s2", [128, 256])
    TMP = p.sb("tmp", [128, 256])
    TMP2 = p.sb("tmp2", [128, 256])
    TMP3 = [p.sb("tmp3", [128, 256]) for _ in range(2)]
    SA = p.sb("sa", [128, 4])
    p.op("dve", lambda e: e.memset(S[0][:], 0.0), writes=[S[0]])
    v3 = lambda ap: ap.rearrange("p (g j) -> p g j", j=64)
    step = 0
    for it, t0 in enumerate(range(0, T, Tt)):
        sl = it % NB
        A, W, B, K, R = BC[sl]
        for a in range(5):
            for half in range(2):
                p.dma("sp" if (a + half) % 2 else "act", BC[sl][a][half * 64:(half + 1) * 64, :],
                      bc.t[a, half, t0 * 256:(t0 + Tt) * 256].partition_broadcast(64), writes=[BC[sl][a]])
        p.dma("sp", Vt[sl][:], vfm[:, t0 * 4:(t0 + Tt) * 4], writes=[Vt[sl]])
        tt(p, "pool", KV[sl][:].rearrange("p (tg j) -> p tg j", j=64), K[:].rearrange("p (tg j) -> p tg j", j=64),
           Vt[sl][:].unsqueeze(2).to_broadcast([128, Tt * 4, 64]), ALU.mult, [K, Vt[sl]], [KV[sl]])
        for q in range(Tt):
            cs = slice(q * 256, (q + 1) * 256)
            Sc, Sn = S[step % 2], S[(step + 1) % 2]
            T3 = TMP3[step % 2]
            tt(p, "dve", TMP[:], Sc[:], A[:, cs], ALU.mult, [Sc, A], [TMP])
            tt(p, "pool", S2[:], Sc[:], W[:, cs], ALU.mult, [Sc, W], [S2])
            rsum(p, "dve", SA[:], v3(TMP[:]), [TMP], [SA])
            tt(p, "dve", v3(TMP2[:]), v3(B[:, cs]), SA[:].unsqueeze(2).to_broadcast([128, 4, 64]), ALU.mult, [B, SA], [TMP2])
            tt(p, "dve", S2[:], S2[:], TMP2[:], ALU.add, [S2, TMP2], [S2])
            tt(p, "dve", Sn[:], S2[:], KV[sl][:, cs], ALU.add, [S2, KV[sl]], [Sn])
            tt(p, "pool", T3[:], Sn[:], R[:, cs], ALU.mult, [Sn, R], [T3])
            rsum(p, "dve", Yt[sl][:, q * 4:(q + 1) * 4], v3(T3[:]), [T3], [Yt[sl]])
            step += 1
        p.dma("pool", yfm[:, t0 * 4:(t0 + Tt) * 4], Yt[sl][:], reads=[Yt[sl]], writes=[yfm])
    return p.finish()


def scan_host_in(a, w, b, k, r, v):
    T = a.shape[0]
    def lay(x):
        return np.ascontiguousarray(x.reshape(T, 4, 2, 64).transpose(2, 0, 1, 3)).reshape(2, T * 256)
    bc = np.stack([lay(a), lay(w), lay(b), lay(k), lay(r)], 0)
    vfm = np.ascontiguousarray(v.reshape(T, 4, 2, 64).transpose(2, 3, 0, 1)).reshape(128, T * 4)
    return {"bc": bc, "vfm": vfm}


def scan_host_out(yfm, T):
    return np.ascontiguousarray(yfm.reshape(2, 64, T, 4).transpose(2, 3, 0, 1)).reshape(T, 8, 64)


def rsqrt_cols(p, S, i, o, n, mul, eps):
    ts(p, "dve", S[:, o:o + n], S[:, i:i + n], float(mul), ALU.mult, [S], [S], s2=float(eps), op1=ALU.add)
    p.op("act", lambda e: e.sqrt(out=S[:, o:o + n], in_=S[:, o:o + n]), reads=[S], writes=[S])
    p.op("dve", lambda e: e.reciprocal(out=S[:, o:o + n], in_=S[:, o:o + n]), reads=[S], writes=[S])


def build_attn(NH, NC, DK, DV, Tq, Tk, causal, scale):
    p = Prog()
    qT = p.din("qT", [NH * NC, DK, Tq])
    kT = p.din("kT", [NH * NC, DK, Tk])
    v = p.din("v", [NH, Tk, DV])
    oT = p.dout("oT", [NH * NC, DV, Tq])
    NKT = Tk // 128
    QB = min(512, Tq)
    if causal:
        msk = p.din("mask", [4, 128, 512])
        M = p.sb("mask", [128, 4, 512])
        for j in range(4):
            p.dma("sp", M[:, j, :], msk[j], writes=[M])
    ones = p.sb("ones", [128, 128])
    p.op("dve", lambda e: e.memset(ones[:], 1.0), writes=[ones])
    QT = [p.sb("qT", [DK, Tq]) for _ in range(2)]
    KT = [p.sb("kT", [DK, Tk]) for _ in range(2)]
    V = [p.sb("v", [128, NKT, DV]) for _ in range(2)]
    PS_S = [p.ps("pss", [128, 512]) for _ in range(2)]
    PS_O = [p.ps("pso", [128, 512]) for _ in range(2)]
    PS_Z = [p.ps("psz", [128, 512]) for _ in range(2)]
    P = [p.sb("p", [128, 512]) for _ in range(3)]
    O = [p.sb("o", [128, 512]) for _ in range(2)]
    RZ = [p.sb("rz", [128, 512]) for _ in range(2)]
    ihc = 0
    iq = 0
    ik = 0
    for h in range(NH):
        Vh = V[h % 2]
        p.dma("sp", Vh[:], v.t[h].rearrange("(kt p) d -> p kt d", p=128), writes=[Vh])
        for c in range(NC):
            hc = h * NC + c
            Q, K = QT[ihc % 2], KT[ihc % 2]
            ihc += 1
            p.dma("act", Q[:], qT[hc], writes=[Q])
            p.dma("sp", K[:], kT[hc], writes=[K])
            for qb in range(Tq // QB):
                q0 = qb * QB
                nkt = min(NKT, (q0 + QB) // 128) if causal else NKT
                pso, psz = PS_O[iq % 2], PS_Z[iq % 2]
                ob, rz = O[iq % 2], RZ[iq % 2]
                iq += 1

                def oz(kt, Pt):
                    p.op("pe", lambda e: e.matmul(pso[0:DV, 0:QB], Vh[:, kt, :], Pt[:, 0:QB], start=(kt == 0), stop=(kt == nkt - 1)),
                         reads=[Vh, Pt], writes=[pso])
                    p.op("pe", lambda e: e.matmul(psz[:, 0:QB], ones[:], Pt[:, 0:QB], start=(kt == 0), stop=(kt == nkt - 1)),
                         reads=[ones, Pt], writes=[psz])
                prev = None
                for kt in range(nkt):
                    pss = PS_S[ik % 2]
                    Pt = P[ik % 3]
                    ik += 1
                    p.op("pe", lambda e: e.matmul(pss[:, 0:QB], K[:, kt * 128:(kt + 1) * 128], Q[:, q0:q0 + QB], start=True, stop=True),
                         reads=[K, Q], writes=[pss])
                    actf(p, Pt[:, 0:QB], pss[:, 0:QB], AF.Exp, [pss], [Pt], scale=scale)
                    if causal and kt * 128 + 127 > q0:
                        j = kt - q0 // 128
                        tt(p, "pool", Pt[:, 0:QB], Pt[:, 0:QB], M[:, j, 0:QB], ALU.mult, [Pt, M], [Pt])
                    if prev is not None:
                        oz(*prev)
                    prev = (kt, Pt)
                oz(*prev)
                p.op("dve", lambda e: e.reciprocal(out=rz[0:DV, 0:QB], in_=psz[0:DV, 0:QB]), reads=[psz], writes=[rz])
                tt(p, "dve", ob[0:DV, 0:QB], pso[0:DV, 0:QB], rz[0:DV, 0:QB], ALU.mult, [pso, rz], [ob])
                p.dma("pool", oT[hc, :, q0:q0 + QB], ob[0:DV, 0:QB], reads=[ob], writes=[oT])
    return p.finish()


def causal_masks():
    k = np.arange(128)[:, None]
    q = np.arange(512)[None, :]
    return np.stack([(j * 128 + k <= q) for j in range(4)], 0).astype(np.float32)


def build_ew3(T, lam_init, gn_eps, ln_eps, C=1024):
    p = Prog()
    names = ("y", "bon", "g", "o0", "o1")
    ins = {k: p.din(k, [T, C]) for k in names}
    lnxw = load_bcast(p, "lnxw", p.din("lnxw", [1, C]), C)
    lnxb = load_bcast(p, "lnxb", p.din("lnxb", [1, C]), C)
    sw = load_bcast(p, "sw", p.din("sw", [1, C]), C)
    lv = {k: load_bcast(p, k, p.din(k, [1, 64]), 64) for k in ("lq1", "lk1", "lq2", "lk2")}
    mix = p.dout("mix", [T, 2 * C])
    L = p.sb("lam", [128, 8])
    tt(p, "dve", lv["lq1"][:], lv["lq1"][:], lv["lk1"][:], ALU.mult, [lv["lq1"], lv["lk1"]], [lv["lq1"]])
    tt(p, "dve", lv["lq2"][:], lv["lq2"][:], lv["lk2"][:], ALU.mult, [lv["lq2"], lv["lk2"]], [lv["lq2"]])
    rsum(p, "dve", L[:, 0:1], lv["lq1"][:], [lv["lq1"]], [L])
    rsum(p, "dve", L[:, 1:2], lv["lq2"][:], [lv["lq2"]], [L])
    actf(p, L[:, 0:2], L[:, 0:2], AF.Exp, [L], [L])
    tt(p, "dve", L[:, 2:3], L[:, 1:2], L[:, 0:1], ALU.subtract, [L], [L])
    ts(p, "dve", L[:, 3:4], L[:, 2:3], -float(lam_init), ALU.add, [L], [L])
    NB = 2
    mk = lambda nm: [p.sb(nm, [128, C]) for _ in range(NB)]
    B_ = {k: mk(k) for k in names}
    TM = mk("tm")
    ST = [p.sb("st", [128, 64]) for _ in range(NB)]
    v64 = lambda ap: ap.rearrange("p (h j) -> p h j", j=64)
    v128 = lambda ap: ap.rearrange("p (h j) -> p h j", j=128)
    for i, t0 in enumerate(range(0, T, 128)):
        Y, BON, G, O0, O1 = (B_[k][i % NB] for k in names)
        X, S = TM[i % NB], ST[i % NB]
        for j, k in enumerate(names):
            p.dma("sp" if j % 2 else "act", B_[k][i % NB][:], ins[k][t0:t0 + 128, :], writes=[B_[k][i % NB]])
        rsum(p, "dve", S[:, 0:16], v64(Y[:]), [Y], [S])
        ts(p, "dve", S[:, 0:16], S[:, 0:16], 1.0 / 64, ALU.mult, [S], [S])
        tt(p, "dve", v64(Y[:]), v64(Y[:]), S[:, 0:16].unsqueeze(2).to_broadcast([128, 16, 64]), ALU.subtract, [Y, S], [Y])
        tt(p, "pool", X[:], Y[:], Y[:], ALU.mult, [Y], [X])
        rsum(p, "dve", S[:, 16:32], v64(X[:]), [X], [S])
        rsqrt_cols(p, S, 16, 16, 16, 1.0 / 64, gn_eps)
        tt(p, "dve", v64(Y[:]), v64(Y[:]), S[:, 16:32].unsqueeze(2).to_broadcast([128, 16, 64]), ALU.mult, [Y, S], [Y])
        tt(p, "pool", Y[:], Y[:], lnxw[:], ALU.mult, [Y, lnxw], [Y])
        tt(p, "pool", Y[:], Y[:], lnxb[:], ALU.add, [Y, lnxb], [Y])
        tt(p, "dve", Y[:], Y[:], BON[:], ALU.add, [Y, BON], [Y])
        tt(p, "pool", Y[:], Y[:], G[:], ALU.mult, [Y, G], [Y])
        p.dma("pool", mix[t0:t0 + 128, 0:C], Y[:], reads=[Y], writes=[mix])
        p.op("dve", lambda e: e.scalar_tensor_tensor(out=O0[:], in0=O1[:], scalar=L[:, 3:4], in1=O0[:],
                                                     op0=ALU.mult, op1=ALU.add), reads=[O0, O1, L], writes=[O0])
        tt(p, "pool", O1[:], O0[:], O0[:], ALU.mult, [O0], [O1])
        rsum(p, "dve", S[:, 32:40], v128(O1[:]), [O1], [S])
        rsqrt_cols(p, S, 32, 32, 8, 1.0 / 128, ln_eps)
        tt(p, "dve", v128(O0[:]), v128(O0[:]), S[:, 32:40].unsqueeze(2).to_broadcast([128, 8, 128]), ALU.mult, [O0, S], [O0])
        p.op("dve", lambda e: e.scalar_tensor_tensor(out=O0[:], in0=O0[:], scalar=float(1.0 - lam_init), in1=sw[:],
                                                     op0=ALU.mult, op1=ALU.mult), reads=[O0, sw], writes=[O0])
        p.dma("pool", mix[t0:t0 + 128, C:2 * C], O0[:], reads=[O0], writes=[mix])
    return p.finish()


def build_rope(T, G=32):
    p = Prog()
    x = p.din("x", [T, G * 64])
    pos = p.din("pos", [T, 1], I32)
    invf = load_bcast(p, "invf", p.din("invf", [1, 8]), 8)
    o = p.dout("o", [T, G * 64])
    NB = 2
    X = [p.sb("x", [128, G * 64]) for _ in range(NB)]
    PI = [p.sb("pi", [128, 1], I32) for _ in range(NB)]
    PF = [p.sb("pf", [128, 1]) for _ in range(NB)]
    CS = [p.sb("cs", [128, 16]) for _ in range(NB)]
    kf = p.sb("kf", [128, 16])
    ki = p.sb("ki", [128, 16], I32)
    TA = [p.sb("ta", [128, G, 8]) for _ in range(NB)]
    TB = [p.sb("tb", [128, G, 8]) for _ in range(NB)]
    TC = [p.sb("tc", [128, G, 8]) for _ in range(NB)]
    TD = [p.sb("td", [128, G, 8]) for _ in range(NB)]
    PI_ = 3.141592653589793
    for i, t0 in enumerate(range(0, T, 128)):
        x_, pi_, pf_, cs, ta, tb, tc, td = (z[i % NB] for z in (X, PI, PF, CS, TA, TB, TC, TD))
        p.dma("sp", x_[:], x[t0:t0 + 128, :], writes=[x_])
        p.dma("act", pi_[:], pos[t0:t0 + 128, :], writes=[pi_])
        p.op("dve", lambda e: e.tensor_copy(out=pf_[:], in_=pi_[:]), reads=[pi_], writes=[pf_])
        ts(p, "dve", cs[:, 0:8], invf[:], pf_[:, 0:1], ALU.mult, [invf, pf_], [cs])
        ts(p, "dve", cs[:, 8:16], cs[:, 0:8], 0.5 * PI_, ALU.add, [cs], [cs])
        ts(p, "dve", kf[:], cs[:], 1.0 / (2 * PI_), ALU.mult, [cs], [kf])
        p.op("dve", lambda e: e.tensor_copy(out=ki[:], in_=kf[:]), reads=[kf], writes=[ki])
        p.op("dve", lambda e: e.tensor_copy(out=kf[:], in_=ki[:]), reads=[ki], writes=[kf])
        p.op("dve", lambda e: e.scalar_tensor_tensor(out=cs[:], in0=kf[:], scalar=-6.28125, in1=cs[:],
                                                     op0=ALU.mult, op1=ALU.add), reads=[kf, cs], writes=[cs])
        p.op("dve", lambda e: e.scalar_tensor_tensor(out=cs[:], in0=kf[:], scalar=-0.0019353071795864769, in1=cs[:],
                                                     op0=ALU.mult, op1=ALU.add), reads=[kf, cs], writes=[cs])
        ts(p, "dve", kf[:], cs[:], PI_, ALU.is_gt, [cs], [kf])
        p.op("dve", lambda e: e.scalar_tensor_tensor(out=cs[:], in0=kf[:], scalar=-2 * PI_, in1=cs[:],
                                                     op0=ALU.mult, op1=ALU.add), reads=[kf, cs], writes=[cs])
        ts(p, "dve", cs[:], cs[:], PI_, ALU.min, [cs], [cs], s2=-PI_, op1=ALU.max)
        actf(p, cs[:], cs[:], AF.Sin, [cs], [cs])
        xv = x_[:].rearrange("p (g d) -> p g d", d=64)
        t1, t2 = xv[:, :, 0:8], xv[:, :, 8:16]
        sinb = cs[:, 0:8].unsqueeze(1).to_broadcast([128, G, 8])
        cosb = cs[:, 8:16].unsqueeze(1).to_broadcast([128, G, 8])
        tt(p, "dve", ta[:], t1, cosb, ALU.mult, [x_, cs], [ta])
        tt(p, "pool", tb[:], t2, sinb, ALU.mult, [x_, cs], [tb])
        tt(p, "dve", tc[:], t2, cosb, ALU.mult, [x_, cs], [tc])
        tt(p, "pool", td[:], t1, sinb, ALU.mult, [x_, cs], [td])
        tt(p, "dve", t1, ta[:], tb[:], ALU.subtract, [ta, tb], [x_])
        tt(p, "dve", t2, tc[:], td[:], ALU.add, [tc, td], [x_])
        p.dma("pool", o[t0:t0 + 128, :], x_[:], reads=[x_], writes=[o])
    return p.finish()


def build_topk(T, H=8, NK=128, K=16):
    p = Prog()
    s1 = p.din("s1", [T, H * NK])
    s2 = p.din("s2", [T, H * NK])
    iota_d = p.din("iota", [1, K * K])
    ex = p.dout("ex", [T, H * K], I32)
    gt = p.dout("gt", [T, H * K])
    IOTA = load_bcast(p, "iota", iota_d, K * K)
    NB = 2
    S = [[p.sb("s", [128, H * NK]) for _ in range(2)] for _ in range(NB)]
    MV = [p.sb("mv", [128, K]) for _ in range(2)]
    MI = [p.sb("mi", [128, K], U32) for _ in range(2)]
    MF = [p.sb("mf", [128, K]) for _ in range(2)]
    WK = p.sb("wk", [128, NK])
    CAND = p.sb("cand", [128, K * K])
    CE = p.sb("ce", [128, K * K])
    WK2 = p.sb("wk2", [128, K * K])
    FV = p.sb("fv", [128, K])
    FI = p.sb("fi", [128, K], U32)
    FF = p.sb("ff", [128, K])
    EQ = p.sb("eq", [128, K, K * K])
    SM = p.sb("sm", [128, 4])
    E = p.sb("e", [128, K])
    EX = [p.sb("ex", [128, H * K]) for _ in range(NB)]
    EXI = [p.sb("exi", [128, H * K], I32) for _ in range(NB)]
    GT = [p.sb("gt", [128, H * K]) for _ in range(NB)]
    NEG = -1e30

    def top16(src, srcbuf, mv, mi, wk):
        p.op("dve", lambda e: e.max(out=mv[:, 0:8], in_=src), reads=[srcbuf], writes=[mv])
        p.op("dve", lambda e: e.max_index(out=mi[:, 0:8], in_max=mv[:, 0:8], in_values=src), reads=[srcbuf, mv], writes=[mi])
        p.op("dve", lambda e: e.match_replace(out=wk[:], in_to_replace=mv[:, 0:8], in_values=src, imm_value=NEG),
             reads=[srcbuf, mv], writes=[wk])
        p.op("dve", lambda e: e.max(out=mv[:, 8:16], in_=wk[:]), reads=[wk], writes=[mv])
        p.op("dve", lambda e: e.max_index(out=mi[:, 8:16], in_max=mv[:, 8:16], in_values=wk[:]), reads=[wk, mv], writes=[mi])

    for i, t0 in enumerate(range(0, T, 128)):
        S1, S2 = S[i % NB]
        ex_, exi_, gt_ = EX[i % NB], EXI[i % NB], GT[i % NB]
        p.dma("sp", S1[:], s1[t0:t0 + 128, :], writes=[S1])
        p.dma("act", S2[:], s2[t0:t0 + 128, :], writes=[S2])
        for h in range(H):
            for c, Sx in ((0, S1), (1, S2)):
                top16(Sx[:, h * NK:(h + 1) * NK], Sx, MV[c], MI[c], WK)
                p.op("dve", lambda e: e.tensor_copy(out=MF[c][:], in_=MI[c][:]), reads=[MI[c]], writes=[MF[c]])
            c3 = lambda ap: ap.rearrange("p (a b) -> p a b", b=K)
            tt(p, "dve", c3(CAND[:]), MV[0][:].unsqueeze(2).to_broadcast([128, K, K]),
               MV[1][:].unsqueeze(1).to_broadcast([128, K, K]), ALU.add, [MV[0], MV[1]], [CAND])
            ts(p, "dve", MF[0][:], MF[0][:], float(NK), ALU.mult, [MF[0]], [MF[0]])
            tt(p, "dve", c3(CE[:]), MF[0][:].unsqueeze(2).to_broadcast([128, K, K]),
               MF[1][:].unsqueeze(1).to_broadcast([128, K, K]), ALU.add, [MF[0], MF[1]], [CE])
            top16(CAND[:], CAND, FV, FI, WK2)
            p.op("dve", lambda e: e.tensor_copy(out=FF[:], in_=FI[:]), reads=[FI], writes=[FF])
            tt(p, "dve", EQ[:], FF[:].unsqueeze(2).to_broadcast([128, K, K * K]),
               IOTA[:].unsqueeze(1).to_broadcast([128, K, K * K]), ALU.is_equal, [FF, IOTA], [EQ])
            tt(p, "dve", EQ[:], EQ[:], CE[:].unsqueeze(1).to_broadcast([128, K, K * K]), ALU.mult, [EQ, CE], [EQ])
            rsum(p, "dve", ex_[:, h * K:(h + 1) * K], EQ[:], [EQ], [ex_])
            ts(p, "dve", SM[:, 0:1], FV[:, 0:1], -1.0, ALU.mult, [FV], [SM])
            p.op("act", lambda e: e.activation(out=E[:], in_=FV[:], func=AF.Exp, bias=SM[:, 0:1], scale=1.0),
                 reads=[FV, SM], writes=[E])
            rsum(p, "dve", SM[:, 1:2], E[:], [E], [SM])
            p.op("dve", lambda e: e.reciprocal(out=SM[:, 2:3], in_=SM[:, 1:2]), reads=[SM], writes=[SM])
            ts(p, "dve", gt_[:, h * K:(h + 1) * K], E[:], SM[:, 2:3], ALU.mult, [E, SM], [gt_])
        p.op("dve", lambda e: e.tensor_copy(out=exi_[:], in_=ex_[:]), reads=[ex_], writes=[exi_])
        p.dma("pool", ex[t0:t0 + 128, :], exi_[:], reads=[exi_], writes=[ex])
        p.dma("pool", gt[t0:t0 + 128, :], gt_[:], reads=[gt_], writes=[gt])
    return p.finish()


def build_peer(T, D=2048, NE=16384, NS=128):
    p = Prog()
    x = p.din("x", [T, D])
    exT = p.din("exT", [NS, T], I32)
    gtT = p.din("gtT", [NS, T])
    pu = p.din("pu", [NE, D])
    pv = p.din("pv", [NE, D])
    c2d = p.din("c2", [1, 255])
    y = p.dout("y", [T, D])
    C2 = load_bcast(p, "c2", c2d, 255)
    EXT = p.sb("ext", [NS, T], I32)
    GTT = p.sb("gtt", [NS, T])
    p.dma("sp", EXT[:], exT[:, :], writes=[EXT])
    p.dma("act", GTT[:], gtT[:, :], writes=[GTT])
    NB = 3
    UG = [p.sb("ug", [NS, D]) for _ in range(NB)]
    VG = [p.sb("vg", [NS, D]) for _ in range(NB)]
    XB = [p.sb("xb", [NS, D]) for _ in range(NB)]
    JK = [p.sb("junk", [NS, D]) for _ in range(2)]
    HT = [p.sb("ht", [NS, 4]) for _ in range(NB)]
    ACM = [p.sb("acm", [NS, 128]) for _ in range(NB)]
    PSY = [p.ps("psy", [128, 512]) for _ in range(4)]
    YO = [p.sb("yo", [128, 512]) for _ in range(4)]
    for t in range(T):
        tl = t % 128
        ug, vg, xb, ht, acm = (z[t % NB] for z in (UG, VG, XB, HT, ACM))
        p.dma_custom("pool", lambda e: e.indirect_dma_start(
            out=ug[:, :], out_offset=None, in_=pu[:, :],
            in_offset=bass.IndirectOffsetOnAxis(ap=EXT[:, t:t + 1], axis=0)), reads=[EXT, pu], writes=[ug])
        p.dma_custom("pool", lambda e: e.indirect_dma_start(
            out=vg[:, :], out_offset=None, in_=pv[:, :],
            in_offset=bass.IndirectOffsetOnAxis(ap=EXT[:, t:t + 1], axis=0)), reads=[EXT, pv], writes=[vg])
        p.dma("sp" if t % 2 else "act", xb[:], x.t[t].partition_broadcast(NS), reads=[x], writes=[xb])
        jk = JK[t % 2]
        tt(p, "pool" if t % 2 else "dve", jk[:], ug[:], xb[:], ALU.mult, [ug, xb], [jk])
        rsum(p, "dve", ht[:, 0:1], jk[:], [jk], [ht])
        actf(p, ht[:, 1:2], ht[:, 0:1], AF.Gelu, [ht], [ht])
        tt(p, "dve", ht[:, 2:3], ht[:, 1:2], GTT[:, t:t + 1], ALU.mult, [ht, GTT], [ht])
        ts(p, "dve", acm[:], C2[:, 127 - tl:255 - tl], ht[:, 2:3], ALU.mult, [C2, ht], [acm])
        for n in range(4):
            p.op("pe", lambda e: e.matmul(PSY[n][:, :], acm[:], vg[:, n * 512:(n + 1) * 512],
                                          start=(tl == 0), stop=(tl == 127 or t == T - 1)),
                 reads=[acm, vg], writes=[PSY[n]])
        if tl == 127 or t == T - 1:
            t0 = t - tl
            nr = tl + 1
            for n in range(4):
                if n % 2:
                    p.op("act", lambda e: e.activation(out=YO[n][:], in_=PSY[n][:], func=AF.Copy), reads=[PSY[n]], writes=[YO[n]])
                else:
                    p.op("dve", lambda e: e.tensor_copy(out=YO[n][:], in_=PSY[n][:]), reads=[PSY[n]], writes=[YO[n]])
                p.dma("sp", y[t0:t0 + nr, n * 512:(n + 1) * 512], YO[n][0:nr, :], reads=[YO[n]], writes=[y])
    return p.finish()


def st_gemm(p, at_b, at_ap, b_b, b_ap, c_b, c_ap, K, M, N, MB=512, NB=512):
    with p.stage():
        KC = (K + 127) // 128
        MB = min(MB, M)
        NB = min(NB, N)
        a_sb = [p.sb("a", [128, KC, MB]) for _ in range(2)]
        b_sb = [p.sb("b", [128, KC, NB]) for _ in range(2)]
        pss = [p.ps("ps", [128, 512]) for _ in range(4)]
        o_sb = [p.sb("o", [128, NB]) for _ in range(4)]
        ia = ib = io = 0
        for m0 in range(0, M, MB):
            mw = min(MB, M - m0)
            A = a_sb[ia % 2]
            ia += 1
            for kc in range(KC):
                kr = min(128, K - kc * 128)
                p.dma("sp" if kc % 2 else "act", A[0:kr, kc, 0:mw], at_ap[kc * 128:kc * 128 + kr, m0:m0 + mw], reads=[at_b], writes=[A])
            for n0 in range(0, N, NB):
                nw = min(NB, N - n0)
                Bt = b_sb[ib % 2]
                ib += 1
                for kc in range(KC):
                    kr = min(128, K - kc * 128)
                    p.dma("act" if kc % 2 else "sp", Bt[0:kr, kc, 0:nw], b_ap[kc * 128:kc * 128 + kr, n0:n0 + nw], reads=[b_b], writes=[Bt])
                for mt in range(0, mw, 128):
                    mm = min(128, mw - mt)
                    P_ = pss[io % 4]
                    O = o_sb[io % 4]
                    io += 1
                    for kc in range(KC):
                        kr = min(128, K - kc * 128)
                        p.op("pe", lambda e, kc=kc, kr=kr: e.matmul(P_[0:mm, 0:nw], A[0:kr, kc, mt:mt + mm], Bt[0:kr, kc, 0:nw],
                                                                    start=(kc == 0), stop=(kc == KC - 1)),
                             reads=[A, Bt], writes=[P_])
                    if io % 2:
                        p.op("act", lambda e: e.activation(out=O[0:mm, 0:nw], in_=P_[0:mm, 0:nw], func=AF.Copy), reads=[P_], writes=[O])
                    else:
                        p.op("dve", lambda e: e.tensor_copy(out=O[0:mm, 0:nw], in_=P_[0:mm, 0:nw]), reads=[P_], writes=[O])
                    p.dma("pool", c_ap[m0 + mt:m0 + mt + mm, n0:n0 + nw], O[0:mm, 0:nw], reads=[O], writes=[c_b])


def st_transpose(p, s_b, s_ap, d_b, d_ap, T, N, ident, rows=None):
    with p.stage():
        CB = 512
        X = [p.sb("x", [128, CB]) for _ in range(3)]
        PS = [p.ps("ps", [128, 512]) for _ in range(3)]
        O = [p.sb("o", [128, 4, 128]) for _ in range(3)]
        i = 0
        for t0 in range(0, T, 128):
            for n0 in range(0, N, CB):
                nw = min(CB, N - n0)
                x, ps, o = X[i % 3], PS[i % 3], O[i % 3]
                i += 1
                src_rows = s_ap[t0:t0 + 128, :] if rows is None else rows(t0)
                p.dma("sp" if i % 2 else "act", x[:, 0:nw], src_rows[:, n0:n0 + nw], reads=[s_b], writes=[x])
                nb = (nw + 127) // 128
                for j in range(nb):
                    cw = min(128, nw - j * 128)
                    p.op("pe", lambda e, j=j, cw=cw: e.transpose(out=ps[0:cw, j * 128:(j + 1) * 128], in_=x[:, j * 128:j * 128 + cw],
                                                                 identity=ident[:, :]), reads=[x, ident], writes=[ps])
                ov = o[:].rearrange("p j t -> p (j t)")
                if i % 2:
                    p.op("act", lambda e: e.activation(out=ov[:, 0:nb * 128], in_=ps[:, 0:nb * 128], func=AF.Copy), reads=[ps], writes=[o])
                else:
                    p.op("dve", lambda e: e.tensor_copy(out=ov[:, 0:nb * 128], in_=ps[:, 0:nb * 128]), reads=[ps], writes=[o])
                if nw == 512:
                    p.dma("pool", d_ap[n0:n0 + 512, t0:t0 + 128].rearrange("(j p) t -> p j t", p=128), o[:, :, :], reads=[o], writes=[d_b])
                else:
                    for j in range(nb):
                        cw = min(128, nw - j * 128)
                        p.dma("pool", d_ap[n0 + j * 128:n0 + j * 128 + cw, t0:t0 + 128], o[0:cw, j, :], reads=[o], writes=[d_b])


def st_ln(p, x_b, x_ap, y_b, y_ap, w_d, b_d, o_b, o_ap, T, D, alpha, eps):
    with p.stage():
        wb = load_bcast(p, "wb", w_d, D)
        bb = load_bcast(p, "bb", b_d, D)
        NBUF = 2
        xs = [p.sb("x", [128, D]) for _ in range(NBUF)]
        ys = [p.sb("y", [128, D]) for _ in range(NBUF)]
        zs = [p.sb("z", [128, D]) for _ in range(NBUF)]
        st = [p.sb("st", [128, 8]) for _ in range(NBUF)]
        for i, t0 in enumerate(range(0, T, 128)):
            X, Y, Z, S = xs[i % NBUF], ys[i % NBUF], zs[i % NBUF], st[i % NBUF]
            p.dma("sp", X[:], x_ap[t0:t0 + 128, :], reads=[x_b], writes=[X])
            p.dma("act", Y[:], y_ap[t0:t0 + 128, :], reads=[y_b], writes=[Y])
            p.op("dve", lambda e: e.scalar_tensor_tensor(out=Z[:], in0=X[:], scalar=float(alpha), in1=Y[:],
                                                         op0=ALU.mult, op1=ALU.add), reads=[X, Y], writes=[Z])
            ln_rows(p, Z, X, S, wb, bb, D, eps)
            p.dma("pool", o_ap[t0:t0 + 128, :], X[:], reads=[X], writes=[o_b])


def st_ew1(p, P_b, P_ap, mu_d, o_b, o_ap, T, NCOL, C3):
    with p.stage():
        mub = load_bcast(p, "mub", mu_d, NCOL)
        NB = 2
        X = [p.sb("x", [128, NCOL]) for _ in range(NB)]
        XP = [p.sb("xp", [128, NCOL]) for _ in range(NB)]
        for i, t0 in enumerate(range(0, T, 128)):
            a, b = X[i % NB], XP[i % NB]
            p.dma("sp", a[:], P_ap[t0:t0 + 128, 0:NCOL], reads=[P_b], writes=[a])
            if t0 == 0:
                p.op("dve", lambda e: e.memset(b[:], 0.0), writes=[b])
                p.dma("act", b[1:128, :], P_ap[0:127, 0:NCOL], reads=[P_b], writes=[b])
            else:
                p.dma("act", b[:], P_ap[t0 - 1:t0 + 127, 0:NCOL], reads=[P_b], writes=[b])
            tt(p, "dve", b[:], b[:], a[:], ALU.subtract, [a, b], [b])
            tt(p, "pool", b[:], b[:], mub[:], ALU.mult, [b, mub], [b])
            tt(p, "dve", a[:], a[:], b[:], ALU.add, [a, b], [a])
            actf(p, a[:, C3:C3 + 64], a[:, C3:C3 + 64], AF.Tanh, [a], [a])
            actf(p, a[:, C3 + 128:NCOL], a[:, C3 + 128:NCOL], AF.Sigmoid, [a], [a])
            p.dma("pool", o_ap[t0:t0 + 128, :], a[:], reads=[a], writes=[o_b])


def scan_out(p, dst, X, t0, roff):
    for hf in range(2):
        p.dma("pool", dst.t[hf, roff + t0:roff + t0 + 128, :].rearrange("t (g j) -> t g j", j=64),
              X[:].rearrange("p (g hf j) -> p hf g j", hf=2, j=64)[:, hf], reads=[X], writes=[dst])


def st_ew2(p, O1_b, O1_ap, WL, AL, prm_d, outs, T, C, N=64):
    H = C // N
    with p.stage():
        prm = {k: load_bcast(p, k, prm_d[k], C) for k in ("w0", "a0", "kk", "ka", "rk")}
        NB = 2
        mk = lambda nm: [p.sb(nm, [128, C]) for _ in range(NB)]
        Rs, Ks, Vs, WLs, ALs, KKs, T1s, T2s = mk("r"), mk("k"), mk("v"), mk("wl"), mk("al"), mk("kk"), mk("t1"), mk("t2")
        SSs = [p.sb("ss", [128, 2 * H]) for _ in range(NB)]
        v3 = lambda ap: ap.rearrange("p (h j) -> p h j", j=N)
        for i, t0 in enumerate(range(0, T, 128)):
            R, K, V, WLt, ALt, KK, T1, T2, SS = (z[i % NB] for z in (Rs, Ks, Vs, WLs, ALs, KKs, T1s, T2s, SSs))
            p.dma("sp", R[:], O1_ap[t0:t0 + 128, 0:C], reads=[O1_b], writes=[R])
            p.dma("act", K[:], O1_ap[t0:t0 + 128, C:2 * C], reads=[O1_b], writes=[K])
            p.dma("sp", V[:], O1_ap[t0:t0 + 128, 2 * C:3 * C], reads=[O1_b], writes=[V])
            scan_out(p, outs["r"], R, t0, 1)
            p.dma("act", WLt[:], WL[t0:t0 + 128, :], reads=[WL], writes=[WLt])
            p.dma("sp", ALt[:], AL[t0:t0 + 128, :], reads=[AL], writes=[ALt])
            tt(p, "dve", WLt[:], WLt[:], prm["w0"][:], ALU.add, [WLt, prm["w0"]], [WLt])
            actf(p, WLt[:], WLt[:], AF.Sigmoid, [WLt], [WLt])
            actf(p, WLt[:], WLt[:], AF.Exp, [WLt], [WLt], scale=-0.6065306597126334)
            scan_out(p, outs["dec"], WLt, t0, 0)
            tt(p, "pool", ALt[:], ALt[:], prm["a0"][:], ALU.add, [ALt, prm["a0"]], [ALt])
            actf(p, ALt[:], ALt[:], AF.Sigmoid, [ALt], [ALt])
            tt(p, "dve", KK[:], K[:], prm["kk"][:], ALU.mult, [K, prm["kk"]], [KK])
            tt(p, "pool", T1[:], KK[:], KK[:], ALU.mult, [KK], [T1])
            rsum(p, "dve", SS[:, 0:H], v3(T1[:]), [T1], [SS])
            p.op("act", lambda e: e.sqrt(out=SS[:, 0:H], in_=SS[:, 0:H]), reads=[SS], writes=[SS])
            ts(p, "dve", SS[:, 0:H], SS[:, 0:H], 1e-12, ALU.max, [SS], [SS])
            p.op("dve", lambda e: e.reciprocal(out=SS[:, 0:H], in_=SS[:, 0:H]), reads=[SS], writes=[SS])
            tt(p, "dve", v3(KK[:]), v3(KK[:]), SS[:, 0:H].unsqueeze(2).to_broadcast([128, H, N]), ALU.mult, [KK, SS], [KK])
            p.op("dve", lambda e: e.scalar_tensor_tensor(out=T1[:], in0=ALt[:], scalar=-1.0, in1=prm["ka"][:],
                                                         op0=ALU.add, op1=ALU.mult), reads=[ALt, prm["ka"]], writes=[T1])
            ts(p, "pool", T1[:], T1[:], 1.0, ALU.add, [T1], [T1])
            tt(p, "dve", T1[:], T1[:], K[:], ALU.mult, [T1, K], [T1])
            scan_out(p, outs["km"], T1, t0, 0)
            tt(p, "pool", T2[:], KK[:], ALt[:], ALU.mult, [KK, ALt], [T2])
            scan_out(p, outs["b"], T2, t0, 0)
            ts(p, "dve", KK[:], KK[:], -1.0, ALU.mult, [KK], [KK])
            scan_out(p, outs["a"], KK, t0, 0)
            tt(p, "dve", R[:], R[:], T1[:], ALU.mult, [R, T1], [R])
            tt(p, "pool", R[:], R[:], prm["rk"][:], ALU.mult, [R, prm["rk"]], [R])
            rsum(p, "dve", SS[:, H:2 * H], v3(R[:]), [R], [SS])
            tt(p, "dve", v3(V[:]), v3(V[:]), SS[:, H:2 * H].unsqueeze(2).to_broadcast([128, H, N]), ALU.mult, [V, SS], [V])
            p.dma("pool", outs["bon"][t0:t0 + 128, :], V[:], reads=[V], writes=[outs["bon"]])


def st_rope(p, x_b, x_ap, pos_d, invf_d, o_b, T, G):
    with p.stage():
        invf = load_bcast(p, "invf", invf_d, 8)
        NB = 2
        X = [p.sb("x", [128, G * 64]) for _ in range(NB)]
        PI = [p.sb("pi", [128, 1], I32) for _ in range(NB)]
        PF = [p.sb("pf", [128, 1]) for _ in range(NB)]
        CS = [p.sb("cs", [128, 16]) for _ in range(NB)]
        kf = p.sb("kf", [128, 16])
        ki = p.sb("ki", [128, 16], I32)
        TA = [p.sb("ta", [128, G, 8]) for _ in range(NB)]
        TB = [p.sb("tb", [128, G, 8]) for _ in range(NB)]
        TC = [p.sb("tc", [128, G, 8]) for _ in range(NB)]
        TD = [p.sb("td", [128, G, 8]) for _ in range(NB)]
        PI_ = 3.141592653589793
        for i, t0 in enumerate(range(0, T, 128)):
            x_, pi_, pf_, cs, ta, tb, tc, td = (z[i % NB] for z in (X, PI, PF, CS, TA, TB, TC, TD))
            p.dma("sp", x_[:], x_ap[t0:t0 + 128, :], reads=[x_b], writes=[x_])
            p.dma("act", pi_[:], pos_d[t0:t0 + 128, :], writes=[pi_])
            p.op("dve", lambda e: e.tensor_copy(out=pf_[:], in_=pi_[:]), reads=[pi_], writes=[pf_])
            ts(p, "dve", cs[:, 0:8], invf[:], pf_[:, 0:1], ALU.mult, [invf, pf_], [cs])
            ts(p, "dve", cs[:, 8:16], cs[:, 0:8], 0.5 * PI_, ALU.add, [cs], [cs])
            ts(p, "dve", kf[:], cs[:], 1.0 / (2 * PI_), ALU.mult, [cs], [kf])
            p.op("dve", lambda e: e.tensor_copy(out=ki[:], in_=kf[:]), reads=[kf], writes=[ki])
            p.op("dve", lambda e: e.tensor_copy(out=kf[:], in_=ki[:]), reads=[ki], writes=[kf])
            p.op("dve", lambda e: e.scalar_tensor_tensor(out=cs[:], in0=kf[:], scalar=-6.28125, in1=cs[:],
                                                         op0=ALU.mult, op1=ALU.add), reads=[kf, cs], writes=[cs])
            p.op("dve", lambda e: e.scalar_tensor_tensor(out=cs[:], in0=kf[:], scalar=-0.0019353071795864769, in1=cs[:],
                                                         op0=ALU.mult, op1=ALU.add), reads=[kf, cs], writes=[cs])
            ts(p, "dve", kf[:], cs[:], PI_, ALU.is_gt, [cs], [kf])
            p.op("dve", lambda e: e.scalar_tensor_tensor(out=cs[:], in0=kf[:], scalar=-2 * PI_, in1=cs[:],
                                                         op0=ALU.mult, op1=ALU.add), reads=[kf, cs], writes=[cs])
            ts(p, "dve", cs[:], cs[:], PI_, ALU.min, [cs], [cs], s2=-PI_, op1=ALU.max)
            actf(p, cs[:], cs[:], AF.Sin, [cs], [cs])
            xv = x_[:].rearrange("p (g d) -> p g d", d=64)
            t1, t2 = xv[:, :, 0:8], xv[:, :, 8:16]
            sinb = cs[:, 0:8].unsqueeze(1).to_broadcast([128, G, 8])
            cosb = cs[:, 8:16].unsqueeze(1).to_broadcast([128, G, 8])
            tt(p, "dve", ta[:], t1, cosb, ALU.mult, [x_, cs], [ta])
            tt(p, "pool", tb[:], t2, sinb, ALU.mult, [x_, cs], [tb])
            tt(p, "dve", tc[:], t2, cosb, ALU.mult, [x_, cs], [tc])
            tt(p, "pool", td[:], t1, sinb, ALU.mult, [x_, cs], [td])
            tt(p, "dve", t1, ta[:], tb[:], ALU.subtract, [ta, tb], [x_])
            tt(p, "dve", t2, tc[:], td[:], ALU.add, [tc, td], [x_])
            p.dma("pool", o_b[t0:t0 + 128, :], x_[:], reads=[x_], writes=[o_b])


def st_scan(p, X2, VT, YT, T, Tt=8):
    if True:
        NB = 2
        AR = [p.sb("ar", [128, Tt, 2, 256]) for _ in range(NB)]
        Wt = [p.sb("w", [128, Tt * 256]) for _ in range(NB)]
        Bt = [p.sb("b", [128, Tt * 256]) for _ in range(NB)]
        Kt = [p.sb("k", [128, Tt * 256]) for _ in range(NB)]
        Vt = [p.sb("v", [128, 4, Tt]) for _ in range(NB)]
        KV = [p.sb("kv", [128, Tt * 256]) for _ in range(NB)]
        RS = [p.sb("rs", [128, 2, 4, Tt]) for _ in range(NB)]
        S = [p.sb("s", [128, 256]) for _ in range(2)]
        S2 = [p.sb("s2", [128, 256]) for _ in range(2)]
        TMP = [p.sb("tmp", [128, 2, 256]) for _ in range(2)]
        TMP2 = p.sb("tmp2", [128, 256])
        p.op("dve", lambda e: e.memset(S[0][:], 0.0), writes=[S[0]])
        v3 = lambda ap: ap.rearrange("p (g j) -> p g j", j=64)
        step = 0
        qn = 0
        for it, t0 in enumerate(range(0, T, Tt)):
            sl = it % NB
            ar, W, B, K, rs = AR[sl], Wt[sl], Bt[sl], Kt[sl], RS[sl]
            for half in range(2):
                ps_ = slice(half * 64, (half + 1) * 64)
                for (dst, src, buf) in ((ar[ps_, :, 0, :], X2["a"].t[half, t0:t0 + Tt, :], ar),
                                        (ar[ps_, :, 1, :], X2["r"].t[half, t0:t0 + Tt, :], ar),
                                        (W[ps_, :].rearrange("p (t c) -> p t c", c=256), X2["dec"].t[half, t0:t0 + Tt, :], W),
                                        (B[ps_, :].rearrange("p (t c) -> p t c", c=256), X2["b"].t[half, t0:t0 + Tt, :], B),
                                        (K[ps_, :].rearrange("p (t c) -> p t c", c=256), X2["km"].t[half, t0:t0 + Tt, :], K)):
                    qn += 1
                    srcb = X2["a"] if buf is ar and dst is not None else None
                    p.dma("sp", dst, src.partition_broadcast(64),
                          reads=[X2["a"], X2["r"], X2["dec"], X2["b"], X2["km"]], writes=[buf])
            for g in range(4):
                p.dma("sp", Vt[sl][:, g, :], VT[g * 128:(g + 1) * 128, t0:t0 + Tt], reads=[VT], writes=[Vt[sl]])
            tt(p, "pool", KV[sl][:].rearrange("p (t g j) -> p t g j", g=4, j=64), K[:].rearrange("p (t g j) -> p t g j", g=4, j=64),
               Vt[sl][:].rearrange("p g t -> p t g").unsqueeze(3).to_broadcast([128, Tt, 4, 64]), ALU.mult, [K, Vt[sl]], [KV[sl]])
            for q in range(Tt):
                cs = slice(q * 256, (q + 1) * 256)
                Sc, Sn = S[step % 2], S[(step + 1) % 2]
                s2, tmp = S2[step % 2], TMP[step % 2]
                tt(p, "pool", s2[:], Sc[:], W[:, cs], ALU.mult, [Sc, W], [s2])
                tt(p, "pool", s2[:], s2[:], KV[sl][:, cs], ALU.add, [s2, KV[sl]], [s2])
                tt(p, "dve", tmp[:], Sc[:].unsqueeze(1).to_broadcast([128, 2, 256]), ar[:, q, :, :], ALU.mult, [Sc, ar], [tmp])
                rsum(p, "dve", rs[:, :, :, q].rearrange("p a g -> p (a g)"), tmp[:].rearrange("p a (g j) -> p (a g) j", j=64), [tmp], [rs])
                tt(p, "dve", v3(TMP2[:]), v3(B[:, cs]), rs[:, 0, :, q].unsqueeze(2).to_broadcast([128, 4, 64]), ALU.mult, [B, rs], [TMP2])
                tt(p, "dve", Sn[:], s2[:], TMP2[:], ALU.add, [s2, TMP2], [Sn])
                step += 1
            qa = 1 if t0 == 0 else 0
            for g in range(4):
                p.dma("pool", YT[g * 128:(g + 1) * 128, t0 - 1 + qa:t0 + Tt - 1], rs[:, 1, g, qa:Tt], reads=[rs], writes=[YT])
        Sc = S[step % 2]
        arL = AR[0]
        for half in range(2):
            p.dma("sp", arL[half * 64:(half + 1) * 64, 0, 0, :], X2["r"].t[half, T, :].partition_broadcast(64), reads=[X2["r"]], writes=[arL])
        tmp = TMP[0]
        tt(p, "dve", tmp[:, 0, :], Sc[:], arL[:, 0, 0, :], ALU.mult, [Sc, arL], [tmp])
        rsum(p, "dve", RS[0][:, 0, :, 0], v3(tmp[:, 0, :]), [tmp], [RS[0]])
        for g in range(4):
            p.dma("sp", YT[g * 128:(g + 1) * 128, T - 1:T], RS[0][:, 0, g, 0:1], reads=[RS[0]], writes=[YT], allow_slow_non_contiguous=True)


def st_attn(p, q_b, qT, k_b, kT, v_b, v_ap, o_b, oT, mask_d, NH, NC, DK, DV, Tq, Tk, causal, scale):
    with p.stage():
        NKT = Tk // 128
        QB = min(512, Tq)
        if causal:
            M = p.sb("mask", [128, 4, 512])
            for j in range(4):
                p.dma("sp", M[:, j, :], mask_d[j], writes=[M])
        ones = p.sb("ones", [128, 128])
        p.op("dve", lambda e: e.memset(ones[:], 1.0), writes=[ones])
        QT = [p.sb("qT", [DK, Tq]) for _ in range(2)]
        KT = [p.sb("kT", [DK, Tk]) for _ in range(2)]
        V = [p.sb("v", [128, NKT, DV]) for _ in range(2)]
        PS_S = [p.ps("pss", [128, 512]) for _ in range(2)]
        PS_O = [p.ps("pso", [128, 512]) for _ in range(2)]
        PS_Z = [p.ps("psz", [128, 512]) for _ in range(2)]
        P = [p.sb("p", [128, 512]) for _ in range(3)]
        O = [p.sb("o", [128, 512]) for _ in range(2)]
        RZ = [p.sb("rz", [128, 512]) for _ in range(2)]
        ihc = iq = ik = 0
        for h in range(NH):
            Vh = V[h % 2]
            p.dma("sp", Vh[:], v_ap(h).rearrange("(kt p) d -> p kt d", p=128), reads=[v_b], writes=[Vh])
            for c in range(NC):
                hc = h * NC + c
                Q, K = QT[ihc % 2], KT[ihc % 2]
                ihc += 1
                p.dma("act", Q[:], qT(hc), reads=[q_b], writes=[Q])
                p.dma("sp", K[:], kT(hc), reads=[k_b], writes=[K])
                for qb in range(Tq // QB):
                    q0 = qb * QB
                    nkt = min(NKT, (q0 + QB) // 128) if causal else NKT
                    pso, psz = PS_O[iq % 2], PS_Z[iq % 2]
                    ob, rz = O[iq % 2], RZ[iq % 2]
                    iq += 1

                    def oz(kt, Pt):
                        p.op("pe", lambda e: e.matmul(pso[0:DV, 0:QB], Vh[:, kt, :], Pt[:, 0:QB], start=(kt == 0), stop=(kt == nkt - 1)),
                             reads=[Vh, Pt], writes=[pso])
                        p.op("pe", lambda e: e.matmul(psz[:, 0:QB], ones[:], Pt[:, 0:QB], start=(kt == 0), stop=(kt == nkt - 1)),
                             reads=[ones, Pt], writes=[psz])
                    prev = None
                    for kt in range(nkt):
                        pss = PS_S[ik % 2]
                        Pt = P[ik % 3]
                        ik += 1
                        p.op("pe", lambda e: e.matmul(pss[:, 0:QB], K[:, kt * 128:(kt + 1) * 128], Q[:, q0:q0 + QB], start=True, stop=True),
                             reads=[K, Q], writes=[pss])
                        actf(p, Pt[:, 0:QB], pss[:, 0:QB], AF.Exp, [pss], [Pt], scale=scale)
                        if causal and kt * 128 + 127 > q0:
                            j = kt - q0 // 128
                            tt(p, "pool", Pt[:, 0:QB], Pt[:, 0:QB], M[:, j, 0:QB], ALU.mult, [Pt, M], [Pt])
                        if prev is not None:
                            oz(*prev)
                        prev = (kt, Pt)
                    oz(*prev)
                    p.op("dve", lambda e: e.reciprocal(out=rz[0:DV, 0:QB], in_=psz[0:DV, 0:QB]), reads=[psz], writes=[rz])
                    tt(p, "dve", ob[0:DV, 0:QB], pso[0:DV, 0:QB], rz[0:DV, 0:QB], ALU.mult, [pso, rz], [ob])
                    p.dma("pool", oT(hc)[:, q0:q0 + QB], ob[0:DV, 0:QB], reads=[ob], writes=[o_b])


def st_ew3(p, Y, BON, G, O01, prm_d, lamc_d, mix, T, C, gn_eps, ln_eps, Z01=None):
    HR = C // 64
    HD = C // 128
    with p.stage():
        lnxw = load_bcast(p, "lnxw", prm_d["lnxw"], C)
        lnxb = load_bcast(p, "lnxb", prm_d["lnxb"], C)
        sw = load_bcast(p, "sw", prm_d["sw"], C)
        lv = {k: load_bcast(p, k, prm_d[k], 64) for k in ("lq1", "lk1", "lq2", "lk2")}
        lc = load_bcast(p, "lamc", lamc_d, 2)
        L = p.sb("lam", [128, 8])
        tt(p, "dve", lv["lq1"][:], lv["lq1"][:], lv["lk1"][:], ALU.mult, [lv["lq1"], lv["lk1"]], [lv["lq1"]])
        tt(p, "dve", lv["lq2"][:], lv["lq2"][:], lv["lk2"][:], ALU.mult, [lv["lq2"], lv["lk2"]], [lv["lq2"]])
        rsum(p, "dve", L[:, 0:1], lv["lq1"][:], [lv["lq1"]], [L])
        rsum(p, "dve", L[:, 1:2], lv["lq2"][:], [lv["lq2"]], [L])
        actf(p, L[:, 0:2], L[:, 0:2], AF.Exp, [L], [L])
        tt(p, "dve", L[:, 2:3], L[:, 1:2], L[:, 0:1], ALU.subtract, [L], [L])
        tt(p, "dve", L[:, 3:4], L[:, 2:3], lc[:, 0:1], ALU.subtract, [L, lc], [L])
        NB = 2
        mk = lambda nm: [p.sb(nm, [128, C]) for _ in range(NB)]
        Ys, Bs, Gs, O0s, O1s, TM = mk("y"), mk("bon"), mk("g"), mk("o0"), mk("o1"), mk("tm")
        ST = [p.sb("st", [128, 64]) for _ in range(NB)]
        v64 = lambda ap: ap.rearrange("p (h j) -> p h j", j=64)
        v128 = lambda ap: ap.rearrange("p (h j) -> p h j", j=128)
        for i, t0 in enumerate(range(0, T, 128)):
            Yt, BONt, Gt, O0, O1, X, S = (z[i % NB] for z in (Ys, Bs, Gs, O0s, O1s, TM, ST))
            p.dma("sp", Yt[:], Y[t0:t0 + 128, :], reads=[Y], writes=[Yt])
            p.dma("act", BONt[:], BON[t0:t0 + 128, :], reads=[BON], writes=[BONt])
            p.dma("sp", Gt[:], G[t0:t0 + 128, :], reads=[G], writes=[Gt])
            o4 = O01[t0:t0 + 128, :].rearrange("t (h c d) -> t h c d", c=2, d=128)
            p.dma("act", v128(O0[:]), o4[:, :, 0, :], reads=[O01], writes=[O0])
            p.dma("sp", v128(O1[:]), o4[:, :, 1, :], reads=[O01], writes=[O1])
            if Z01 is not None:
                z4 = Z01[t0:t0 + 128, :].rearrange("t (h c d) -> t h c d", c=2, d=128)
                p.dma("act", v128(X[:]), z4[:, :, 0, :], reads=[Z01], writes=[X])
                p.op("dve", lambda e: e.reciprocal(out=X[:], in_=X[:]), reads=[X], writes=[X])
                tt(p, "dve", O0[:], O0[:], X[:], ALU.mult, [O0, X], [O0])
                p.dma("act", v128(X[:]), z4[:, :, 1, :], reads=[Z01], writes=[X])
                p.op("dve", lambda e: e.reciprocal(out=X[:], in_=X[:]), reads=[X], writes=[X])
                tt(p, "dve", O1[:], O1[:], X[:], ALU.mult, [O1, X], [O1])
            rsum(p, "dve", S[:, 0:HR], v64(Yt[:]), [Yt], [S])
            ts(p, "dve", S[:, 0:HR], S[:, 0:HR], 1.0 / 64, ALU.mult, [S], [S])
            tt(p, "dve", v64(Yt[:]), v64(Yt[:]), S[:, 0:HR].unsqueeze(2).to_broadcast([128, HR, 64]), ALU.subtract, [Yt, S], [Yt])
            tt(p, "pool", X[:], Yt[:], Yt[:], ALU.mult, [Yt], [X])
            rsum(p, "dve", S[:, 16:16 + HR], v64(X[:]), [X], [S])
            rsqrt_cols(p, S, 16, 16, HR, 1.0 / 64, gn_eps)
            tt(p, "dve", v64(Yt[:]), v64(Yt[:]), S[:, 16:16 + HR].unsqueeze(2).to_broadcast([128, HR, 64]), ALU.mult, [Yt, S], [Yt])
            tt(p, "pool", Yt[:], Yt[:], lnxw[:], ALU.mult, [Yt, lnxw], [Yt])
            tt(p, "pool", Yt[:], Yt[:], lnxb[:], ALU.add, [Yt, lnxb], [Yt])
            tt(p, "dve", Yt[:], Yt[:], BONt[:], ALU.add, [Yt, BONt], [Yt])
            tt(p, "pool", Yt[:], Yt[:], Gt[:], ALU.mult, [Yt, Gt], [Yt])
            p.dma("pool", mix[t0:t0 + 128, 0:C], Yt[:], reads=[Yt], writes=[mix])
            p.op("dve", lambda e: e.scalar_tensor_tensor(out=O0[:], in0=O1[:], scalar=L[:, 3:4], in1=O0[:],
                                                         op0=ALU.mult, op1=ALU.add), reads=[O0, O1, L], writes=[O0])
            tt(p, "pool", O1[:], O0[:], O0[:], ALU.mult, [O0], [O1])
            rsum(p, "dve", S[:, 32:32 + HD], v128(O1[:]), [O1], [S])
            rsqrt_cols(p, S, 32, 32, HD, 1.0 / 128, ln_eps)
            tt(p, "dve", v128(O0[:]), v128(O0[:]), S[:, 32:32 + HD].unsqueeze(2).to_broadcast([128, HD, 128]), ALU.mult, [O0, S], [O0])
            p.op("dve", lambda e: e.scalar_tensor_tensor(out=O0[:], in0=O0[:], scalar=lc[:, 1:2], in1=sw[:],
                                                         op0=ALU.mult, op1=ALU.mult), reads=[O0, sw, lc], writes=[O0])
            p.dma("pool", mix[t0:t0 + 128, C:2 * C], O0[:], reads=[O0], writes=[mix])


def st_topk(p, S1, S2, iota_d, EXF, GT, T, H=8, NK=128, K=16):
    with p.stage():
        IOTA = load_bcast(p, "iota", iota_d, K * K)
        NB = 2
        S = [[p.sb("s", [128, H * NK]) for _ in range(2)] for _ in range(NB)]
        MV = [p.sb("mv", [128, K]) for _ in range(2)]
        MI = [p.sb("mi", [128, K], U32) for _ in range(2)]
        MF = [p.sb("mf", [128, K]) for _ in range(2)]
        WK = p.sb("wk", [128, NK])
        CAND = p.sb("cand", [128, K * K])
        CE = p.sb("ce", [128, K * K])
        WK2 = p.sb("wk2", [128, K * K])
        FV = p.sb("fv", [128, K])
        FI = p.sb("fi", [128, K], U32)
        FF = p.sb("ff", [128, K])
        OH = p.sb("oh", [128, K, K])
        T2 = p.sb("t2", [128, K, K])
        IX = p.sb("ix", [128, 4 * K])
        LO16 = p.sb("lo16", [128, 2 * K])
        ts(p, "dve", LO16[:, 0:K], IOTA[:, 0:K], 16.0, ALU.mult, [IOTA], [LO16])
        ts(p, "dve", LO16[:, K:2 * K], LO16[:, 0:K], 16.0, ALU.add, [LO16], [LO16])
        SM = p.sb("sm", [128, 4])
        E = p.sb("e", [128, K])
        EX = [p.sb("ex", [128, H * K]) for _ in range(NB)]
        GTt = [p.sb("gt", [128, H * K]) for _ in range(NB)]
        NEG = -1e30

        def top16(src, srcbuf, mv, mi, wk):
            p.op("dve", lambda e: e.max(out=mv[:, 0:8], in_=src), reads=[srcbuf], writes=[mv])
            p.op("dve", lambda e: e.max_index(out=mi[:, 0:8], in_max=mv[:, 0:8], in_values=src), reads=[srcbuf, mv], writes=[mi])
            p.op("dve", lambda e: e.match_replace(out=wk[:], in_to_replace=mv[:, 0:8], in_values=src, imm_value=NEG),
                 reads=[srcbuf, mv], writes=[wk])
            p.op("dve", lambda e: e.max(out=mv[:, 8:16], in_=wk[:]), reads=[wk], writes=[mv])
            p.op("dve", lambda e: e.max_index(out=mi[:, 8:16], in_max=mv[:, 8:16], in_values=wk[:]), reads=[wk, mv], writes=[mi])

        for i, t0 in enumerate(range(0, T, 128)):
            S1t, S2t = S[i % NB]
            ex_, gt_ = EX[i % NB], GTt[i % NB]
            p.dma("sp", S1t[:], S1[t0:t0 + 128, :], reads=[S1], writes=[S1t])
            p.dma("act", S2t[:], S2[t0:t0 + 128, :], reads=[S2], writes=[S2t])
            for h in range(H):
                for c, Sx in ((0, S1t), (1, S2t)):
                    top16(Sx[:, h * NK:(h + 1) * NK], Sx, MV[c], MI[c], WK)
                    p.op("dve", lambda e: e.tensor_copy(out=MF[c][:], in_=MI[c][:]), reads=[MI[c]], writes=[MF[c]])
                c3 = lambda ap: ap.rearrange("p (a b) -> p a b", b=K)
                tt(p, "dve", c3(CAND[:]), MV[0][:].unsqueeze(2).to_broadcast([128, K, K]),
                   MV[1][:].unsqueeze(1).to_broadcast([128, K, K]), ALU.add, [MV[0], MV[1]], [CAND])
                ts(p, "dve", MF[0][:], MF[0][:], float(NK), ALU.mult, [MF[0]], [MF[0]])
                top16(CAND[:], CAND, FV, FI, WK2)
                p.op("dve", lambda e: e.tensor_copy(out=FF[:], in_=FI[:]), reads=[FI], writes=[FF])
                colv = lambda t_: t_[:].unsqueeze(2).to_broadcast([128, K, K])
                rowv = lambda ap: ap.unsqueeze(1).to_broadcast([128, K, K])
                tt(p, "dve", OH[:], colv(FF), rowv(LO16[:, 0:K]), ALU.is_ge, [FF, LO16], [OH])
                tt(p, "dve", T2[:], colv(FF), rowv(LO16[:, K:2 * K]), ALU.is_lt, [FF, LO16], [T2])
                tt(p, "dve", OH[:], OH[:], T2[:], ALU.mult, [OH, T2], [OH])
                tt(p, "dve", T2[:], OH[:], rowv(MF[0][:]), ALU.mult, [OH, MF[0]], [T2])
                rsum(p, "dve", IX[:, 0:K], T2[:], [T2], [IX])
                tt(p, "dve", T2[:], OH[:], rowv(LO16[:, 0:K]), ALU.mult, [OH, LO16], [T2])
                rsum(p, "dve", IX[:, K:2 * K], T2[:], [T2], [IX])
                tt(p, "dve", IX[:, 2 * K:3 * K], FF[:], IX[:, K:2 * K], ALU.subtract, [FF, IX], [IX])
                tt(p, "dve", OH[:], IX[:, 2 * K:3 * K].unsqueeze(2).to_broadcast([128, K, K]), rowv(IOTA[:, 0:K]), ALU.is_equal, [IX, IOTA], [OH])
                tt(p, "dve", T2[:], OH[:], rowv(MF[1][:]), ALU.mult, [OH, MF[1]], [T2])
                rsum(p, "dve", IX[:, 3 * K:4 * K], T2[:], [T2], [IX])
                tt(p, "dve", ex_[:, h * K:(h + 1) * K], IX[:, 0:K], IX[:, 3 * K:4 * K], ALU.add, [IX], [ex_])
                ts(p, "dve", SM[:, 0:1], FV[:, 0:1], -1.0, ALU.mult, [FV], [SM])
                p.op("act", lambda e: e.activation(out=E[:], in_=FV[:], func=AF.Exp, bias=SM[:, 0:1], scale=1.0),
                     reads=[FV, SM], writes=[E])
                rsum(p, "dve", SM[:, 1:2], E[:], [E], [SM])
                p.op("dve", lambda e: e.reciprocal(out=SM[:, 2:3], in_=SM[:, 1:2]), reads=[SM], writes=[SM])
                ts(p, "dve", gt_[:, h * K:(h + 1) * K], E[:], SM[:, 2:3], ALU.mult, [E, SM], [gt_])
            p.dma("pool", EXF[t0:t0 + 128, :], ex_[:], reads=[ex_], writes=[EXF])
            p.dma("pool", GT[t0:t0 + 128, :], gt_[:], reads=[gt_], writes=[GT])


def st_peer(p, x_b, x_ap, EXFT, GTT_d, pu, pv, c2_d, y_b, T, D=2048, NS=128):
    with p.stage():
        C2 = load_bcast(p, "c2", c2_d, 255)
        EXF = p.sb("exf", [NS, T])
        EXT = p.sb("ext", [NS, T], I32)
        GTT = p.sb("gtt", [NS, T])
        p.dma("sp", EXF[:], EXFT[:, :], reads=[EXFT], writes=[EXF])
        p.dma("act", GTT[:], GTT_d[:, :], reads=[GTT_d], writes=[GTT])
        p.op("dve", lambda e: e.tensor_copy(out=EXT[:], in_=EXF[:]), reads=[EXF], writes=[EXT])
        NB = 3
        UG = [p.sb("ug", [NS, D]) for _ in range(NB)]
        VG = [p.sb("vg", [NS, D]) for _ in range(NB)]
        XB = [p.sb("xb", [NS, D]) for _ in range(NB)]
        JK = [p.sb("junk", [NS, D]) for _ in range(2)]
        HT = [p.sb("ht", [NS, 4]) for _ in range(NB)]
        ACM = [p.sb("acm", [NS, 128]) for _ in range(NB)]
        PSY = [p.ps("psy", [128, 512]) for _ in range(4)]
        YO = [p.sb("yo", [128, 512]) for _ in range(4)]
        for t in range(T):
            tl = t % 128
            ug, vg, xb, ht, acm = (z[t % NB] for z in (UG, VG, XB, HT, ACM))
            p.dma_custom("pool", lambda e: e.indirect_dma_start(
                out=ug[:, :], out_offset=None, in_=pu[:, :],
                in_offset=bass.IndirectOffsetOnAxis(ap=EXT[:, t:t + 1], axis=0)), reads=[EXT, pu], writes=[ug])
            p.dma_custom("pool", lambda e: e.indirect_dma_start(
                out=vg[:, :], out_offset=None, in_=pv[:, :],
                in_offset=bass.IndirectOffsetOnAxis(ap=EXT[:, t:t + 1], axis=0)), reads=[EXT, pv], writes=[vg])
            p.dma("sp" if t % 2 else "act", xb[:], x_ap[t].partition_broadcast(NS), reads=[x_b], writes=[xb])
            jk = JK[t % 2]
            tt(p, "pool" if t % 2 else "dve", jk[:], ug[:], xb[:], ALU.mult, [ug, xb], [jk])
            rsum(p, "dve", ht[:, 0:1], jk[:], [jk], [ht])
            actf(p, ht[:, 1:2], ht[:, 0:1], AF.Gelu, [ht], [ht])
            tt(p, "dve", ht[:, 2:3], ht[:, 1:2], GTT[:, t:t + 1], ALU.mult, [ht, GTT], [ht])
            ts(p, "dve", acm[:], C2[:, 127 - tl:255 - tl], ht[:, 2:3], ALU.mult, [C2, ht], [acm])
            for n in range(4):
                p.op("pe", lambda e: e.matmul(PSY[n][:, :], acm[:], vg[:, n * 512:(n + 1) * 512],
                                              start=(tl == 0), stop=(tl == 127 or t == T - 1)),
                     reads=[acm, vg], writes=[PSY[n]])
            if tl == 127 or t == T - 1:
                t0 = t - tl
                nr = tl + 1
                for n in range(4):
                    if n % 2:
                        p.op("act", lambda e: e.activation(out=YO[n][:], in_=PSY[n][:], func=AF.Copy), reads=[PSY[n]], writes=[YO[n]])
                    else:
                        p.op("dve", lambda e: e.tensor_copy(out=YO[n][:], in_=PSY[n][:]), reads=[PSY[n]], writes=[YO[n]])
                    p.dma("sp", y_b[t0:t0 + nr, n * 512:(n + 1) * 512], YO[n][0:nr, :], reads=[YO[n]], writes=[y_b])


def make_ident(p):
    idd = p.din("ident", [128, 128])
    ident = p.sb("ident", [128, 128])
    p.dma("sp", ident[:], idd[:, :], writes=[ident])
    return ident


def phase_A(p, T, xT, mixout, pfx=""):
    D = 2048
    C = 512
    NR = 3 * C + 288
    NCOL = NR + 3 * C
    I = lambda n, s, dt=F32: p.din(pfx + n, s, dt)
    w_c = I("w_c", [D, NCOL])
    mu = I("mu", [1, NR])
    wup, aup, gup = I("wup", [64, C]), I("aup", [64, C]), I("gup", [160, C])
    prm2 = {k: I(k, [1, C]) for k in ("w0", "a0", "kk", "ka", "rk")}
    prm3 = {k: I(k, [1, C]) for k in ("lnxw", "lnxb", "sw")}
    prm3.update({k: I(k, [1, 64]) for k in ("lq1", "lk1", "lq2", "lk2")})
    lamc = I("lamc", [1, 2])
    return dict(w_c=w_c, mu=mu, wup=wup, aup=aup, gup=gup, prm2=prm2, prm3=prm3, lamc=lamc, NR=NR, NCOL=NCOL, C=C)


def run_phase_A(p, T, xT, mixout, W, cst, tag):
    D, C, NR, NCOL = 2048, W["C"], W["NR"], W["NCOL"]
    Dt = lambda n, s: p.dtmp(tag + n, s)
    P = Dt("P", [T, NCOL])
    st_gemm(p, xT, xT.t, W["w_c"], W["w_c"].t, P, P.t, D, T, NCOL)
    O1 = Dt("O1", [T, NR])
    st_ew1(p, P, P.t, W["mu"], O1, O1.t, T, NR, 3 * C)
    LT = Dt("LT", [288, T])
    st_transpose(p, O1, O1.t[:, 3 * C:NR], LT, LT.t, T, 288, cst["ident"])
    WL, AL, GG = Dt("WL", [T, C]), Dt("AL", [T, C]), Dt("GG", [T, C])
    st_gemm(p, LT, LT.t[0:64, :], W["wup"], W["wup"].t, WL, WL.t, 64, T, C)
    st_gemm(p, LT, LT.t[64:128, :], W["aup"], W["aup"].t, AL, AL.t, 64, T, C)
    st_gemm(p, LT, LT.t[128:288, :], W["gup"], W["gup"].t, GG, GG.t, 160, T, C)
    e2 = {k: Dt(k, [2, T, C // 2]) for k in ("dec", "km", "a", "b")}
    e2["r"] = Dt("r", [2, T + 1, C // 2])
    e2["bon"] = Dt("bon", [T, C])
    st_ew2(p, O1, O1.t, WL, AL, W["prm2"], e2, T, C)
    QKR = Dt("QKR", [T, 2 * C])
    st_rope(p, P, P.t[:, NR:NR + 2 * C], cst["pos"], cst["invf"], QKR, T, 16)
    VT = Dt("VT", [C, T])
    st_transpose(p, O1, O1.t[:, 2 * C:3 * C], VT, VT.t, T, C, cst["ident"])
    YT = Dt("YT", [C, T])
    QKT = Dt("QKT", [2 * C, T])
    st_transpose(p, QKR, QKR.t, QKT, QKT.t, T, 2 * C, cst["ident"])
    OT = Dt("OT", [2 * C, T])
    ZT = Dt("ZT", [2 * C, T])
    with p.stage():
        p.dpool = (24, 48)
        attn_body_pe_act(p, QKT, lambda hc: QKT.t[hc * 64:(hc + 1) * 64, :], QKT, lambda hc: QKT.t[C + hc * 64:C + (hc + 1) * 64, :],
                         P, lambda h: P.t[:, NR + 2 * C + h * 128:NR + 2 * C + (h + 1) * 128],
                         OT, lambda hc: OT.t[hc * 128:(hc + 1) * 128, :], ZT, lambda hc: ZT.t[hc * 128:(hc + 1) * 128, :],
                         cst["maskb"], cst["ones"], cst["ident"], 4, 2, 64, 128, T, 64 ** -0.5)
        p.dpool = (0, 24)
        st_scan(p, e2, VT, YT, T)
        p.dpool = None
    Y = Dt("Y", [T, C])
    st_transpose(p, YT, YT.t, Y, Y.t, C, T, cst["ident"])
    O01 = Dt("O01", [T, 2 * C])
    st_transpose(p, OT, OT.t, O01, O01.t, 2 * C, T, cst["ident"])
    Z01 = Dt("Z01", [T, 2 * C])
    st_transpose(p, ZT, ZT.t, Z01, Z01.t, 2 * C, T, cst["ident"])
    st_ew3(p, Y, e2["bon"], GG, O01, W["prm3"], W["lamc"], mixout, T, C, 64e-5, 1e-5, Z01=Z01)


def build_A(T):
    p = Prog()
    xT = p.din("xT", [2048, T])
    mix = p.dout("mix", [T, 1024])
    cst = dict(pos=p.din("pos", [T, 1], I32), invf=p.din("invf", [1, 8]), maskb=p.din("maskb", [4, 128, 512]),
               ones=p.din("ones", [128, 128]), ident=make_ident(p))
    W = phase_A(p, T, xT, mix)
    run_phase_A(p, T, xT, mix, W, cst, "a_")
    return p.finish()


def phase_B_inputs(p, pfx=""):
    D = 2048
    I = lambda n, s, dt=F32: p.din(pfx + n, s, dt)
    W = dict(w_out=I("w_out", [D, D]), xq=I("xq", [D, 512]), xk=I("xk", [D, 512]), xv=I("xv", [D, 512]), xo=I("xo", [512, D]),
             pq=I("pq", [D, D]), skT=I("skT", [2, 128, 128]), pu=I("pu", [16384, D]), pv=I("pv", [16384, D]))
    for k in ("ln1w", "ln1b", "ln2w", "ln2b", "ln3w", "ln3b"):
        W[k] = I(k, [1, D])
    return W


def run_phase_B(p, Tc, x_b, mixT, memT, out_b, W, cst, alpha, tag):
    D = 2048
    Dt = lambda n, s: p.dtmp(tag + n, s)
    MO = Dt("MO", [Tc, D])
    st_gemm(p, mixT, mixT.t, W["w_out"], W["w_out"].t, MO, MO.t, D, Tc, D)
    X1 = Dt("X1", [Tc, D])
    st_ln(p, x_b, x_b.t, MO, MO.t, W["ln1w"], W["ln1b"], X1, X1.t, Tc, D, alpha, 1e-5)
    X1T = Dt("X1T", [D, Tc])
    st_transpose(p, X1, X1.t, X1T, X1T.t, Tc, D, cst["ident"])
    QXT, KXT, VX = Dt("QXT", [512, Tc]), Dt("KXT", [512, 256]), Dt("VX", [256, 512])
    st_gemm(p, W["xq"], W["xq"].t, X1T, X1T.t, QXT, QXT.t, D, 512, Tc)
    st_gemm(p, W["xk"], W["xk"].t, memT, memT.t, KXT, KXT.t, D, 512, 256)
    st_gemm(p, memT, memT.t, W["xv"], W["xv"].t, VX, VX.t, D, 256, 512)
    OXT = Dt("OXT", [512, Tc])
    st_attn(p, QXT, lambda h: QXT.t[h * 128:(h + 1) * 128, :], KXT, lambda h: KXT.t[h * 128:(h + 1) * 128, :],
            VX, lambda h: VX.t[:, h * 128:(h + 1) * 128], OXT, lambda h: OXT.t[h * 128:(h + 1) * 128, :],
            None, 4, 1, 128, 128, Tc, 256, False, 128 ** -0.5)
    XA = Dt("XA", [Tc, D])
    st_gemm(p, OXT, OXT.t, W["xo"], W["xo"].t, XA, XA.t, 512, Tc, D)
    X2 = Dt("X2", [Tc, D])
    st_ln(p, X1, X1.t, XA, XA.t, W["ln2w"], W["ln2b"], X2, X2.t, Tc, D, alpha, 1e-5)
    X2T = Dt("X2T", [D, Tc])
    st_transpose(p, X2, X2.t, X2T, X2T.t, Tc, D, cst["ident"])
    QPT = Dt("QPT", [D, Tc])
    st_gemm(p, W["pq"], W["pq"].t, X2T, X2T.t, QPT, QPT.t, D, D, Tc)
    S1, S2 = Dt("S1", [Tc, 1024]), Dt("S2", [Tc, 1024])
    st_scores(p, QPT, W["skT"], S1, S2, Tc)
    EXF, GT = Dt("EXF", [Tc, 128]), Dt("GT", [Tc, 128])
    st_topk(p, S1, S2, cst["iota"], EXF, GT, Tc)
    EXFT, GTT = Dt("EXFT", [128, Tc]), Dt("GTT", [128, Tc])
    st_transpose(p, EXF, EXF.t, EXFT, EXFT.t, Tc, 128, cst["ident"])
    st_transpose(p, GT, GT.t, GTT, GTT.t, Tc, 128, cst["ident"])
    YP = Dt("YP", [Tc, D])
    st_peer2(p, X2, X2.t, EXFT, GTT, W["pu"], W["pv"], cst["c2"], cst["ident"], YP, Tc)
    st_ln(p, X2, X2.t, YP, YP.t, W["ln3w"], W["ln3b"], out_b, out_b.t, Tc, D, alpha, 1e-5)


def build_B(Tc, alpha):
    p = Prog()
    x = p.din("x", [Tc, 2048])
    mixT = p.din("mixT", [2048, Tc])
    memT = p.din("memT", [2048, 256])
    out = p.dout("out", [Tc, 2048])
    cst = dict(iota=p.din("iota", [1, 256]), c2=p.din("c2", [1, 255]), ident=make_ident(p))
    W = phase_B_inputs(p)
    run_phase_B(p, Tc, x, mixT, memT, out, W, cst, alpha, "b_")
    return p.finish()


PAIRS = [[0, 1], [2, 3], [4, 5], [6, 7]]


def st_allgather(p, src, dst, rows, CH):
    for i in range(rows // CH):
        p.op("pool", lambda e: e.collective_compute("AllGather", ALU.bypass, replica_groups=PAIRS,
                                                    ins=[src.t[i * CH:(i + 1) * CH, :].opt()],
                                                    outs=[dst.t[i * 2 * CH:(i + 1) * 2 * CH, :].opt()]), reads=[src], writes=[dst])


def ag_rows(dst, CH, r, t0, n):
    i, j = t0 // CH, t0 % CH
    base = i * 2 * CH + r * CH + j
    return dst.t[base:base + n, :]


def st_sel_transpose(p, G, flags_d, MIXT, T, Tc, ident, CH):
    with p.stage():
        F = load_bcast(p, "flags", flags_d, 2)
        LO = [p.sb("lo", [128, 512]) for _ in range(3)]
        HI = [p.sb("hi", [128, 512]) for _ in range(3)]
        PS = [p.ps("ps", [128, 512]) for _ in range(3)]
        O = [p.sb("o", [128, 4, 128]) for _ in range(3)]
        i = 0
        for r in range(2):
            for t0 in range(0, Tc, 128):
                for n0 in range(0, 1024, 512):
                    lo, hi, ps, o = LO[i % 3], HI[i % 3], PS[i % 3], O[i % 3]
                    i += 1
                    p.dma("sp", lo[:], ag_rows(G, CH, r, t0, 128)[:, n0:n0 + 512], reads=[G], writes=[lo])
                    p.dma("act", hi[:], ag_rows(G, CH, r, Tc + t0, 128)[:, n0:n0 + 512], reads=[G], writes=[hi])
                    ts(p, "dve", lo[:], lo[:], F[:, 0:1], ALU.mult, [lo, F], [lo])
                    p.op("dve", lambda e: e.scalar_tensor_tensor(out=lo[:], in0=hi[:], scalar=F[:, 1:2], in1=lo[:],
                                                                 op0=ALU.mult, op1=ALU.add), reads=[lo, hi, F], writes=[lo])
                    for j in range(4):
                        p.op("pe", lambda e, j=j: e.transpose(out=ps[:, j * 128:(j + 1) * 128], in_=lo[:, j * 128:(j + 1) * 128],
                                                              identity=ident[:, :]), reads=[lo, ident], writes=[ps])
                    ov = o[:].rearrange("p j t -> p (j t)")
                    if i % 2:
                        p.op("act", lambda e: e.activation(out=ov, in_=ps[:, :], func=AF.Copy), reads=[ps], writes=[o])
                    else:
                        p.op("dve", lambda e: e.tensor_copy(out=ov, in_=ps[:, :]), reads=[ps], writes=[o])
                    r0 = r * 1024 + n0
                    p.dma("pool", MIXT.t[r0:r0 + 512, t0:t0 + 128].rearrange("(j p) t -> p j t", p=128), o[:, :, :], reads=[o], writes=[MIXT])


def build_M(T, L, alpha):
    Tc = T // 2
    p = Prog()
    xT = p.din("xT", [2048, T])
    x0 = p.din("x", [Tc, 2048])
    memT = p.din("memT", [2048, 256])
    flags = p.din("flags", [1, 2])
    out = p.dout("out", [Tc, 2048])
    cst = dict(pos=p.din("pos", [T, 1], I32), invf=p.din("invf", [1, 8]), maskb=p.din("maskb", [4, 128, 512]),
               ones=p.din("ones", [128, 128]), iota=p.din("iota", [1, 256]), c2=p.din("c2", [1, 255]), ident=make_ident(p))
    WA = [phase_A(p, T, None, None, pfx="l%d_" % l) for l in range(L)]
    WB = [phase_B_inputs(p, pfx="l%d_" % l) for l in range(L)]
    x_res = x0
    for l in range(L):
        mix = p.dtmp("mix%d" % l, [T, 1024])
        run_phase_A(p, T, xT, mix, WA[l], cst, "a%d_" % l)
        G = p.dtmp("G%d" % l, [2 * T, 1024])
        CHM = min(512, T)
        st_allgather(p, mix, G, T, CHM)
        MIXT = p.dtmp("MIXT%d" % l, [2048, Tc])
        st_sel_transpose(p, G, flags, MIXT, T, Tc, cst["ident"], CHM)
        last = (l == L - 1)
        xn = out if last else p.dtmp("XN%d" % l, [Tc, 2048])
        run_phase_B(p, Tc, x_res, MIXT, memT, xn, WB[l], cst, alpha, "b%d_" % l)
        if not last:
            GX = p.dtmp("GX%d" % l, [T, 2048])
            CHX = min(256, Tc)
            st_allgather(p, xn, GX, Tc, CHX)
            xT = p.dtmp("XT%d" % (l + 1), [2048, T])
            st_transpose(p, GX, None, xT, xT.t, T, 2048, cst["ident"],
                         rows=lambda t0: ag_rows(GX, CHX, t0 // Tc, t0 % Tc, 128))
            x_res = xn
    return p.finish()


BF16 = mybir.dt.bfloat16


def st_peer2(p, x_b, x_ap, EXFT, GTT_d, pu, pv, c2_d, ident, y_b, T, D=2048, NS=128):
    with p.stage():
        C2 = load_bcast(p, "c2", c2_d, 255)
        identb = p.sb("identb", [128, 128], BF16)
        p.op("dve", lambda e: e.tensor_copy(out=identb[:], in_=ident[:]), reads=[ident], writes=[identb])
        EXF = p.sb("exf", [NS, T])
        EXT = p.sb("ext", [NS, T], I32)
        GTT = p.sb("gtt", [NS, T])
        p.dma("sp", EXF[:], EXFT[:, :], reads=[EXFT], writes=[EXF])
        p.dma("act", GTT[:], GTT_d[:, :], reads=[GTT_d], writes=[GTT])
        p.op("dve", lambda e: e.tensor_copy(out=EXT[:], in_=EXF[:]), reads=[EXF], writes=[EXT])
        NB = 3
        UG = [p.sb("ug", [NS, D]) for _ in range(NB)]
        VG = [p.sb("vg", [NS, D]) for _ in range(NB)]
        JK = p.sb("junk", [NS, D])
        XF = p.sb("xf", [128, D])
        R1 = p.sb("r1", [128, D])
        XS = [[p.sb("xs%d" % k, [128, D], BF16) for k in range(3)] for _ in range(2)]
        SEL = [p.sb("sel", [128, 128], BF16) for _ in range(3)]
        HT = [p.sb("ht", [NS, 4]) for _ in range(NB)]
        ACM = [p.sb("acm", [NS, 128]) for _ in range(NB)]
        PSX = [p.ps("psx", [128, 512]) for _ in range(4)]
        PSY = [p.ps("psy", [128, 512]) for _ in range(4)]
        YO = [p.sb("yo", [128, 512]) for _ in range(4)]

        def gathers(t):
            ug, vg = UG[t % NB], VG[t % NB]
            p.dma_custom("pool", lambda e: e.indirect_dma_start(
                out=ug[:, :], out_offset=None, in_=pu[:, :],
                in_offset=bass.IndirectOffsetOnAxis(ap=EXT[:, t:t + 1], axis=0)), reads=[EXT, pu], writes=[ug])
            p.dma_custom("pool", lambda e: e.indirect_dma_start(
                out=vg[:, :], out_offset=None, in_=pv[:, :],
                in_offset=bass.IndirectOffsetOnAxis(ap=EXT[:, t:t + 1], axis=0)), reads=[EXT, pv], writes=[vg])

        def split_tile(t0):
            xs = XS[(t0 // 128) % 2]
            nr = min(128, T - t0)
            p.dma("sp", XF[0:nr, :], x_ap[t0:t0 + nr, :], reads=[x_b], writes=[XF])
            p.op("act", lambda e: e.activation(out=xs[0][:], in_=XF[:], func=AF.Copy), reads=[XF], writes=[xs[0]])
            tt(p, "pool", R1[:], XF[:], xs[0][:], ALU.subtract, [XF, xs[0]], [R1])
            p.op("act", lambda e: e.activation(out=xs[1][:], in_=R1[:], func=AF.Copy), reads=[R1], writes=[xs[1]])
            tt(p, "pool", R1[:], R1[:], xs[1][:], ALU.subtract, [R1, xs[1]], [R1])
            p.op("act", lambda e: e.activation(out=xs[2][:], in_=R1[:], func=AF.Copy), reads=[R1], writes=[xs[2]])

        def bcast(t):
            tl = t % 128
            xs = XS[(t // 128) % 2]
            sel = SEL[t % 3]
            p.op("act", lambda e: e.activation(out=sel[:], in_=identb[:, tl:tl + 1].to_broadcast([128, 128]), func=AF.Copy),
                 reads=[identb], writes=[sel])
            for n in range(4):
                for k in range(3):
                    p.op("pe", lambda e, n=n, k=k: e.matmul(PSX[n][:, :], sel[:], xs[k][:, n * 512:(n + 1) * 512],
                                                            start=(k == 0), stop=(k == 2)), reads=[sel, xs[k]], writes=[PSX[n]])

        PRE = 2
        for t in range(min(PRE, T)):
            gathers(t)
        split_tile(0)
        bcast(0)
        for t in range(T):
            tl = t % 128
            ug, vg, ht, acm = (z[t % NB] for z in (UG, VG, HT, ACM))
            if t + PRE < T:
                gathers(t + PRE)
            for n in range(4):
                tt(p, "dve", JK[:, n * 512:(n + 1) * 512], ug[:, n * 512:(n + 1) * 512], PSX[n][:, :], ALU.mult, [ug, PSX[n]], [JK])
            rsum(p, "dve", ht[:, 0:1], JK[:], [JK], [ht])
            if t + 1 < T:
                if (t + 1) % 128 == 0:
                    split_tile(t + 1)
                bcast(t + 1)
            actf(p, ht[:, 1:2], ht[:, 0:1], AF.Gelu, [ht], [ht])
            tt(p, "dve", ht[:, 2:3], ht[:, 1:2], GTT[:, t:t + 1], ALU.mult, [ht, GTT], [ht])
            ts(p, "dve", acm[:], C2[:, 127 - tl:255 - tl], ht[:, 2:3], ALU.mult, [C2, ht], [acm])
            for n in range(4):
                p.op("pe", lambda e: e.matmul(PSY[n][:, :], acm[:], vg[:, n * 512:(n + 1) * 512],
                                              start=(tl == 0), stop=(tl == 127 or t == T - 1)),
                     reads=[acm, vg], writes=[PSY[n]])
            if tl == 127 or t == T - 1:
                t0 = t - tl
                nr = tl + 1
                for n in range(4):
                    if n % 2:
                        p.op("act", lambda e: e.activation(out=YO[n][:], in_=PSY[n][:], func=AF.Copy), reads=[PSY[n]], writes=[YO[n]])
                    else:
                        p.op("dve", lambda e: e.tensor_copy(out=YO[n][:], in_=PSY[n][:]), reads=[PSY[n]], writes=[YO[n]])
                    p.dma("sp", y_b[t0:t0 + nr, n * 512:(n + 1) * 512], YO[n][0:nr, :], reads=[YO[n]], writes=[y_b])


def attn_body_pe_act(p, q_b, qT, k_b, kT, v_b, v_ap, o_b, oT, z_b, zT, maskb_d, ones_d, ident, NH, NC, DK, DV, T, scale):
    NKT = T // 128
    QB = 512
    MB = p.sb("maskb", [128, 4, 512])
    for j in range(4):
        p.dma("act", MB[:, j, :], maskb_d[j], writes=[MB])
    ones = p.sb("ones", [128, 128])
    p.dma("act", ones[:], ones_d[:, :], writes=[ones])
    Q = p.sb("qT", [DK, T])
    K = p.sb("kT", [DK, T])
    Vh = p.sb("v", [128, NKT, DV])
    PS_S = [p.ps("pss", [128, 512]) for _ in range(2)]
    PS_O = [p.ps("pso", [128, 512]) for _ in range(2)]
    PS_Z = [p.ps("psz", [128, 512]) for _ in range(2)]
    P = [p.sb("p", [128, 512]) for _ in range(3)]
    O = [p.sb("o", [128, 512]) for _ in range(2)]
    Z = [p.sb("z", [128, 512]) for _ in range(2)]
    iq = ik = 0
    for h in range(NH):
        p.dma("act", Vh[:], v_ap(h).rearrange("(kt p) d -> p kt d", p=128), reads=[v_b], writes=[Vh])
        for c in range(NC):
            hc = h * NC + c
            p.dma("act", Q[:], qT(hc), reads=[q_b], writes=[Q])
            p.dma("act", K[:], kT(hc), reads=[k_b], writes=[K])
            for qb in range(T // QB):
                q0 = qb * QB
                nkt = (q0 + QB) // 128
                pso, psz = PS_O[iq % 2], PS_Z[iq % 2]
                ob, zb = O[iq % 2], Z[iq % 2]
                iq += 1

                def oz(kt, Pt):
                    p.op("pe", lambda e: e.matmul(pso[0:DV, :], Vh[:, kt, :], Pt[:, :], start=(kt == 0), stop=(kt == nkt - 1)),
                         reads=[Vh, Pt], writes=[pso])
                    p.op("pe", lambda e: e.matmul(psz[:, :], ones[:], Pt[:, :], start=(kt == 0), stop=(kt == nkt - 1)),
                         reads=[ones, Pt], writes=[psz])
                prev = None
                for kt in range(nkt):
                    pss = PS_S[ik % 2]
                    Pt = P[ik % 3]
                    ik += 1
                    diag = kt * 128 + 127 > q0
                    p.op("pe", lambda e: e.matmul(pss[:, :], K[:, kt * 128:(kt + 1) * 128], Q[:, q0:q0 + QB], start=True, stop=not diag),
                         reads=[K, Q], writes=[pss])
                    if diag:
                        j = kt - q0 // 128
                        p.op("pe", lambda e: e.matmul(pss[:, :], ident[:, :], MB[:, j, :], start=False, stop=True),
                             reads=[ident, MB], writes=[pss])
                    actf(p, Pt[:, :], pss[:, :], AF.Exp, [pss], [Pt], scale=scale)
                    if prev is not None:
                        oz(*prev)
                    prev = (kt, Pt)
                oz(*prev)
                p.op("act", lambda e: e.activation(out=ob[0:DV, :], in_=pso[0:DV, :], func=AF.Copy), reads=[pso], writes=[ob])
                p.op("act", lambda e: e.activation(out=zb[0:DV, :], in_=psz[0:DV, :], func=AF.Copy), reads=[psz], writes=[zb])
                p.dma("act", oT(hc)[:, q0:q0 + QB], ob[0:DV, :], reads=[ob], writes=[o_b])
                p.dma("act", zT(hc)[:, q0:q0 + QB], zb[0:DV, :], reads=[zb], writes=[z_b])


def st_scores(p, QPT, skT_b, S1, S2, Tc):
    with p.stage():
        SK = p.sb("sk", [128, 2, 128])
        for c in range(2):
            p.dma("sp", SK[:, c, :], skT_b.t[c], reads=[skT_b], writes=[SK])
        A = [p.sb("a", [128, 512]) for _ in range(3)]
        PS = [p.ps("ps", [128, 512]) for _ in range(8)]
        O = [p.sb("o", [128, 512]) for _ in range(8)]
        ia = ig = 0
        for m0 in range(0, Tc, 512):
            mw = min(512, Tc - m0)
            for c, Sx in ((0, S1), (1, S2)):
                for hg in range(2):
                    base = (ig % 2) * 4
                    ig += 1
                    for hh in range(4):
                        h = hg * 4 + hh
                        r0 = (h * 2 + c) * 128
                        a = A[ia % 3]
                        ia += 1
                        p.dma("sp" if ia % 2 else "act", a[:, 0:mw], QPT.t[r0:r0 + 128, m0:m0 + mw], reads=[QPT], writes=[a])
                        for mt in range(0, mw, 128):
                            ps = PS[base + mt // 128]
                            p.op("pe", lambda e: e.matmul(ps[:, hh * 128:(hh + 1) * 128], a[:, mt:mt + 128], SK[:, c, :],
                                                          start=True, stop=True), reads=[a, SK], writes=[ps])
                    for mt in range(0, mw, 128):
                        ps, o = PS[base + mt // 128], O[base + mt // 128]
                        if (mt // 128) % 2:
                            p.op("act", lambda e: e.activation(out=o[:], in_=ps[:], func=AF.Copy), reads=[ps], writes=[o])
                        else:
                            p.op("dve", lambda e: e.tensor_copy(out=o[:], in_=ps[:]), reads=[ps], writes=[o])
                        p.dma("pool", Sx[m0 + mt:m0 + mt + 128, hg * 512:(hg + 1) * 512], o[:], reads=[o], writes=[Sx])


NCORES = 8
_PROGS = {}
_WPERM = np.concatenate([np.arange(0, 512), np.arange(1024, 1536), np.arange(512, 1024), np.arange(1536, 2048)])


def _c(a):
    return np.ascontiguousarray(a)


def _row(v):
    return _c(np.asarray(v, np.float32).reshape(1, -1))


def _consts():
    invf = (np.float32(500000.0) ** (-(np.arange(0, 16, 2, dtype=np.float32) / np.float32(16)))).astype(np.float32)[None]
    iota = np.arange(256, dtype=np.float32)[None]
    c2 = np.zeros((1, 255), np.float32)
    c2[0, 127] = 1
    return dict(invf=invf, iota=iota, c2=c2, maskb=(causal_masks() - 1.0).astype(np.float32) * np.float32(1e5),
                ones=np.ones((128, 128), np.float32), ident=np.eye(128, dtype=np.float32))


def _A_weights(l, hh, P):
    f = lambda a: np.asarray(a, np.float32)
    C = 512
    s = slice(hh * C, (hh + 1) * C)
    ar = np.arange(hh * C, (hh + 1) * C)
    cols = np.concatenate([ar, 1024 + ar, 2048 + ar, np.arange(3072, 3360), 3360 + ar, 4384 + ar, 5408 + ar])
    lam_init = 0.8 - 0.6 * float(np.exp(-0.3 * l))
    return {"w_c": _c(f(P["w_in"][l])[:, cols]), "mu": _row(f(P["shift_mu"][l])[cols[:1824]]),
            "wup": _c(f(P["w_up"][l])[:, s]), "aup": _c(f(P["a_up"][l])[:, s]), "gup": _c(f(P["g_up"][l])[:, s]),
            "w0": _row(f(P["w0"][l])[s]), "a0": _row(f(P["a0"][l])[s]), "kk": _row(f(P["k_k"][l])[s]), "ka": _row(f(P["k_a"][l])[s]),
            "rk": _row(f(P["r_k"][l]).reshape(-1)[s]), "lnxw": _row(f(P["lnx_w"][l])[s]), "lnxb": _row(f(P["lnx_b"][l])[s]),
            "sw": _row(np.tile(f(P["subln_w"][l]), 4)),
            "lq1": _row(P["lam_q1"][l]), "lk1": _row(P["lam_k1"][l]), "lq2": _row(P["lam_q2"][l]), "lk2": _row(P["lam_k2"][l]),
            "lamc": np.array([[lam_init, 1.0 - lam_init]], np.float32)}


def _B_weights(l, P):
    f = lambda a: np.asarray(a, np.float32)
    return {"w_out": _c(f(P["w_out"][l])[_WPERM]), "xq": _c(f(P["xq"][l])), "xk": _c(f(P["xk"][l])), "xv": _c(f(P["xv"][l])),
            "xo": _c(f(P["xo"][l])), "pq": _c(f(P["pq"][l])), "skT": _c(f(P["subkeys"][l]).transpose(0, 2, 1)),
            "pu": _c(f(P["peer_u"][l])), "pv": _c(f(P["peer_v"][l])),
            "ln1w": _row(P["ln1_w"][l]), "ln1b": _row(P["ln1_b"][l]), "ln2w": _row(P["ln2_w"][l]), "ln2b": _row(P["ln2_b"][l]),
            "ln3w": _row(P["ln3_w"][l]), "ln3b": _row(P["ln3_b"][l])}


def kernel(**inputs):
    P = inputs
    x = np.asarray(P["x"], np.float32)
    mem = np.asarray(P["mem"], np.float32)
    pos = np.asarray(P["positions"]).astype(np.int32)
    B, S, D = x.shape
    L = np.asarray(P["w_in"]).shape[0]
    alpha = (2.0 * L) ** 0.25
    Tc = S // 2
    K = _consts()
    WB = [_B_weights(l, P) for l in range(L)]
    WA = [[_A_weights(l, hh, P) for hh in range(2)] for l in range(L)]
    maps = []
    for c in range(NCORES):
        b_, hh = c // 2, c % 2
        m = {"xT": _c(x[b_].T), "x": _c(x[b_, hh * Tc:(hh + 1) * Tc]), "memT": _c(mem[b_].T),
             "flags": np.array([[1.0 - hh, float(hh)]], np.float32), "pos": _c(pos[b_].reshape(-1, 1))}
        m.update(K)
        for l in range(L):
            m.update({"l%d_%s" % (l, k): v for k, v in WA[l][hh].items()})
            m.update({"l%d_%s" % (l, k): v for k, v in WB[l].items()})
        maps.append(m)
    key = (S, L)
    if key not in _PROGS:
        _PROGS[key] = build_M(S, L, alpha)
    res = run_bass_kernel_spmd(_PROGS[key], maps, core_ids=list(range(NCORES)))
    out = np.zeros((B, S, D), np.float32)
    for c in range(NCORES):
        b_, hh = c // 2, c % 2
        out[b_, hh * Tc:(hh + 1) * Tc] = res.results[c]["out"]
    return out
```
